# Optimizing a Trainium2 kernel written in Bass

```python
import math
import jax
import jax.numpy as jnp
from jax import lax
import numpy as np

D_MODEL = 1024
BATCH = 8
SEQ = 4096
DEPTH = 2

N_EVEN = (DEPTH + 1) // 2
N_ODD = DEPTH // 2

SB_HEADS = 8
SB_HEAD_DIM = 64
SB_WIDTH = SB_HEADS * SB_HEAD_DIM
Q_BLOCK = 128

HG_HEADS = 4
HG_KEY_DIM = 128
HG_VAL_DIM = 128
HG_KEY_WIDTH = HG_HEADS * HG_KEY_DIM
HG_VAL_WIDTH = HG_HEADS * HG_VAL_DIM
HG_CHUNK = 64

IN_SPLITS = (SB_WIDTH, SB_WIDTH, SB_WIDTH, HG_KEY_WIDTH, HG_KEY_WIDTH, HG_VAL_WIDTH, HG_VAL_WIDTH)
IN_COLS = sum(IN_SPLITS)
MIX_WIDTH = SB_WIDTH + HG_VAL_WIDTH

S5_GROUP = 16
S5_GROUPS = D_MODEL // S5_GROUP
S5_STATE = 64
DT_MIN = 1e-3
DT_MAX = 1e-1

D_FF = -(-8 * D_MODEL // (3 * 256)) * 256
N_MOD = 6
EPS = 1e-6

kernel_name = 'sb_hgrn2_s5_hybrid_adaln'


def rms_norm(x, gain):
    xf = x.astype(jnp.float32)
    y = xf * lax.rsqrt(jnp.mean(xf * xf, axis=-1, keepdims=True) + EPS)
    return (y * gain.astype(jnp.float32)).astype(x.dtype)


def _heads(t, n):
    b, s, w = t.shape
    return t.reshape(b, s, n, w // n).transpose(0, 2, 1, 3)


def _merge(t):
    b, h, s, d = t.shape
    return t.transpose(0, 2, 1, 3).reshape(b, s, h * d)


def stick_breaking_attention(q, k, v):
    b, h, s, dh = q.shape
    nb = s // Q_BLOCK
    q_blocks = q.reshape(b, h, nb, Q_BLOCK, dh).transpose(2, 0, 1, 3, 4)
    key_pos = jnp.arange(s)
    scale = dh ** -0.5

    def one_block(args):
        q_blk, blk = args
        z = jnp.einsum('bhqd,bhkd->bhqk', q_blk, k).astype(jnp.float32) * scale
        q_pos = blk * Q_BLOCK + jnp.arange(Q_BLOCK)
        causal = key_pos[None, :] < q_pos[:, None]
        log_beta = jax.nn.log_sigmoid(z)
        log_keep = jnp.where(causal, jax.nn.log_sigmoid(-z), 0.0)
        tail = lax.cumsum(log_keep, axis=3, reverse=True) - log_keep
        w = jnp.where(causal, jnp.exp(jnp.where(causal, log_beta + tail, 0.0)), 0.0)
        return jnp.einsum('bhqk,bhkd->bhqd', w, v.astype(jnp.float32))

    out = lax.map(one_block, (q_blocks, jnp.arange(nb)))
    return out.transpose(1, 2, 0, 3, 4).reshape(b, h, s, dh)


def hgrn2_chunkwise(q, log_f, k, v):
    b, h, s, dk = q.shape
    dv = v.shape[-1]
    nc = s // HG_CHUNK

    def chunks(t):
        return t.astype(jnp.float32).reshape(b, h, nc, HG_CHUNK, t.shape[-1]).transpose(2, 0, 1, 3, 4)

    tri = jnp.tril(jnp.ones((HG_CHUNK, HG_CHUNK), dtype=bool))

    def step(state, inp):
        q_c, g_c, k_c, v_c = inp
        cum = jnp.cumsum(g_c, axis=2)
        diff = cum[:, :, :, None, :] - cum[:, :, None, :, :]
        decay = jnp.exp(jnp.where(tri[:, :, None], diff, -jnp.inf))
        scores = jnp.einsum('bhtd,bhtsd,bhsd->bhts', q_c, decay, k_c)
        intra = jnp.einsum('bhts,bhsv->bhtv', scores, v_c)
        inter = jnp.einsum('bhtd,bhdv->bhtv', q_c * jnp.exp(cum), state)
        last = cum[:, :, -1:, :]
        k_dec = k_c * jnp.exp(last - cum)
        new_state = state * jnp.exp(last[:, :, 0, :, None]) + jnp.einsum('bhsd,bhsv->bhdv', k_dec, v_c)
        return new_state, intra + inter

    state0 = jnp.zeros((b, h, dk, dv), jnp.float32)
    _, out = lax.scan(step, state0, (chunks(q), chunks(log_f), chunks(k), chunks(v)))
    return out.transpose(1, 2, 0, 3, 4).reshape(b, h, s, dv)


def hybrid_mixer(h, w_in, w_out, lower_bound, hg_gain):
    b, s, _ = h.shape
    proj = h @ w_in
    offs = np.cumsum(IN_SPLITS)[:-1].tolist()
    q_a, k_a, v_a, q_b, f_b, i_b, g_b = jnp.split(proj, offs, axis=-1)
    o_a = stick_breaking_attention(_heads(q_a, SB_HEADS), _heads(k_a, SB_HEADS), _heads(v_a, SB_HEADS))
    f_logit = f_b.astype(jnp.float32)
    forget = lower_bound + (1.0 - lower_bound) * jax.nn.sigmoid(f_logit)
    log_f = jnp.log(forget)
    key_b = (1.0 - lower_bound) * jax.nn.sigmoid(-f_logit)
    o_b = hgrn2_chunkwise(_heads(jax.nn.silu(q_b), HG_HEADS), _heads(log_f, HG_HEADS),
                          _heads(key_b, HG_HEADS), _heads(i_b, HG_HEADS))
    o_b = o_b.transpose(0, 2, 1, 3)
    o_b = o_b * lax.rsqrt(jnp.mean(o_b * o_b, axis=-1, keepdims=True) + EPS)
    o_b = o_b * hg_gain.astype(jnp.float32).reshape(HG_HEADS, HG_VAL_DIM)
    o_b = o_b.reshape(b, s, HG_VAL_WIDTH) * jax.nn.silu(g_b.astype(jnp.float32))
    merged = jnp.concatenate([_merge(o_a), o_b], axis=-1).astype(h.dtype)
    return merged @ w_out


def s5_mixer(h, w_in, lam_re, lam_im, log_dt, b_re, b_im, c_re, c_im, d_skip, w_glu):
    b, s, d = h.shape
    u = (h @ w_in).astype(jnp.float32)
    u_g = u.reshape(b, s, S5_GROUPS, S5_GROUP)
    f32 = jnp.float32
    lr = jnp.minimum(lam_re.astype(f32), -1e-4)
    li = lam_im.astype(f32)
    dt = jnp.exp(log_dt.astype(f32))[:, None]
    mag = jnp.exp(lr * dt)
    a_re = mag * jnp.cos(li * dt)
    a_im = mag * jnp.sin(li * dt)
    e_re = a_re - 1.0
    e_im = a_im
    den = lr * lr + li * li
    z_re = (e_re * lr + e_im * li) / den
    z_im = (e_im * lr - e_re * li) / den
    br = b_re.astype(f32)
    bi = b_im.astype(f32)
    bbar_re = z_re[..., None] * br - z_im[..., None] * bi
    bbar_im = z_re[..., None] * bi + z_im[..., None] * br
    cr = c_re.astype(f32)
    ci = c_im.astype(f32)

    def combine(e1, e2):
        a1r, a1i, b1r, b1i = e1
        a2r, a2i, b2r, b2i = e2
        return (a2r * a1r - a2i * a1i, a2r * a1i + a2i * a1r,
                a2r * b1r - a2i * b1i + b2r, a2r * b1i + a2i * b1r + b2i)

    def one_sequence(u_seq):
        bu_re = jnp.einsum('sgh,gph->sgp', u_seq, bbar_re)
        bu_im = jnp.einsum('sgh,gph->sgp', u_seq, bbar_im)
        ar = jnp.broadcast_to(a_re, bu_re.shape)
        ai = jnp.broadcast_to(a_im, bu_re.shape)
        _, _, x_re, x_im = lax.associative_scan(combine, (ar, ai, bu_re, bu_im), axis=0)
        return jnp.einsum('ghp,sgp->sgh', cr, x_re) - jnp.einsum('ghp,sgp->sgh', ci, x_im)

    y = lax.map(one_sequence, u_g).reshape(b, s, d)
    y = y + d_skip.astype(f32) * u
    y = jax.nn.gelu(y).astype(h.dtype)
    val, gate = jnp.split(y @ w_glu, 2, axis=-1)
    return val * jax.nn.sigmoid(gate)


def swiglu(h, w_in, w_out):
    g, u = jnp.split(h @ w_in, 2, axis=-1)
    return (jax.nn.silu(g) * u) @ w_out


def setup_inputs(seed: int = 0) -> dict:
    key = jax.random.key(seed)
    ks = jax.random.split(key, 24)
    f32 = jnp.float32
    d = D_MODEL

    def nrm(k, shape, scale):
        return jax.random.normal(k, shape, f32) * scale

    x = nrm(ks[0], (BATCH, SEQ, d), 1.0)
    c = nrm(ks[1], (BATCH, d), 1.0)
    norm_mix_g = 1.0 + nrm(ks[2], (DEPTH, d), 0.02)
    norm_ffn_g = 1.0 + nrm(ks[3], (DEPTH, d), 0.02)
    ada_w = nrm(ks[4], (DEPTH, d, N_MOD * d), d ** -0.5)
    ada_b = nrm(ks[5], (DEPTH, N_MOD * d), 0.01)
    ffn_w_in = nrm(ks[6], (DEPTH, d, 2 * D_FF), d ** -0.5)
    ffn_w_out = nrm(ks[7], (DEPTH, D_FF, d), D_FF ** -0.5)
    final_norm_g = 1.0 + nrm(ks[8], (d,), 0.02)
    hy_w_in = nrm(ks[9], (N_EVEN, d, IN_COLS), d ** -0.5)
    hy_w_out = nrm(ks[10], (N_EVEN, MIX_WIDTH, d), MIX_WIDTH ** -0.5)
    hg_norm_g = 1.0 + nrm(ks[11], (N_EVEN, HG_VAL_WIDTH), 0.02)
    hg_lb_logits = nrm(ks[12], (DEPTH + 1, HG_KEY_WIDTH), 0.1)
    s5_w_in = nrm(ks[13], (N_ODD, d, d), d ** -0.5)
    s5_lam_re = -0.5 + nrm(ks[14], (N_ODD, S5_GROUPS, S5_STATE), 0.01)
    s5_lam_im = jnp.pi * jnp.arange(S5_STATE, dtype=f32) + nrm(ks[15], (N_ODD, S5_GROUPS, S5_STATE), 0.01)
    s5_log_dt = jax.random.uniform(ks[16], (N_ODD, S5_GROUPS), f32, math.log(DT_MIN), math.log(DT_MAX))
    s5_b_re = nrm(ks[17], (N_ODD, S5_GROUPS, S5_STATE, S5_GROUP), (2 * S5_GROUP) ** -0.5)
    s5_b_im = nrm(ks[18], (N_ODD, S5_GROUPS, S5_STATE, S5_GROUP), (2 * S5_GROUP) ** -0.5)
    s5_c_re = nrm(ks[19], (N_ODD, S5_GROUPS, S5_GROUP, S5_STATE), 0.5)
    s5_c_im = nrm(ks[20], (N_ODD, S5_GROUPS, S5_GROUP, S5_STATE), 0.5)
    s5_d = nrm(ks[21], (N_ODD, d), 1.0)
    s5_w_glu = nrm(ks[22], (N_ODD, d, 2 * d), d ** -0.5)
    return {'x': x, 'c': c, 'norm_mix_g': norm_mix_g, 'norm_ffn_g': norm_ffn_g,
            'ada_w': ada_w, 'ada_b': ada_b, 'ffn_w_in': ffn_w_in, 'ffn_w_out': ffn_w_out,
            'final_norm_g': final_norm_g, 'hy_w_in': hy_w_in, 'hy_w_out': hy_w_out,
            'hg_norm_g': hg_norm_g, 'hg_lb_logits': hg_lb_logits, 's5_w_in': s5_w_in,
            's5_lam_re': s5_lam_re, 's5_lam_im': s5_lam_im, 's5_log_dt': s5_log_dt,
            's5_b_re': s5_b_re, 's5_b_im': s5_b_im, 's5_c_re': s5_c_re, 's5_c_im': s5_c_im,
            's5_d': s5_d, 's5_w_glu': s5_w_glu}


def reference(x, c, norm_mix_g, norm_ffn_g, ada_w, ada_b, ffn_w_in, ffn_w_out, final_norm_g,
              hy_w_in, hy_w_out, hg_norm_g, hg_lb_logits, s5_w_in, s5_lam_re, s5_lam_im,
              s5_log_dt, s5_b_re, s5_b_im, s5_c_re, s5_c_im, s5_d, s5_w_glu):
    lb_all = jnp.cumsum(jax.nn.softmax(hg_lb_logits.astype(jnp.float32), axis=0), axis=0)
    c_act = jax.nn.silu(c)
    for layer in range(DEPTH):
        mod = (c_act @ ada_w[layer] + ada_b[layer])[:, None, :]
        sh_m, sc_m, g_m, sh_f, sc_f, g_f = jnp.split(mod, N_MOD, axis=-1)
        h = rms_norm(x, norm_mix_g[layer]) * (1.0 + sc_m) + sh_m
        i = layer // 2
        if layer % 2 == 0:
            mix = hybrid_mixer(h, hy_w_in[i], hy_w_out[i], lb_all[layer], hg_norm_g[i])
        else:
            mix = s5_mixer(h, s5_w_in[i], s5_lam_re[i], s5_lam_im[i], s5_log_dt[i], s5_b_re[i],
                           s5_b_im[i], s5_c_re[i], s5_c_im[i], s5_d[i], s5_w_glu[i])
        x = x + g_m * mix
        h = rms_norm(x, norm_ffn_g[layer]) * (1.0 + sc_f) + sh_f
        x = x + g_f * swiglu(h, ffn_w_in[layer], ffn_w_out[layer])
    return rms_norm(x, final_norm_g)
```

```python
from contextlib import ExitStack

import numpy as np
import concourse.bass as bass
import concourse.mybir as mybir
from concourse.bass_utils import run_bass_kernel_spmd

F32 = mybir.dt.float32
BF16 = mybir.dt.bfloat16
I32 = mybir.dt.int32
AF = mybir.ActivationFunctionType
ALU = mybir.AluOpType

ENGS = ["pe", "act", "dve", "pool", "sp"]
RING = 8
S = 4096
D = 1024
TB = 512
NB = S // TB
DFF = 2816
EPS = 1e-6
TWO_PI = float(2 * np.pi)


class Op:
    __slots__ = ("eng", "fn", "deps", "cnt", "need_inc", "dma", "ring", "gen")

    def __init__(self, eng, fn, dma):
        self.eng = eng
        self.fn = fn
        self.dma = dma
        self.deps = []
        self.cnt = 0
        self.need_inc = False
        self.ring = 0
        self.gen = 0


class Sched:
    def __init__(self, nc, es):
        self.nc = nc
        self.ops = {e: [] for e in ENGS}
        self.last_w = {}
        self.readers = {}
        self.sems = {e: es.enter_context(nc.semaphore(f"s_{e}")) for e in ENGS}
        self.rings = {e: [es.enter_context(nc.semaphore(f"r_{e}{i}")) for i in range(RING)] for e in ("sp", "pool", "act")}
        self.cnt = {e: 0 for e in ENGS}
        self.ndma = {e: 0 for e in ENGS}
        self.waited = {e: {} for e in ENGS}

    def add(self, eng, fn, reads=(), writes=(), dma=False):
        op = Op(eng, fn, dma)
        deps = {}
        for b in reads:
            w = self.last_w.get(b)
            if w is not None:
                deps[id(w)] = w
        for b in writes:
            w = self.last_w.get(b)
            if w is not None:
                deps[id(w)] = w
            for r in self.readers.get(b, ()):
                deps[id(r)] = r
        for d in deps.values():
            if d.eng == "pe" and eng == "pe" and not d.dma and not dma:
                continue
            op.deps.append(d)
            d.need_inc = True
        for b in reads:
            self.readers.setdefault(b, []).append(op)
        for b in writes:
            self.last_w[b] = op
            self.readers[b] = []
        self.ops[eng].append(op)
        return op

    def pe(self, fn, reads=(), writes=()):
        return self.add("pe", fn, reads, writes)

    def act(self, fn, reads=(), writes=()):
        return self.add("act", fn, reads, writes)

    def dve(self, fn, reads=(), writes=()):
        return self.add("dve", fn, reads, writes)

    def pool(self, fn, reads=(), writes=()):
        return self.add("pool", fn, reads, writes)

    def dma(self, out, in_, reads=(), writes=(), eng="sp"):
        return self.add(eng, lambda e: e.dma_start(out=out, in_=in_), reads, writes, dma=True)

    def emit_phase(self):
        nc = self.nc
        ops = self.ops
        self.ops = {e: [] for e in ENGS}
        for e in ENGS:
            for op in ops[e]:
                if op.dma:
                    di = self.ndma[e]
                    op.ring = di % RING
                    op.gen = di // RING + 1
                    self.ndma[e] = di + 1
                elif op.need_inc:
                    self.cnt[e] += 1
                    op.cnt = self.cnt[e]
        sems, rings = self.sems, self.rings

        def run_engine(e, eng):
            waited = self.waited[e]

            def wait(key, sem, val):
                if waited.get(key, 0) >= val:
                    return
                eng.wait_ge(sem, val)
                waited[key] = val

            for op in ops[e]:
                for d in op.deps:
                    if d.dma:
                        wait((d.eng, d.ring), rings[d.eng][d.ring], 16 * d.gen)
                    else:
                        wait((d.eng,), sems[d.eng], d.cnt)
                if op.dma:
                    if op.gen > 1:
                        wait((e, op.ring), rings[e][op.ring], 16 * (op.gen - 1))
                    op.fn(eng).then_inc(rings[e][op.ring], 16)
                else:
                    ins = op.fn(eng)
                    if op.need_inc:
                        ins.then_inc(sems[e], 1)
            if e in rings:
                n = self.ndma[e]
                for r in range(RING):
                    c = (n - r + RING - 1) // RING if n > r else 0
                    if c > 0:
                        wait((e, r), rings[e][r], 16 * c)

        with nc.Block() as block:
            @block.tensor
            def _(eng):
                run_engine("pe", eng)

            @block.scalar
            def _(eng):
                run_engine("act", eng)

            @block.vector
            def _(eng):
                run_engine("dve", eng)

            @block.gpsimd
            def _(eng):
                run_engine("pool", eng)

            @block.sync
            def _(eng):
                run_engine("sp", eng)

        self.last_w = {}
        self.readers = {}


class K:
    pass


def mm_group(s, out, pairs, reads, writes):
    n = len(pairs)

    def fn(e):
        ins = None
        for i, (l, r) in enumerate(pairs):
            ins = e.matmul(out, l, r, start=(i == 0), stop=(i == n - 1))
        return ins

    return s.pe(fn, reads, writes)


def phase_pre(k):
    nc, s, din = k.nc, k.s, k.din
    with ExitStack() as es:
        T = lambda name, shape, dt=F32: es.enter_context(nc.sbuf_tensor(f"p{k.pid}_{name}", shape, dt))
        P = lambda name, shape, dt=F32: es.enter_context(nc.psum_tensor(f"p{k.pid}_{name}", shape, dt))
        load_weight_bf16(k, k.pre["w_in0"], din["hy_w_in"][0], 8, 3584, "w_in0")
        c_sb = T("c_sb", [128, 8])
        cs = T("cs", [128, 8])
        s.dma(c_sb[:], din["c"].rearrange("o (j p) -> p (o j)", p=128), writes=["c_sb"])
        s.act(lambda e: e.activation(cs[:], c_sb[:], AF.Silu), ["c_sb"], ["cs"])
        s.dma(k.nmg[:], din["norm_mix_g"].rearrange("l (j p) -> p l j", p=128), writes=["nmg"])
        s.dma(k.nfg[:], din["norm_ffn_g"].rearrange("l (j p) -> p l j", p=128), writes=["nfg"])
        s.dma(k.fng[:], din["final_norm_g"].rearrange("o (j p) -> p (o j)", p=128), writes=["fng"])
        s.dma(k.s5d[:], din["s5_d"].rearrange("o (j p) -> p (o j)", p=128), writes=["s5d"])
        s.dma(k.hgg[:], din["hg_norm_g"].rearrange("o (j p) -> p (o j)", p=128), writes=["hgg"])
        adb = T("adb", [128, 2, 48])
        s.dma(adb[:], din["ada_b"].rearrange("l (j p) -> p l j", p=128), writes=["adb"])
        lg = T("lg", [128, 3, 4])
        s.dma(lg[:], din["hg_lb_logits"].rearrange("r (h p) -> p r h", p=128), writes=["lg"])
        dl = T("dl", [128, 2, 4])
        s.dve(lambda e: e.tensor_tensor(dl[:, 0, :], lg[:, 1, :], lg[:, 0, :], ALU.subtract), ["lg"], ["dl0"])
        s.dve(lambda e: e.tensor_tensor(dl[:, 1, :], lg[:, 2, :], lg[:, 0, :], ALU.subtract), ["lg"], ["dl1"])
        s.act(lambda e: e.activation(dl[:], dl[:], AF.Exp), ["dl0", "dl1"], ["dle"])
        den = T("den", [128, 4])
        s.dve(lambda e: e.scalar_tensor_tensor(den[:], dl[:, 0, :], 1.0, dl[:, 1, :], ALU.add, ALU.add), ["dle"], ["den"])
        s.dve(lambda e: e.reciprocal(k.lb[:], den[:]), ["den"], ["lb"])
        s.dve(lambda e: e.tensor_scalar(k.oml[:], k.lb[:], -1.0, 1.0, ALU.mult, ALU.add), ["lb"], ["oml"])
        slabs = [T(f"slab{i}", [128, 8, 512]) for i in range(3)]
        mod_ps = [P(f"mod_ps{l}", [128, 48]) for l in range(2)]
        it = 0
        for l in range(2):
            for si in range(12):
                sl = slabs[it % 3]
                key = f"slab{it % 3}"
                it += 1
                src = din["ada_w"][l, :, si * 512:(si + 1) * 512].rearrange("(kk p) n -> p kk n", p=128)
                s.dma(sl[:], src, writes=[key])
                for jj in range(4):
                    j = si * 4 + jj
                    mm_group(s, mod_ps[l][:, j:j + 1],
                             [(sl[:, kk, jj * 128:(jj + 1) * 128], cs[:, kk:kk + 1]) for kk in range(8)],
                             [key, "cs"], [f"mod_ps{l}"])
            s.dve(lambda e, l=l: e.tensor_tensor(k.mod[:, l, :], mod_ps[l][:], adb[:, l, :], ALU.add),
                  [f"mod_ps{l}", "adb"], [f"mod{l}"])
            s.dve(lambda e, l=l: e.scalar_tensor_tensor(k.gm[:, l, :], k.mod[:, l, 8:16], 1.0, k.nmg[:, l, :], ALU.add, ALU.mult),
                  [f"mod{l}", "nmg"], [f"gm{l}"])
            s.dve(lambda e, l=l: e.scalar_tensor_tensor(k.gf[:, l, :], k.mod[:, l, 32:40], 1.0, k.nfg[:, l, :], ALU.add, ALU.mult),
                  [f"mod{l}", "nfg"], [f"gf{l}"])
        s.emit_phase()


def load_weight_bf16(k, dst, src_rows, nk, ncols, key, chunk=2048):
    s = k.s
    for kk in range(nk):
        for c0 in range(0, ncols, chunk):
            c1 = min(ncols, c0 + chunk)
            s.dma(dst[:, kk, c0:c1], src_rows[kk * 128:(kk + 1) * 128, c0:c1], writes=[(key, kk, c0)], eng="pool")


def wkeys(key, nk, ncols, chunk=2048):
    return [(key, kk, c0) for kk in range(nk) for c0 in range(0, ncols, chunk)]


def norm_block(k, X, xkey, hT, hkey, sq, sqkey, G, SH, tmps, ss_ps, sskey, rstd, tag):
    s = k.s
    n = X.shape[2]
    for c in range(8):
        s.act(lambda e, c=c: e.activation(sq[:, c, :], X[:, c, :], AF.Square), [xkey], [(sqkey, c)])
    mm_group(s, ss_ps[:, :n], [(k.ones_bf[:], sq[:, c, :]) for c in range(8)], [(sqkey, c) for c in range(8)], [sskey])
    s.act(lambda e: e.activation(rstd[:, :n], ss_ps[:, :n], AF.Sqrt, bias=k.eps_col[:, 0:1], scale=1.0 / D), [sskey], [tag + "rs0"])
    s.dve(lambda e: e.reciprocal(rstd[:, :n], rstd[:, :n]), [tag + "rs0"], [tag + "rstd"])
    if hT is None:
        return
    for c in range(8):
        tm = tmps[c % 2]
        tk = f"{tag}tmp{c % 2}"
        s.dve(lambda e, c=c, tm=tm: e.scalar_tensor_tensor(tm[:, :n], X[:, c, :], G[:, c:c + 1], rstd[:, :n], ALU.mult, ALU.mult),
              [xkey, tag + "rstd"], [tk])
        s.act(lambda e, c=c, tm=tm: e.activation(hT[:, c, :], tm[:, :n], AF.Identity, bias=SH[:, c:c + 1], scale=1.0),
              [tk], [(hkey, c)])


def phase_l0a(k):
    nc, s, din, dr = k.nc, k.s, k.din, k.dr
    TQ = 256
    NQ = S // TQ
    with ExitStack() as es:
        T = lambda name, shape, dt=F32: es.enter_context(nc.sbuf_tensor(f"p{k.pid}_{name}", shape, dt))
        P = lambda name, shape, dt=F32: es.enter_context(nc.psum_tensor(f"p{k.pid}_{name}", shape, dt))
        w = k.pre["w_in0"]
        xtok = T("xtok", [128, 2, 1024])
        X = [T(f"X{i}", [128, 8, TQ]) for i in range(2)]
        hT = [T(f"hT{i}", [128, 8, TQ], BF16) for i in range(2)]
        sq = T("sq", [128, 8, TQ], BF16)
        tmps = [T(f"tmp{i}", [128, TQ]) for i in range(2)]
        rstd = [T(f"rstd{i}", [128, TQ]) for i in range(2)]
        tr_ps = [P(f"tr_ps{i}", [128, 512]) for i in range(2)]
        ssb = P("ssb", [128, 512])
        mm_ps = [P(f"mm_ps{i}", [128, 512]) for i in range(4)]
        st_qk = [T(f"st_qk{i}", [128, 8, TQ], BF16) for i in range(2)]
        st_f = {nm: [T(f"st_{nm}{i}", [128, 4, TQ]) for i in range(2)] for nm in ("qb", "fb", "gb")}
        st_v = {nm: [T(f"st_{nm}{i}", [128, 2, 512], BF16) for i in range(2)] for nm in ("va", "ib")}
        G = k.gm[:, 0, :]
        SH = k.mod[:, 0, 0:8]
        cnt = {"ev": 0, "mi": 0}

        def loadx(n):
            s.dma(xtok[:], din["x"][n * TQ:(n + 1) * TQ, :].rearrange("(a p) d -> p a d", p=128), writes=["xtok"])

        def pre(n):
            pb = n % 2
            for c in range(8):
                tp, tk = tr_ps[c % 2], f"tr_ps{c % 2}"

                def trf(e, c=c, tp=tp):
                    ins = None
                    for a_ in range(2):
                        ins = e.transpose(tp[:, a_ * 128:(a_ + 1) * 128], xtok[:, a_, c * 128:(c + 1) * 128], k.ident[:])
                    return ins
                s.pe(trf, ["xtok", "ident"], [tk])
                if c % 2 == 0:
                    s.dve(lambda e, c=c, tp=tp: e.tensor_copy(X[pb][:, c, :], tp[:, 0:TQ]), [tk], [(f"X{pb}", c)])
                else:
                    s.act(lambda e, c=c, tp=tp: e.copy(X[pb][:, c, :], tp[:, 0:TQ]), [tk], [(f"X{pb}", c)])
            norm256(k, n, X, hT, sq, tmps, rstd, ssb, G, SH)
            s.dma(dr["XT"].rearrange("(c p) t -> p c t", p=128)[:, :, n * TQ:(n + 1) * TQ], X[pb][:],
                  reads=[(f"X{pb}", c) for c in range(8)], writes=[("XT", n)])

        def evac(dst, ps, pk, dk):
            if cnt["ev"] % 2 == 0:
                s.act(lambda e: e.copy(dst, ps), [pk], [dk])
            else:
                s.dve(lambda e: e.tensor_copy(dst, ps), [pk], [dk])
            cnt["ev"] += 1

        def groups(n):
            pb = n % 2
            hk = [(f"hT{pb}", c) for c in range(8)]
            h = hT[pb]
            tcs = slice(n * TQ, (n + 1) * TQ)
            out = []
            fm = [("qk", 0, 0, 4), ("qk", 512, 4, 4), ("qb", 1536, 0, 4), ("fb", 2048, 0, 4), ("gb", 3072, 0, 4)]
            for nm, col0, slot0, nchunk in fm:
                st = st_qk[pb] if nm == "qk" else st_f[nm][pb]
                stkey = f"st_{nm}{pb}"
                for j in range(nchunk):
                    def g_(nm=nm, col0=col0, slot0=slot0, j=j, st=st, stkey=stkey):
                        ps, pk = mm_ps[cnt["mi"] % 4], f"mm_ps{cnt['mi'] % 4}"
                        cnt["mi"] += 1
                        cc = col0 + j * 128
                        mm_group(s, ps[:, 0:TQ], [(w[:, kk, cc:cc + 128], h[:, kk, :]) for kk in range(8)], hk, [pk])
                        evac(st[:, slot0 + j, :], ps[:, 0:TQ], pk, (stkey, slot0 + j))
                        if nm == "qk" and slot0 + j == 7:
                            s.dma(dr["QK"].rearrange("(c p) t -> p c t", p=128)[:, :, tcs], st[:], reads=[(stkey, x) for x in range(8)],
                                  writes=[("QK", n)])
                        if nm != "qk" and j == 3:
                            dn = {"qb": "QB", "fb": "FB", "gb": "GB"}[nm]
                            s.dma(dr[dn].rearrange("(c p) t -> p c t", p=128)[:, :, tcs], st[:], reads=[(stkey, x) for x in range(4)],
                                  writes=[(dn, n)])
                    out.append(g_)
            for nm, col0, dn in (("va", 1024, "VA"), ("ib", 2560, "IB")):
                st = st_v[nm][pb]
                stkey = f"st_{nm}{pb}"
                for a_ in range(2):
                    def g_(nm=nm, col0=col0, dn=dn, a_=a_, st=st, stkey=stkey):
                        ps, pk = mm_ps[cnt["mi"] % 4], f"mm_ps{cnt['mi'] % 4}"
                        cnt["mi"] += 1
                        mm_group(s, ps[:], [(h[:, kk, a_ * 128:(a_ + 1) * 128], w[:, kk, col0:col0 + 512]) for kk in range(8)], hk, [pk])
                        evac(st[:, a_, :], ps[:], pk, (stkey, a_))
                        if a_ == 1:
                            s.dma(dr[dn][tcs, :].rearrange("(a p) f -> p a f", p=128), st[:], reads=[(stkey, 0), (stkey, 1)], writes=[(dn, n)])
                    out.append(g_)
            return out

        loadx(0)
        pre(0)
        for n in range(NQ):
            if n + 1 < NQ:
                loadx(n + 1)
            gs = groups(n)
            for g_ in gs[:12]:
                g_()
            if n + 1 < NQ:
                pre(n + 1)
            for g_ in gs[12:]:
                g_()
        s.emit_phase()


def norm_block_multi(k, X, xkeys, hT, hkey, sq, sqkey, G, SH, tmps, ss_ps, sskey, rstd, tag):
    s = k.s
    n = X.shape[2]
    for c in range(8):
        s.act(lambda e, c=c: e.activation(sq[:, c, :], X[:, c, :], AF.Square), [xkeys[c]], [(sqkey, c)])
    mm_group(s, ss_ps[:, :n], [(k.ones_bf[:], sq[:, c, :]) for c in range(8)], [(sqkey, c) for c in range(8)], [sskey])
    s.act(lambda e: e.activation(rstd[:, :n], ss_ps[:, :n], AF.Sqrt, bias=k.eps_col[:, 0:1], scale=1.0 / D), [sskey], [tag + "rs0"])
    s.dve(lambda e: e.reciprocal(rstd[:, :n], rstd[:, :n]), [tag + "rs0"], [tag + "rstd"])
    if hT is None:
        return
    for c in range(8):
        tm = tmps[c % 2]
        tk = f"{tag}tmp{c % 2}"
        s.dve(lambda e, c=c, tm=tm: e.scalar_tensor_tensor(tm[:, :n], X[:, c, :], G[:, c:c + 1], rstd[:, :n], ALU.mult, ALU.mult),
              [xkeys[c], tag + "rstd"], [tk])
        s.act(lambda e, c=c, tm=tm: e.activation(hT[:, c, :], tm[:, :n], AF.Identity, bias=SH[:, c:c + 1], scale=1.0),
              [tk], [(hkey, c)])


def phase_attn(k):
    nc, s, dr = k.nc, k.s, k.dr
    with ExitStack() as es:
        T = lambda name, shape, dt=F32: es.enter_context(nc.sbuf_tensor(f"p{k.pid}_{name}", shape, dt))
        P = lambda name, shape, dt=F32: es.enter_context(nc.psum_tensor(f"p{k.pid}_{name}", shape, dt))
        Um = T("Um", [128, 128], BF16)
        mask = T("mask", [128, 128])
        s.pool(lambda e: e.memset(mask[:], 1.0), [], ["mask"])
        s.pool(lambda e: e.affine_select(mask[:], mask[:], pattern=[[-1, 128]], compare_op=ALU.is_gt, fill=0.0, base=0,
                                         channel_multiplier=1), ["mask"], ["mask"])
        s.pool(lambda e: e.tensor_copy(Um[:], mask[:]), ["mask"], ["Um"])
        maskb = T("maskb", [128, 128], BF16)
        s.pool(lambda e: e.memset(mask[:], 1.0), ["Um"], ["mask"])
        s.pool(lambda e: e.affine_select(mask[:], mask[:], pattern=[[1, 128]], compare_op=ALU.is_gt, fill=0.0, base=0,
                                         channel_multiplier=-1), ["mask"], ["mask"])
        s.pool(lambda e: e.tensor_copy(maskb[:], mask[:]), ["mask"], ["maskb"])
        qT = [T(f"qT{i}", [128, S], BF16) for i in range(2)]
        kT = [T(f"kT{i}", [128, S], BF16) for i in range(2)]
        V2 = [T(f"V2{i}", [128, 32, 128], BF16) for i in range(2)]
        Et = [T(f"Et{i}", [128, 512]) for i in range(2)]
        Lt = [T(f"Lt{i}", [128, 512], BF16) for i in range(3)]
        LKt = [T(f"LKt{i}", [128, 512], BF16) for i in range(2)]
        At = [T(f"At{i}", [128, 512]) for i in range(2)]
        Wt = [T(f"Wt{i}", [128, 512], BF16) for i in range(2)]
        Ss = [T(f"Ss{i}", [128, 512], BF16) for i in range(3)]
        acc = [T(f"acc{i}", [128, 4, 128]) for i in range(2)]
        mst = [T(f"mst{i}", [128, 512], BF16) for i in range(2)]
        z_ps = [P(f"z_ps{i}", [128, 512]) for i in range(2)]
        tri_ps = [P(f"tri_ps{i}", [128, 512]) for i in range(3)]
        pv_ps = [P(f"pv_ps{i}", [128, 4, 64]) for i in range(2)]
        tr_ps = P("atr_ps", [128, 512])
        stZ, stA, stB0, stB = [], [], [], []
        it = 0
        rnd = 0

        def keep_warm(nrep):
            def fn(e):
                ins = None
                for _ in range(nrep):
                    ins = e.matmul(tr_ps[:], k.ident_bf[:], k.warm_src[:], start=True, stop=True)
                return ins
            s.pe(fn, [], ["atr_ps"])
        for hp in range(4):
            hb = hp % 2
            for G in range(8):
                ab = G % 2
                for hl in range(2):
                    rb = rnd % 2
                    rnd += 1
                    for j in range(4 * G + 3, -1, -1):
                        first = (j == 4 * G + 3)
                        last = (j == 0)
                        ib = it % 2
                        i3 = it % 3
                        it += 1

                        def Z_(hp=hp, hb=hb, G=G, hl=hl, j=j, ib=ib, first=first):
                            q, kk_, v = qT[hb], kT[hb], V2[hb]
                            qk_, kk_key, vk = f"qT{hb}", f"kT{hb}", f"V2{hb}"
                            if first and G == 0 and hl == 0:
                                s.dma(q[:], dr["QK"][hp * 128:(hp + 1) * 128, :], writes=[qk_])
                                s.dma(kk_[:], dr["QK"][512 + hp * 128:512 + (hp + 1) * 128, :], writes=[kk_key])
                                s.dma(v[:], dr["VA"][:, hp * 128:(hp + 1) * 128].rearrange("(n p) f -> p n f", p=128), writes=[vk])
                            hs = slice(64 * hl, 64 * hl + 64)
                            c0 = max(j - 4 * G, 0) * 128
                            zp = z_ps[ib]
                            s.pe(lambda e: e.matmul(zp[:, c0:512], kk_[hs, j * 128:(j + 1) * 128], q[hs, G * 512 + c0:(G + 1) * 512],
                                                    start=True, stop=True), [qk_, kk_key], [f"z_ps{ib}"])
                            keep_warm(2)

                        def A_(hp=hp, hb=hb, G=G, hl=hl, j=j, ib=ib, i3=i3, rb=rb, first=first):
                            q, kk_, v = qT[hb], kT[hb], V2[hb]
                            qk_, kk_key, vk = f"qT{hb}", f"kT{hb}", f"V2{hb}"
                            so, sn = (j + 1) % 3, j % 3
                            Sk, Snk = f"Ss{so}", f"Ss{sn}"
                            Sm, Sn = Ss[so], Ss[sn]
                            hs = slice(64 * hl, 64 * hl + 64)
                            jl = j - 4 * G
                            c0 = max(jl, 0) * 128
                            cols = slice(c0, 512)
                            zp, tp = z_ps[ib], tri_ps[i3]
                            E, L, LK = Et[ib], Lt[i3], LKt[ib]
                            zk, tk = f"z_ps{ib}", f"tri_ps{i3}"
                            Ek, Lk, LKk = f"Et{ib}", f"Lt{i3}", f"LKt{ib}"
                            s.act(lambda e: e.activation(E[:, cols], zp[:, cols], AF.Exp, scale=-0.125), [zk], [Ek])
                            s.act(lambda e: e.activation(L[:, cols], E[:, cols], AF.Ln, bias=1.0), [Ek], [Lk])
                            s.dve(lambda e: e.scalar_tensor_tensor(LK[:, cols], zp[:, cols], 0.125, L[:, cols], ALU.mult, ALU.add),
                                  [zk, Lk], [LKk])
                            if jl >= 0:
                                s.dve(lambda e: e.tensor_tensor(LK[:, c0:c0 + 128], LK[:, c0:c0 + 128], maskb[:], ALU.mult),
                                      [LKk, "maskb"], [LKk])
                            cc0 = c0 + 128 if jl >= 0 else 0

                            def trif(e):
                                e.matmul(tp[:, cols], Um[:], LK[:, cols], start=True, stop=False)
                                ins = e.matmul(tp[:, cols], k.ident_bf[:], L[:, cols], start=False, stop=(first or cc0 >= 512))
                                if (not first) and cc0 < 512:
                                    ins = e.matmul(tp[:, cc0:512], k.ones_bf[:], Sm[:, cc0:512], start=False, stop=True)
                                return ins
                            if first:
                                s.pe(trif, [LKk, Lk, "Um", "ident_bf"], [tk])
                            else:
                                s.pe(trif, [LKk, Lk, "Um", "ident_bf", Sk, (Sk, 0)], [tk])
                            if j > 0:
                                if jl >= 0:
                                    s.dve(lambda e: e.tensor_copy(Sn[:, c0:c0 + 128], LK[:, c0:c0 + 128]), [LKk], [(Snk, 0)])
                                    if c0 + 128 < 512:
                                        s.dve(lambda e: e.tensor_tensor(Sn[:, c0 + 128:512], Sm[:, c0 + 128:512], LK[:, c0 + 128:512], ALU.add),
                                              [Sk, (Sk, 0), LKk], [Snk])
                                else:
                                    s.dve(lambda e: e.tensor_tensor(Sn[:, cols], Sm[:, cols], LK[:, cols], ALU.add), [Sk, (Sk, 0), LKk], [Snk])

                        def B0_(G=G, j=j, ib=ib, i3=i3):
                            c0 = max(j - 4 * G, 0) * 128
                            cols = slice(c0, 512)
                            pass

                        def B_(hp=hp, hb=hb, G=G, hl=hl, j=j, ib=ib, i3=i3, rb=rb, first=first, last=last, ab=ab):
                            v, vk = V2[hb], f"V2{hb}"
                            hs = slice(64 * hl, 64 * hl + 64)
                            jl = j - 4 * G
                            c0 = max(jl, 0) * 128
                            cols = slice(c0, 512)
                            b0 = max(jl, 0)
                            pp = pv_ps[rb]
                            A, W = At[ib], Wt[ib]
                            pk = f"pv_ps{rb}"
                            Ak, Wk = f"At{ib}", f"Wt{ib}"
                            ac = acc[ab]
                            tp = tri_ps[i3]
                            s.act(lambda e: e.activation(W[:, cols], tp[:, cols], AF.Exp, scale=-1.0), [f"tri_ps{i3}"], [Wk])
                            if jl >= 0:
                                s.dve(lambda e: e.tensor_tensor(W[:, c0:c0 + 128], W[:, c0:c0 + 128], maskb[:], ALU.mult),
                                      [Wk, "maskb"], [Wk])
                            def pvf(e):
                                ins = None
                                for b in range(3, b0 - 1, -1):
                                    ins = e.matmul(pp[:, b, :], W[:, b * 128:(b + 1) * 128], v[:, j, hs],
                                                   start=(first and b == 3), stop=last, skip_group_check=True)
                                return ins
                            s.pe(pvf, [Wk, vk], [pk])
                            if last:
                                s.act(lambda e: e.copy(ac[:, :, hs], pp[:]), [pk], [(f"acc{ab}", hl)])
                                if hl == 1:
                                    def trf(e):
                                        ins = None
                                        for b in range(4):
                                            ins = e.transpose(tr_ps[:, b * 128:(b + 1) * 128], ac[:, b, :], k.ident[:])
                                        return ins
                                    s.pe(trf, [(f"acc{ab}", 0), (f"acc{ab}", 1), "ident"], ["atr_ps"])
                                    ms = mst[G % 2]
                                    s.dve(lambda e: e.tensor_copy(ms[:], tr_ps[:]), ["atr_ps"], [f"mst{G % 2}"])
                                    s.dma(dr["MT"][hp * 128:(hp + 1) * 128, G * 512:(G + 1) * 512], ms[:], reads=[f"mst{G % 2}"],
                                          writes=[("MT", hp, G)])
                        stZ.append(Z_)
                        stA.append(A_)
                        stB0.append(B0_)
                        stB.append(B_)
        n = len(stA)
        for t in range(-3, n):
            if 0 <= t + 3 < n:
                stZ[t + 3]()
            if 0 <= t < n:
                stB0[t]()
            if 0 <= t + 2 < n:
                stA[t + 2]()
            if 0 <= t < n:
                stB[t]()
        s.emit_phase()


def phase_hgrn(k):
    nc, s, dr = k.nc, k.s, k.dr
    with ExitStack() as es:
        T = lambda name, shape, dt=F32: es.enter_context(nc.sbuf_tensor(f"p{k.pid}_{name}", shape, dt))
        P = lambda name, shape, dt=F32: es.enter_context(nc.psum_tensor(f"p{k.pid}_{name}", shape, dt))
        rmask = T("rmask", [128, S])
        s.pool(lambda e: e.memset(rmask[:], 1.0), [], ["rmask"])
        s.pool(lambda e: e.memset(rmask[:].rearrange("p (c t) -> p c t", t=64)[:, :, 0:1], 0.0), ["rmask"], ["rmask"])
        mle = T("mle", [128, 64])
        s.pool(lambda e: e.memset(mle[:], 1.0), [], ["mle"])
        for half in range(2):
            s.pool(lambda e, half=half: e.affine_select(mle[64 * half:64 * half + 64, :], mle[64 * half:64 * half + 64, :],
                                                        pattern=[[1, 64]], compare_op=ALU.is_ge, fill=0.0, base=0,
                                                        channel_multiplier=-1), ["mle"], ["mle"])
        a1 = T("a1", [128, S]); a2 = T("a2", [128, S]); a3 = T("a3", [128, S]); a4 = T("a4", [128, S])
        kh = T("kh", [128, S], BF16)
        qt = [T(f"qt{i}", [128, S], BF16) for i in range(2)]
        kt = [T(f"kt{i}", [128, S], BF16) for i in range(2)]
        khT = [T(f"khT{i}", [128, 32, 128], BF16) for i in range(2)]
        ibt = [T(f"ibt{i}", [128, 32, 128], BF16) for i in range(2)]
        elast = [T(f"elast{i}", [128, 64]) for i in range(2)]
        St = [T(f"St{i}", [128, 128]) for i in range(2)]
        Sb = [[T(f"Sb{i}_{j}", [128, 128], BF16) for j in range(2)] for i in range(2)]
        sT = [T(f"sT{i}", [128, 64], BF16) for i in range(2)]
        Ob = [[T(f"Ob{i}_{pb}", [128, 512]) for pb in range(2)] for i in range(2)]
        gbk = [T(f"gbk{i}", [128, 512]) for i in range(2)]
        sqo = T("sqo", [128, 512], BF16)
        rs = T("hrs", [128, 512])
        t1 = T("ht1", [128, 512])
        obf = [T(f"obf{i}", [128, 512], BF16) for i in range(2)]
        sc_ps = [P(f"sc_ps{i}", [128, 64]) for i in range(2)]
        o_ps = [P(f"o_ps{i}", [128, 64]) for i in range(2)]
        kv_ps = [P(f"kv_ps{i}", [128, 128]) for i in range(2)]
        ms_ps = P("ms_ps", [128, 512])
        ktr_ps = P("ktr_ps", [128, 512], BF16)
        a4v = a4[:].rearrange("p (c t) -> p c t", t=64)
        fin = 0
        for hg in range(2):
            for i in range(2):
                hd = 2 * hg + i
                rows = slice(hd * 128, (hd + 1) * 128)
                s.dma(a1[:], dr["FB"][rows, :], writes=["a1"])
                s.dma(a2[:], dr["QB"][rows, :], writes=["a2"])
                s.dma(ibt[i][:], dr["IB"][:, rows].rearrange("(n p) f -> p n f", p=128), writes=[f"ibt{i}"])
                s.act(lambda e: e.activation(a1[:], a1[:], AF.Sigmoid), ["a1"], ["a1"])
                s.dve(lambda e, hd=hd: e.tensor_scalar(a1[:], a1[:], k.oml[:, hd:hd + 1], k.lb[:, hd:hd + 1], ALU.mult, ALU.add), ["a1"], ["a1"])
                s.dve(lambda e: e.tensor_scalar(a3[:], a1[:], -1.0, 1.0, ALU.mult, ALU.add), ["a1"], ["a3"])
                s.act(lambda e: e.activation(a1[:], a1[:], AF.Ln), ["a1"], ["a1"])
                s.dve(lambda e: e.tensor_tensor_scan(a4[:], rmask[:], a1[:], 0.0, ALU.mult, ALU.add), ["a1", "rmask"], ["a4"])
                s.act(lambda e: e.activation(a2[:], a2[:], AF.Silu), ["a2"], ["a2"])
                s.act(lambda e: e.activation(a1[:], a4[:], AF.Exp), ["a4"], ["a1"])
                s.dve(lambda e, i=i: e.tensor_tensor(qt[i][:], a2[:], a1[:], ALU.mult), ["a1", "a2"], [f"qt{i}"])
                s.act(lambda e: e.activation(a1[:], a4[:], AF.Exp, scale=-1.0), ["a4", f"qt{i}"], ["a1"])
                s.dve(lambda e, i=i: e.tensor_tensor(kt[i][:], a3[:], a1[:], ALU.mult), ["a1", "a3"], [f"kt{i}"])
                s.act(lambda e, i=i: e.activation(elast[i][:], a4v[:, :, 63], AF.Exp), ["a4"], [f"elast{i}"])
                s.dve(lambda e: e.tensor_tensor(a1[:].rearrange("p (c t) -> p c t", t=64), a4v[:, :, 63:64].to_broadcast([128, 64, 64]),
                                                a4v, ALU.subtract), ["a4", f"kt{i}"], ["a1"])
                s.act(lambda e: e.activation(a1[:], a1[:], AF.Exp), ["a1"], ["a1"])
                s.dve(lambda e: e.tensor_tensor(kh[:], a3[:], a1[:], ALU.mult), ["a1", "a3"], ["kh"])
                for n4 in range(8):
                    def trf(e, n4=n4):
                        ins = None
                        for a in range(4):
                            n = n4 * 4 + a
                            ins = e.transpose(ktr_ps[:, a * 128:(a + 1) * 128], kh[:, n * 128:(n + 1) * 128], k.ident_bf[:])
                        return ins
                    s.pe(trf, ["kh", "ident_bf"], ["ktr_ps"])
                    s.act(lambda e, i=i, n4=n4: e.copy(khT[i][:, n4 * 4:(n4 + 1) * 4, :], ktr_ps[:].rearrange("p (a f) -> p a f", f=128)),
                          ["ktr_ps"], [(f"khT{i}", n4)])
            pend = [None]
            for c in range(64):
                n, half = c // 2, c % 2
                pbs = slice(64 * half, 64 * half + 64)
                cs_ = slice(c * 64, (c + 1) * 64)
                kb, cl = c // 8, c % 8
                for i in range(2):
                    hd = 2 * hg + i
                    O = Ob[i][kb % 2]
                    Ok = f"Ob{i}_{kb % 2}"
                    sbn, sbo = Sb[i][c % 2], Sb[i][(c + 1) % 2]
                    sbnk, sbok = f"Sb{i}_{c % 2}", f"Sb{i}_{(c + 1) % 2}"
                    s.pe(lambda e, i=i, pbs=pbs, cs_=cs_: e.matmul(sc_ps[i][pbs, :], kt[i][:, cs_], qt[i][:, cs_], start=True, stop=True),
                         [f"kt{i}", f"qt{i}"], [f"sc_ps{i}"])
                    def warm(e):
                        e.matmul(ms_ps[:], k.ident_bf[:], k.warm_src[:], start=True, stop=True)
                        return e.matmul(ms_ps[:], k.ident_bf[:], k.warm_src[:], start=True, stop=True)
                    s.pe(warm, [], ["ms_ps"])
                    if c < 63:
                        s.pe(lambda e, i=i, pbs=pbs, n=n: e.matmul(kv_ps[i][:], khT[i][pbs, n, :], ibt[i][pbs, n, :], start=True, stop=True),
                             [(f"khT{i}", n // 4), f"ibt{i}"], [f"kv_ps{i}"])
                    s.dve(lambda e, i=i, pbs=pbs: e.tensor_tensor(sT[i][pbs, :], sc_ps[i][pbs, :], mle[pbs, :], ALU.mult),
                          [f"sc_ps{i}", "mle"], [f"sT{i}"])
                    if c < 63:
                        if c == 0:
                            s.dve(lambda e, i=i: e.tensor_copy(St[i][:], kv_ps[i][:]), [f"kv_ps{i}"], [f"St{i}"])
                        else:
                            s.dve(lambda e, i=i, c=c: e.scalar_tensor_tensor(St[i][:], St[i][:], elast[i][:, c:c + 1], kv_ps[i][:],
                                                                              ALU.mult, ALU.add), [f"kv_ps{i}", f"St{i}", f"elast{i}"], [f"St{i}"])
                        s.act(lambda e, i=i, sbn=sbn: e.copy(sbn[:], St[i][:]), [f"St{i}"], [sbnk])
                    def late_(i=i, pbs=pbs, n=n, c=c, cs_=cs_, sbo=sbo, sbok=sbok, O=O, Ok=Ok, cl=cl):
                        pairs = [(ibt[i][pbs, n, :], sT[i][pbs, :])]
                        rd = [f"ibt{i}", f"sT{i}", f"qt{i}"]
                        if c > 0:
                            pairs.append((sbo[:], qt[i][:, cs_]))
                            rd.append(sbok)
                        mm_group(s, o_ps[i][:], pairs, rd, [f"o_ps{i}"])
                        s.act(lambda e: e.copy(O[:, cl * 64:(cl + 1) * 64], o_ps[i][:]), [f"o_ps{i}"], [(Ok, cl)])
                    if pend[0] is not None:
                        pend[0]()
                    pend[0] = late_
                    if cl == 7:
                        pend[0]()
                        pend[0] = None
                    if cl == 7:
                        rows = slice(hd * 128, (hd + 1) * 128)
                        tcols = slice(kb * 512, (kb + 1) * 512)
                        fb_ = fin % 2
                        fin += 1
                        Oks = [(Ok, x) for x in range(8)]
                        s.dma(gbk[fb_][:], dr["GB"][rows, tcols], writes=[f"gbk{fb_}"])
                        s.act(lambda e, O=O: e.activation(sqo[:], O[:], AF.Square), Oks, ["sqo"])
                        s.pe(lambda e: e.matmul(ms_ps[:], k.ones_bf[:], sqo[:], start=True, stop=True), ["sqo"], ["ms_ps"])
                        s.act(lambda e: e.activation(rs[:], ms_ps[:], AF.Sqrt, bias=k.eps_col[:, 0:1], scale=1.0 / 128), ["ms_ps"], ["hrs"])
                        s.dve(lambda e: e.reciprocal(rs[:], rs[:]), ["hrs"], ["hrs"])
                        s.act(lambda e, fb_=fb_: e.activation(gbk[fb_][:], gbk[fb_][:], AF.Silu), [f"gbk{fb_}"], [f"gbk{fb_}"])
                        s.dve(lambda e, O=O, hd=hd: e.scalar_tensor_tensor(t1[:], O[:], k.hgg[:, hd:hd + 1], rs[:], ALU.mult, ALU.mult),
                              Oks + ["hrs"], ["ht1"])
                        s.dve(lambda e, fb_=fb_: e.tensor_tensor(obf[fb_][:], t1[:], gbk[fb_][:], ALU.mult), ["ht1", f"gbk{fb_}"], [f"obf{fb_}"])
                        s.dma(dr["MT"][512 + hd * 128:512 + (hd + 1) * 128, tcols], obf[fb_][:], reads=[f"obf{fb_}"], writes=[("MTB", hd, kb)])
        s.emit_phase()


def phase_b1(k, layer):
    nc, s, din, dr = k.nc, k.s, k.din, k.dr
    TQ = 256
    NQ = S // TQ
    with ExitStack() as es:
        T = lambda name, shape, dt=F32: es.enter_context(nc.sbuf_tensor(f"p{k.pid}_{name}", shape, dt))
        P = lambda name, shape, dt=F32: es.enter_context(nc.psum_tensor(f"p{k.pid}_{name}", shape, dt))
        ncol = 1024 if layer == 0 else 2048
        w = T("w_b1", [128, 8, ncol], BF16)
        load_weight_bf16(k, w, din["hy_w_out"][0] if layer == 0 else din["s5_w_glu"][0], 8, ncol, "w")
        WK = wkeys("w", 8, ncol)
        load_weight_bf16(k, k.pre["ffn_w1"], din["ffn_w_in"][layer], 8, 2 * DFF, "pw1")
        load_weight_bf16(k, k.pre["ffn_w2"], din["ffn_w_out"][layer], 22, D, "pw2")
        src = (dr["MT"] if layer == 0 else dr["YG"]).rearrange("(c p) t -> p c t", p=128)
        XTv = dr["XT"].rearrange("(c p) t -> p c t", p=128)
        g = k.mod[:, layer, 16:24]
        mT = [T(f"mT{i}", [128, 8, TQ], BF16) for i in range(2)]
        X = [T(f"X{i}", [128, 8, TQ]) for i in range(2)]
        sg = [T(f"sg{i}", [128, TQ]) for i in range(2)]
        mix = [T(f"mix{i}", [128, TQ]) for i in range(2)]
        v_ps = [P(f"v_ps{i}", [128, 512]) for i in range(2)]
        g_ps = [P(f"g_ps{i}", [128, 512]) for i in range(2)]

        def load(n):
            pb = n % 2
            tc_ = slice(n * TQ, (n + 1) * TQ)
            s.dma(mT[pb][:], src[:, :, tc_], writes=[f"mT{pb}"])
            s.dma(X[pb][:], XTv[:, :, tc_], writes=[(f"X{pb}", c) for c in range(8)])
        it = 0
        load(0)
        for n in range(NQ):
            pb = n % 2
            tc_ = slice(n * TQ, (n + 1) * TQ)
            if n + 1 < NQ:
                load(n + 1)
            for dc in range(8):
                ib = it % 2
                it += 1
                vp, gp = v_ps[ib][:, 0:TQ], g_ps[ib][:, 0:TQ]
                mm_group(s, vp, [(w[:, kk, dc * 128:(dc + 1) * 128], mT[pb][:, kk, :]) for kk in range(8)],
                         WK + [f"mT{pb}"], [f"v_ps{ib}"])
                if layer == 0:
                    s.dve(lambda e, pb=pb, dc=dc, vp=vp: e.scalar_tensor_tensor(X[pb][:, dc, :], vp, g[:, dc:dc + 1], X[pb][:, dc, :],
                                                                                ALU.mult, ALU.add), [f"v_ps{ib}", (f"X{pb}", dc)], [(f"X{pb}", dc)])
                else:
                    mm_group(s, gp, [(w[:, kk, 1024 + dc * 128:1024 + (dc + 1) * 128], mT[pb][:, kk, :]) for kk in range(8)],
                             WK + [f"mT{pb}"], [f"g_ps{ib}"])
                    s.act(lambda e, ib=ib, gp=gp: e.activation(sg[ib][:], gp, AF.Sigmoid), [f"g_ps{ib}"], [f"sg{ib}"])
                    s.dve(lambda e, ib=ib, vp=vp: e.tensor_tensor(mix[ib][:], vp, sg[ib][:], ALU.mult), [f"v_ps{ib}", f"sg{ib}"], [f"mix{ib}"])
                    s.dve(lambda e, pb=pb, dc=dc, ib=ib: e.scalar_tensor_tensor(X[pb][:, dc, :], mix[ib][:], g[:, dc:dc + 1], X[pb][:, dc, :],
                                                                                ALU.mult, ALU.add), [f"mix{ib}", (f"X{pb}", dc)], [(f"X{pb}", dc)])
            s.dma(XTv[:, :, tc_], X[pb][:], reads=[(f"X{pb}", c) for c in range(8)], writes=[("XT", n)])
        s.emit_phase()


def phase_ffn(k, layer, final):
    nc, s, din, dr = k.nc, k.s, k.din, k.dr
    TQ = 256
    NQ = S // TQ
    with ExitStack() as es:
        T = lambda name, shape, dt=F32: es.enter_context(nc.sbuf_tensor(f"p{k.pid}_{name}", shape, dt))
        P = lambda name, shape, dt=F32: es.enter_context(nc.psum_tensor(f"p{k.pid}_{name}", shape, dt))
        w1, w2 = k.pre["ffn_w1"], k.pre["ffn_w2"]
        W1K, W2K = [], []

        def w1k(col):
            return W1K
        NX = 3 if final else 2
        X = [T(f"X{i}", [128, 8, TQ]) for i in range(NX)]
        hT = [T(f"hT{i}", [128, 8, TQ], BF16) for i in range(2)]
        sq = T("sq", [128, 8, TQ], BF16)
        a = T("a", [128, 22, TQ], BF16)
        tmps = [T(f"tmp{i}", [128, TQ]) for i in range(2)]
        rstd = [T(f"rstd{i}", [128, TQ]) for i in range(2)]
        rstdf = T("rstdf", [128, TQ])
        sg = [T(f"sg{i}", [128, TQ]) for i in range(2)]
        ys = [T(f"ys{i}", [128, 1024]) for i in range(2)] if final else None
        mmb = [P(f"mmb{i}", [128, 512]) for i in range(4)]
        ob = [P(f"ob{i}", [128, 512]) for i in range(2)]
        ssb = P("ssb", [128, 512])
        trp = [P(f"trp{i}", [128, 512]) for i in range(1)] * 2 if final else None
        G = k.gf[:, layer, :]
        SH = k.mod[:, layer, 24:32]
        gate = k.mod[:, layer, 40:48]
        XTv = dr["XT"].rearrange("(c p) t -> p c t", p=128)
        cnt = {"mi": 0, "oi": 0, "yi": 0, "ti": 0}

        def load(n):
            xb = n % NX
            s.dma(X[xb][:], XTv[:, :, n * TQ:(n + 1) * TQ], writes=[(f"X{xb}", c) for c in range(8)])

        def norm(n):
            pb = n % 2
            xb = n % NX
            Xk = [(f"X{xb}", c) for c in range(8)]
            ssp = ssb[:, 0:TQ]
            for c in range(8):
                s.act(lambda e, c=c: e.activation(sq[:, c, :], X[xb][:, c, :], AF.Square), [Xk[c]], [("sq", c)])
            mm_group(s, ssp, [(k.ones_bf[:], sq[:, c, :]) for c in range(8)], [("sq", c) for c in range(8)], ["ssb"])
            s.act(lambda e: e.activation(rstd[pb][:], ssp, AF.Sqrt, bias=k.eps_col[:, 0:1], scale=1.0 / D), ["ssb"], [f"rs0{pb}"])
            s.dve(lambda e: e.reciprocal(rstd[pb][:], rstd[pb][:]), [f"rs0{pb}"], [f"rstd{pb}"])
            for c in range(8):
                tm, tk = tmps[c % 2], f"tmp{c % 2}"
                s.dve(lambda e, c=c, tm=tm: e.scalar_tensor_tensor(tm[:], X[xb][:, c, :], G[:, c:c + 1], rstd[pb][:], ALU.mult, ALU.mult),
                      [Xk[c], f"rstd{pb}"], [tk])
                s.act(lambda e, c=c, tm=tm: e.activation(hT[pb][:, c, :], tm[:], AF.Identity, bias=SH[:, c:c + 1], scale=1.0),
                      [tk], [(f"hT{pb}", c)])

        def gu(n):
            pb = n % 2
            hk = [(f"hT{pb}", c) for c in range(8)]
            h = hT[pb]
            out = []
            for j in range(22):
                out.append(lambda j=j: gu1(j, h, hk))
            return out

        def gu1(j, h, hk):
            if True:
                m0, m1 = cnt["mi"] % 4, (cnt["mi"] + 1) % 4
                cnt["mi"] += 2
                gp, gk = mmb[m0][:, 0:TQ], ("mmb", m0)
                up, uk = mmb[m1][:, 0:TQ], ("mmb", m1)
                mm_group(s, gp, [(w1[:, kk, j * 128:(j + 1) * 128], h[:, kk, :]) for kk in range(8)], w1k(j * 128) + hk, [gk])
                mm_group(s, up, [(w1[:, kk, DFF + j * 128:DFF + (j + 1) * 128], h[:, kk, :]) for kk in range(8)], w1k(DFF + j * 128) + hk, [uk])
                sb = j % 2
                s.act(lambda e, sb=sb, gp=gp: e.activation(sg[sb][:], gp, AF.Silu), [gk], [f"sg{sb}"])
                s.dve(lambda e, sb=sb, up=up, j=j: e.tensor_tensor(a[:, j, :], up, sg[sb][:], ALU.mult), [uk, f"sg{sb}"], [("a", j)])

        def down(n):
            pb = n % NX
            ak = [("a", j) for j in range(22)]
            for dc in range(8):
                o0 = cnt["oi"] % 2
                cnt["oi"] += 1
                op_, ok = ob[o0][:, 0:TQ], ("ob", o0)
                mm_group(s, op_, [(w2[:, j, dc * 128:(dc + 1) * 128], a[:, j, :]) for j in range(22)], W2K + ak, [ok])
                s.dve(lambda e, dc=dc, op_=op_: e.scalar_tensor_tensor(X[pb][:, dc, :], op_, gate[:, dc:dc + 1], X[pb][:, dc, :], ALU.mult, ALU.add),
                      [ok, (f"X{pb}", dc)], [(f"X{pb}", dc)])

        def finish(n):
            pb = n % NX
            Xk = [(f"X{pb}", c) for c in range(8)]
            s.dma(XTv[:, :, n * TQ:(n + 1) * TQ], X[pb][:], reads=Xk, writes=[("XT", n)])

        def fin_a(n):
            pb = n % NX
            rb = n % 2
            Xk = [(f"X{pb}", c) for c in range(8)]
            ssp = ssb[:, 0:TQ]
            for c in range(8):
                s.act(lambda e, c=c: e.activation(sq[:, c, :], X[pb][:, c, :], AF.Square), [Xk[c]], [("sq", c)])
            mm_group(s, ssp, [(k.ones_bf[:], sq[:, c, :]) for c in range(8)], [("sq", c) for c in range(8)], ["ssb"])
            s.act(lambda e: e.activation(rstdf[:], ssp, AF.Sqrt, bias=k.eps_col[:, 0:1], scale=1.0 / D), ["ssb"], ["rsf0"])
            s.dve(lambda e: e.reciprocal(rstdf[:], rstdf[:]), ["rsf0"], ["rstdf"])
            for c in range(8):
                s.dve(lambda e, c=c: e.scalar_tensor_tensor(X[pb][:, c, :], X[pb][:, c, :], k.fng[:, c:c + 1], rstdf[:], ALU.mult, ALU.mult),
                      [Xk[c], "rstdf"], [Xk[c]])

        def fin_b(n):
            pb = n % NX
            Xk = [(f"X{pb}", c) for c in range(8)]
            for a2 in range(TQ // 128):
                y, yk = ys[cnt["yi"] % 2], f"ys{cnt['yi'] % 2}"
                cnt["yi"] += 1
                for hh in range(2):
                    tp, tk = trp[0], "trp0"
                    cnt["ti"] += 1

                    def trf(e, tp=tp, hh=hh, a2=a2):
                        ins = None
                        for cc in range(4):
                            c = hh * 4 + cc
                            ins = e.transpose(tp[:, cc * 128:(cc + 1) * 128], X[pb][:, c, a2 * 128:(a2 + 1) * 128], k.ident[:])
                        return ins
                    s.pe(trf, Xk + ["ident"], [tk])
                    if hh == 0:
                        s.act(lambda e, y=y, tp=tp, hh=hh: e.copy(y[:, hh * 512:(hh + 1) * 512], tp[:]), [tk], [(yk, hh)])
                    else:
                        s.dve(lambda e, y=y, tp=tp, hh=hh: e.tensor_copy(y[:, hh * 512:(hh + 1) * 512], tp[:]), [tk], [(yk, hh)])
                r0 = n * TQ + a2 * 128
                s.dma(k.out[r0:r0 + 128, :], y[:], reads=[(yk, 0), (yk, 1)], writes=[("out", r0)])

        load(0)
        norm(0)
        for n in range(NQ):
            if n + 1 < NQ:
                load(n + 1)
            gs = gu(n)
            if final and n > 0:
                for g_ in gs[:5]:
                    g_()
                fin_a(n - 1)
                for g_ in gs[5:14]:
                    g_()
                fin_b(n - 1)
                for g_ in gs[14:]:
                    g_()
            else:
                for g_ in gs:
                    g_()
            if n + 1 < NQ:
                norm(n + 1)
            down(n)
            if not final:
                finish(n)
        if final:
            fin_a(NQ - 1)
            fin_b(NQ - 1)
        s.emit_phase()


def norm256(k, n, X, hT, sq, tmps, rstd, ssb, G, SH):
    s = k.s
    pb = n % 2
    Xk = [(f"X{pb}", c) for c in range(8)]
    ssp = ssb[:, 0:256]
    for c in range(8):
        s.act(lambda e, c=c: e.activation(sq[:, c, :], X[pb][:, c, :], AF.Square), [Xk[c]], [("sq", c)])
    mm_group(s, ssp, [(k.ones_bf[:], sq[:, c, :]) for c in range(8)], [("sq", c) for c in range(8)], ["ssb"])
    s.act(lambda e: e.activation(rstd[pb][:], ssp, AF.Sqrt, bias=k.eps_col[:, 0:1], scale=1.0 / D), ["ssb"], [f"rs0{pb}"])
    s.dve(lambda e: e.reciprocal(rstd[pb][:], rstd[pb][:]), [f"rs0{pb}"], [f"rstd{pb}"])
    for c in range(8):
        tm, tk = tmps[c % 2], f"tmp{c % 2}"
        s.dve(lambda e, c=c, tm=tm: e.scalar_tensor_tensor(tm[:], X[pb][:, c, :], G[:, c:c + 1], rstd[pb][:], ALU.mult, ALU.mult),
              [Xk[c], f"rstd{pb}"], [tk])
        s.act(lambda e, c=c, tm=tm: e.activation(hT[pb][:, c, :], tm[:], AF.Identity, bias=SH[:, c:c + 1], scale=1.0),
              [tk], [(f"hT{pb}", c)])


def phase_l1a(k):
    nc, s, din, dr = k.nc, k.s, k.din, k.dr
    TQ = 256
    NQ = S // TQ
    with ExitStack() as es:
        T = lambda name, shape, dt=F32: es.enter_context(nc.sbuf_tensor(f"p{k.pid}_{name}", shape, dt))
        P = lambda name, shape, dt=F32: es.enter_context(nc.psum_tensor(f"p{k.pid}_{name}", shape, dt))
        w = T("w_s5in", [128, 8, D], BF16)
        load_weight_bf16(k, w, din["s5_w_in"][0], 8, D, "w")
        WK = wkeys("w", 8, D)
        X = [T(f"X{i}", [128, 8, TQ]) for i in range(2)]
        hT = [T(f"hT{i}", [128, 8, TQ], BF16) for i in range(2)]
        sq = T("sq", [128, 8, TQ], BF16)
        tmps = [T(f"tmp{i}", [128, TQ]) for i in range(2)]
        rstd = [T(f"rstd{i}", [128, TQ]) for i in range(2)]
        st = [T(f"st{i}", [128, 8, TQ]) for i in range(2)]
        mm_ps = [P(f"mm_ps{i}", [128, 512]) for i in range(4)]
        ssb = P("ssb", [128, 512])
        XTv = dr["XT"].rearrange("(c p) t -> p c t", p=128)
        Uv = dr["U"].rearrange("(c p) t -> p c t", p=128)

        def load(n):
            pb = n % 2
            s.dma(X[pb][:], XTv[:, :, n * TQ:(n + 1) * TQ], writes=[(f"X{pb}", c) for c in range(8)])
        mi = 0
        load(0)
        norm256(k, 0, X, hT, sq, tmps, rstd, ssb, k.gm[:, 1, :], k.mod[:, 1, 0:8])
        for n in range(NQ):
            pb = n % 2
            if n + 1 < NQ:
                load(n + 1)
            hk = [(f"hT{pb}", c) for c in range(8)]
            for j in range(8):
                ps, pk = mm_ps[mi % 4][:, 0:TQ], f"mm_ps{mi % 4}"
                mi += 1
                mm_group(s, ps, [(w[:, kk, j * 128:(j + 1) * 128], hT[pb][:, kk, :]) for kk in range(8)], WK + hk, [pk])
                if j % 2 == 0:
                    s.act(lambda e, ps=ps, j=j, pb=pb: e.copy(st[pb][:, j, :], ps), [pk], [(f"st{pb}", j)])
                else:
                    s.dve(lambda e, ps=ps, j=j, pb=pb: e.tensor_copy(st[pb][:, j, :], ps), [pk], [(f"st{pb}", j)])
            if n + 1 < NQ:
                norm256(k, n + 1, X, hT, sq, tmps, rstd, ssb, k.gm[:, 1, :], k.mod[:, 1, 0:8])
            s.dma(Uv[:, :, n * TQ:(n + 1) * TQ], st[pb][:], reads=[(f"st{pb}", j) for j in range(8)], writes=[("U", n)])
        s.emit_phase()


def phase_s5(k):
    nc, s, din, dr = k.nc, k.s, k.din, k.dr
    with ExitStack() as es:
        T = lambda name, shape, dt=F32: es.enter_context(nc.sbuf_tensor(f"p{k.pid}_{name}", shape, dt))
        P = lambda name, shape, dt=F32: es.enter_context(nc.psum_tensor(f"p{k.pid}_{name}", shape, dt))
        LRe = T("LRe", [128, 64]); LIm = T("LIm", [128, 64]); dtb = T("dtb", [128, 64])
        for hf in range(2):
            ps_ = slice(64 * hf, 64 * hf + 64)
            s.dma(LRe[ps_, :], din["s5_lam_re"][0].rearrange("g p -> p g"), writes=[("LRe", hf)])
            s.dma(LIm[ps_, :], din["s5_lam_im"][0].rearrange("g p -> p g"), writes=[("LIm", hf)])
        s.dma(dtb[:], din["s5_log_dt"].to_broadcast([128, 64]), writes=["dtb"])
        Bre = T("Bre", [128, 64, 16]); Bim = T("Bim", [128, 64, 16])
        for hf in range(2):
            ps_ = slice(64 * hf, 64 * hf + 64)
            for g8 in range(8):
                gs = slice(g8 * 8, (g8 + 1) * 8)
                s.dma(Bre[ps_, gs, :], din["s5_b_re"][0, gs].rearrange("g p h -> p g h"), writes=[("Bre", hf, g8)])
                s.dma(Bim[ps_, gs, :], din["s5_b_im"][0, gs].rearrange("g p h -> p g h"), writes=[("Bim", hf, g8)])
        BreK = [("Bre", hf, g8) for hf in range(2) for g8 in range(8)]
        BimK = [("Bim", hf, g8) for hf in range(2) for g8 in range(8)]
        CCa = T("CCa", [128, 8, 128]); CCb = T("CCb", [128, 8, 128])
        cre = din["s5_c_re"][0].rearrange("g h p -> (g h) p").rearrange("(b q) p -> q b p", q=128)
        cim = din["s5_c_im"][0].rearrange("g h p -> (g h) p").rearrange("(b q) p -> q b p", q=128)
        s.dma(CCa[:, :, 0:64], cre, writes=[("CCa", 0)]); s.dma(CCa[:, :, 64:128], cim, writes=[("CCa", 1)])
        s.dma(CCb[:, :, 0:64], cim, writes=[("CCb", 0)]); s.dma(CCb[:, :, 64:128], cre, writes=[("CCb", 1)])
        LReK = [("LRe", 0), ("LRe", 1)]; LImK = [("LIm", 0), ("LIm", 1)]
        sm = {}

        def SM(name):
            sm[name] = T("sm_" + name, [128, 64])
            return sm[name]
        for nm in ("lr", "mag", "phi", "phis", "t0", "t1", "fs", "fc", "sinv", "cosv", "are", "aim", "den", "zre", "zim",
                   "za", "zb", "zas", "zbs", "c512", "s512"):
            SM(nm)
        smi = T("smi", [128, 64], I32)
        sgnA = T("sgnA", [128, 1])
        s.pool(lambda e: e.memset(sgnA[0:64, :], 1.0), [], [("sgnA", 0)])
        s.pool(lambda e: e.memset(sgnA[64:128, :], -1.0), [], [("sgnA", 1)])
        SGK = [("sgnA", 0), ("sgnA", 1)]
        s.act(lambda e: e.activation(dtb[:], dtb[:], AF.Exp), ["dtb"], ["dtb"])
        s.dve(lambda e: e.tensor_scalar(sm["lr"][:], LRe[:], -1e-4, None, ALU.min), LReK, ["lr"])
        s.dve(lambda e: e.tensor_tensor(sm["t0"][:], sm["lr"][:], dtb[:], ALU.mult), ["lr", "dtb"], ["t0"])
        s.act(lambda e: e.activation(sm["mag"][:], sm["t0"][:], AF.Exp), ["t0"], ["mag"])
        s.dve(lambda e: e.scalar_tensor_tensor(sm["phi"][:], LIm[:], 1.0 / TWO_PI, dtb[:], ALU.mult, ALU.mult), LImK + ["dtb"], ["phi"])
        s.dve(lambda e: e.tensor_scalar(sm["phis"][:], sm["phi"][:], sgnA[:, 0:1], None, ALU.mult), ["phi"] + SGK, ["phis"])

        def frac_sin(dst, src, add, key_src, key_dst, mult=1.0):
            s.dve(lambda e: e.tensor_scalar(sm["t0"][:], sm[src][:], mult, add, ALU.mult, ALU.add), [key_src], ["t0"])
            s.dve(lambda e: e.tensor_copy(smi[:], sm["t0"][:]), ["t0"], ["smi"])
            s.dve(lambda e: e.tensor_tensor(sm["t1"][:], sm["t0"][:], smi[:], ALU.subtract), ["t0", "smi"], ["t1"])
            s.act(lambda e: e.activation(sm[dst][:], sm["t1"][:], AF.Sin, scale=TWO_PI), ["t1"], [key_dst])
        frac_sin("sinv", "phi", 0.0, "phi", "sinv")
        frac_sin("cosv", "phi", 0.25, "phi", "cosv")
        frac_sin("c512", "phi", 0.25, "phi", "c512", mult=512.0)
        frac_sin("s512", "phis", 0.0, "phis", "s512", mult=512.0)
        s.dve(lambda e: e.tensor_tensor(sm["are"][:], sm["mag"][:], sm["cosv"][:], ALU.mult), ["mag", "cosv"], ["are"])
        s.dve(lambda e: e.tensor_tensor(sm["aim"][:], sm["mag"][:], sm["sinv"][:], ALU.mult), ["mag", "sinv"], ["aim"])
        s.dve(lambda e: e.tensor_scalar(sm["are"][:], sm["are"][:], -1.0, None, ALU.add), ["are"], ["are"])
        s.dve(lambda e: e.tensor_tensor(sm["den"][:], sm["lr"][:], sm["lr"][:], ALU.mult), ["lr"], ["den"])
        s.dve(lambda e: e.tensor_tensor(sm["t0"][:], LIm[:], LIm[:], ALU.mult), LImK, ["t0"])
        s.dve(lambda e: e.tensor_tensor(sm["den"][:], sm["den"][:], sm["t0"][:], ALU.add), ["den", "t0"], ["den"])
        s.dve(lambda e: e.reciprocal(sm["den"][:], sm["den"][:]), ["den"], ["den"])
        s.dve(lambda e: e.tensor_tensor(sm["t0"][:], sm["are"][:], sm["lr"][:], ALU.mult), ["are", "lr"], ["t0"])
        s.dve(lambda e: e.tensor_tensor(sm["t1"][:], sm["aim"][:], LIm[:], ALU.mult), ["aim"] + LImK, ["t1"])
        s.dve(lambda e: e.tensor_tensor(sm["zre"][:], sm["t0"][:], sm["t1"][:], ALU.add), ["t0", "t1"], ["zre"])
        s.dve(lambda e: e.tensor_tensor(sm["zre"][:], sm["zre"][:], sm["den"][:], ALU.mult), ["zre", "den"], ["zre"])
        s.dve(lambda e: e.tensor_tensor(sm["t0"][:], sm["aim"][:], sm["lr"][:], ALU.mult), ["aim", "lr", "zre"], ["t0"])
        s.dve(lambda e: e.tensor_tensor(sm["t1"][:], sm["are"][:], LIm[:], ALU.mult), ["are", "zre"] + LImK, ["t1"])
        s.dve(lambda e: e.tensor_tensor(sm["zim"][:], sm["t0"][:], sm["t1"][:], ALU.subtract), ["t0", "t1"], ["zim"])
        s.dve(lambda e: e.tensor_tensor(sm["zim"][:], sm["zim"][:], sm["den"][:], ALU.mult), ["zim", "den"], ["zim"])
        lo, hi = slice(0, 64), slice(64, 128)
        s.dve(lambda e: e.tensor_copy(sm["za"][lo, :], sm["zre"][lo, :]), ["zre"], [("za", 0)])
        s.dve(lambda e: e.tensor_copy(sm["za"][hi, :], sm["zim"][hi, :]), ["zim"], [("za", 1)])
        s.dve(lambda e: e.tensor_scalar(sm["zb"][lo, :], sm["zim"][lo, :], -1.0, None, ALU.mult), ["zim"], [("zb", 0)])
        s.dve(lambda e: e.tensor_copy(sm["zb"][hi, :], sm["zre"][hi, :]), ["zre"], [("zb", 1)])
        s.dve(lambda e: e.tensor_copy(sm["zas"][lo, :], sm["zim"][lo, :]), ["zim"], [("zas", 0)])
        s.dve(lambda e: e.tensor_copy(sm["zas"][hi, :], sm["zre"][hi, :]), ["zre"], [("zas", 1)])
        s.dve(lambda e: e.tensor_copy(sm["zbs"][lo, :], sm["zre"][lo, :]), ["zre"], [("zbs", 0)])
        s.dve(lambda e: e.tensor_scalar(sm["zbs"][hi, :], sm["zim"][hi, :], -1.0, None, ALU.mult), ["zim"], [("zbs", 1)])
        BB = T("BB", [128, 64, 16]); BBs = T("BBs", [128, 64, 16]); bt0 = T("bbt0", [128, 64, 16])

        def mkbb(dst, dkey, za, zb):
            zak = [(za, 0), (za, 1)]; zbk = [(zb, 0), (zb, 1)]
            s.dve(lambda e: e.tensor_tensor(bt0[:], Bre[:], sm[za][:].unsqueeze(2).to_broadcast([128, 64, 16]), ALU.mult), BreK + zak, ["bt0"])
            s.dve(lambda e: e.tensor_tensor(dst[:], Bim[:], sm[zb][:].unsqueeze(2).to_broadcast([128, 64, 16]), ALU.mult), BimK + zbk, [dkey])
            s.dve(lambda e: e.tensor_tensor(dst[:], dst[:], bt0[:], ALU.add), [dkey, "bt0"], [dkey])
        mkbb(BB, "BB", "za", "zb")
        mkbb(BBs, "BBs", "zas", "zbs")
        Bfull = T("Bfull", [128, 8, 128]); Bsfull = T("Bsfull", [128, 8, 128])
        M1full = T("M1full", [128, 8, 128]); M2full = T("M2full", [128, 8, 128])
        ctr_ps = P("ctr_ps", [128, 512])
        BBv = BB[:].rearrange("p (b g) h -> p b (g h)", b=8)
        BBsv = BBs[:].rearrange("p (b g) h -> p b (g h)", b=8)
        for (src, skeys, dst, dkey, sgn) in ((BBv, ["BB"], Bfull, "Bfull", None), (BBsv, ["BBs"], Bsfull, "Bsfull", None),
                                             (CCa[:], [("CCa", 0), ("CCa", 1)], M1full, "M1full", 1.0),
                                             (CCb[:], [("CCb", 0), ("CCb", 1)], M2full, "M2full", -1.0)):
            for b4 in range(2):
                def trf(e, src=src, b4=b4):
                    ins = None
                    for a in range(4):
                        ins = e.transpose(ctr_ps[:, a * 128:(a + 1) * 128], src[:, b4 * 4 + a, :], k.ident[:])
                    return ins
                s.pe(trf, skeys + ["ident"], ["ctr_ps"])
                dv = dst[:, b4 * 4:(b4 + 1) * 4, :]
                pv = ctr_ps[:].rearrange("p (a f) -> p a f", f=128)
                if sgn is None:
                    s.act(lambda e, dv=dv, pv=pv: e.copy(dv, pv), ["ctr_ps"], [(dkey, b4)])
                else:
                    s.dve(lambda e, dv=dv, pv=pv, sgn=sgn: e.tensor_scalar(dv, pv, sgnA[:, 0:1], sgn, ALU.mult, ALU.mult), ["ctr_ps"] + SGK, [(dkey, b4)])
        rowm = T("rowm", [128, 8]); rtmp = T("rtmp", [128, 8]); rm1 = T("rm1", [128, 8])
        s.pool(lambda e: e.iota(rtmp[:], pattern=[[-16, 8]], base=0, channel_multiplier=1, allow_small_or_imprecise_dtypes=True), [], ["rtmp"])
        s.dve(lambda e: e.tensor_single_scalar(rm1[:], rtmp[:], 0.0, ALU.is_ge), ["rtmp"], ["rm1"])
        s.dve(lambda e: e.scalar_tensor_tensor(rowm[:], rtmp[:], 16.0, rm1[:], ALU.is_lt, ALU.mult), ["rtmp", "rm1"], ["rowm"])
        colm = T("colm", [128, 8, 128]); ctmp = T("ctmp", [128, 8, 128]); cm1 = T("cm1", [128, 8, 128])
        s.pool(lambda e: e.iota(ctmp[:], pattern=[[-16, 8], [1, 128]], base=0, channel_multiplier=0, allow_small_or_imprecise_dtypes=True), [], ["ctmp"])
        s.dve(lambda e: e.tensor_single_scalar(cm1[:], ctmp[:], 0.0, ALU.is_ge), ["ctmp"], ["cm1"])
        s.dve(lambda e: e.scalar_tensor_tensor(colm[:], ctmp[:], 16.0, cm1[:], ALU.is_lt, ALU.mult), ["ctmp", "cm1"], ["colm"])
        Jsw = T("Jsw", [128, 128]); jt = T("jt", [128, 128]); je = T("je", [128, 128])
        s.pool(lambda e: e.iota(jt[:], pattern=[[1, 128]], base=0, channel_multiplier=-1, allow_small_or_imprecise_dtypes=True), [], ["jt"])
        s.dve(lambda e: e.tensor_single_scalar(je[:], jt[:], 64.0, ALU.is_equal), ["jt"], ["je"])
        s.dve(lambda e: e.scalar_tensor_tensor(Jsw[:], jt[:], -64.0, je[:], ALU.is_equal, ALU.add), ["jt", "je"], ["Jsw"])
        iot = T("iot", [128, 512])
        s.pool(lambda e: e.iota(iot[:], pattern=[[1, 512]], base=0, channel_multiplier=0, allow_small_or_imprecise_dtypes=True), [], ["iot"])
        u32 = [T(f"u32_{i}", [128, S]) for i in range(2)]
        ub = [T(f"ub{i}", [128, S], BF16) for i in range(2)]
        Bpad = [T(f"Bpad{i}", [128, 8, 128], BF16) for i in range(2)]
        Bspad = [T(f"Bspad{i}", [128, 8, 128], BF16) for i in range(2)]
        M1pad = [T(f"M1pad{i}", [128, 8, 128], BF16) for i in range(2)]
        M2pad = [T(f"M2pad{i}", [128, 8, 128], BF16) for i in range(2)]
        CSb = [T(f"CStabb{i}", [128, 8, 2, 512], BF16) for i in range(2)]
        t12b = [T(f"t12b{i}", [128, 2, 512], BF16) for i in range(2)]
        P12 = [T(f"P12_{i}", [128, 2, 512], BF16) for i in range(2)]
        wb = [T(f"wb{i}", [128, 512], BF16) for i in range(2)]
        Rot = [T(f"Rot{i}", [128, 8, 128]) for i in range(2)]
        vt = T("vt", [128, 512])
        vti = T("vti", [128, 512], I32)
        vf = T("vf", [128, 512])
        wt = [T(f"wt{i}", [128, 512]) for i in range(2)]
        wlast = T("wlast", [128, 8])
        inits = T("inits", [128, 8])
        yt = [T(f"yt{i}", [128, 512]) for i in range(2)]
        yg = [T(f"yg{i}", [128, 512], BF16) for i in range(2)]
        bub = [P(f"bub{i}", [128, 2, 512]) for i in range(2)]
        y_ps = [P(f"y_ps{i}", [128, 512]) for i in range(1)]
        bt_ps = [P(f"bt_ps{i}", [128, 512]) for i in range(1)] * 2
        dm_ps = P("dm_ps", [128, 512])

        def keep_warm(nrep):
            def fn(e):
                ins = None
                for _ in range(nrep):
                    ins = e.matmul(dm_ps[:], k.ident_bf[:], k.warm_src[:], start=True, stop=True)
                return ins
            s.pe(fn, [], [])

        in_ps = ctr_ps
        it = 0

        def load_u(b):
            bp = b % 2
            rows = slice(b * 128, (b + 1) * 128)
            s.dma(u32[bp][:], dr["U"][rows, :], writes=[f"u32_{bp}"])
            s.dma(ub[bp][:], dr["U"][rows, :], writes=[f"ub{bp}"], eng="pool")

        def setup_group(b, gl):
            bp = b % 2
            g = b * 8 + gl
            s.dve(lambda e: e.tensor_scalar(Bpad[bp][:, gl, :], Bfull[:, b, :], rowm[:, gl:gl + 1], None, ALU.mult),
                  [("Bfull", b // 4), "rowm"], [("Bpad", bp, gl)])
            s.dve(lambda e: e.tensor_scalar(Bspad[bp][:, gl, :], Bsfull[:, b, :], rowm[:, gl:gl + 1], None, ALU.mult),
                  [("Bsfull", b // 4), "rowm"], [("Bspad", bp, gl)])
            s.dve(lambda e: e.tensor_tensor(M1pad[bp][:, gl, :], M1full[:, b, :], colm[:, gl, :], ALU.mult),
                  [("M1full", b // 4), "colm"], [("M1pad", bp, gl)])
            s.dve(lambda e: e.tensor_tensor(M2pad[bp][:, gl, :], M2full[:, b, :], colm[:, gl, :], ALU.mult),
                  [("M2full", b // 4), "colm"], [("M2pad", bp, gl)])
            for (ti, ph, add) in ((0, "phi", 0.25), (1, "phis", 0.0)):
                s.act(lambda e, ph=ph, add=add: e.activation(vt[:], iot[:], AF.Identity, bias=add, scale=sm[ph][:, g:g + 1]), ["iot", ph], ["vt"])
                s.act(lambda e: e.copy(vti[:], vt[:]), ["vt"], ["vti"])
                s.pool(lambda e: e.tensor_tensor(vf[:], vt[:], vti[:], ALU.subtract), ["vt", "vti"], ["vf"])
                s.act(lambda e, ti=ti: e.activation(CSb[bp][:, gl, ti, :], vf[:], AF.Sin, scale=TWO_PI), ["vf"], [("CSb", bp, gl, ti)])
            s.dve(lambda e: e.tensor_scalar(Rot[bp][:, gl, :], k.ident[:], sm["c512"][:, g:g + 1], None, ALU.mult),
                  ["ident", "c512"], [("Rot", bp, gl)])
            s.dve(lambda e: e.scalar_tensor_tensor(Rot[bp][:, gl, :], Jsw[:], sm["s512"][:, g:g + 1], Rot[bp][:, gl, :], ALU.mult, ALU.add),
                  ["Jsw", "s512", ("Rot", bp, gl)], [("Rot", bp, gl)])

        load_u(0)
        for gl in range(8):
            setup_group(0, gl)
        for b in range(8):
            bp = b % 2
            rows = slice(b * 128, (b + 1) * 128)
            if b + 1 < 8:
                load_u(b + 1)
            stZ, stZi, stZa, stA, stB0, stB = [], [], [], [], [], []
            for kb in range(8):
                for gl in range(8):
                    i2 = it % 2
                    it += 1

                    def Z_(kb=kb, gl=gl, i2=i2, bp=bp):
                        tcs = slice(kb * 512, (kb + 1) * 512)
                        def buf(e):
                            e.matmul(bub[i2][:, 0, :], Bpad[bp][:, gl, :], ub[bp][:, tcs], start=True, stop=True)
                            return e.matmul(bub[i2][:, 1, :], Bspad[bp][:, gl, :], ub[bp][:, tcs], start=True, stop=True)
                        s.pe(buf, [("Bpad", bp, gl), ("Bspad", bp, gl), f"ub{bp}"], [f"bub{i2}"])
                        keep_warm(2)

                    def Zi_(kb=kb, gl=gl, bp=bp):
                        if kb > 0:
                            s.pe(lambda e: e.matmul(in_ps[:, gl:gl + 1], Rot[bp][:, gl, :], wlast[:, gl:gl + 1], start=True, stop=True),
                                 [("Rot", bp, gl), ("wlast", gl)], ["in_ps"])

                    def Za_(kb=kb, gl=gl):
                        if kb > 0:
                            s.act(lambda e: e.copy(inits[:, gl:gl + 1], in_ps[:, gl:gl + 1]), ["in_ps"], [("inits", gl)])

                    def A_(kb=kb, gl=gl, i2=i2, bp=bp):
                        s.dve(lambda e: e.tensor_tensor(t12b[i2][:, 0, :], bub[i2][:, 0, :], CSb[bp][:, gl, 0, :], ALU.mult),
                              [f"bub{i2}", ("CSb", bp, gl, 0)], [(f"t12b{i2}", 0)])
                        s.pe(lambda e: e.matmul(bt_ps[i2][:], k.ident_bf[:], t12b[i2][:, 0, :], start=True, stop=False),
                             [(f"t12b{i2}", 0), "ident_bf"], [f"bt_ps{i2}"])
                        s.dve(lambda e: e.tensor_tensor(t12b[i2][:, 1, :], bub[i2][:, 1, :], CSb[bp][:, gl, 1, :], ALU.mult),
                              [f"bub{i2}", ("CSb", bp, gl, 1)], [(f"t12b{i2}", 1)])
                        s.pe(lambda e: e.matmul(bt_ps[i2][:], k.ident_bf[:], t12b[i2][:, 1, :], start=False, stop=True),
                             [(f"t12b{i2}", 1), "ident_bf"], [f"bt_ps{i2}"])
                        keep_warm(1)

                    def B0_(kb=kb, gl=gl, i2=i2, b=b, rows=rows):
                        g = b * 8 + gl
                        tcs = slice(kb * 512, (kb + 1) * 512)
                        yp, ypk = y_ps[0], "y_ps0"
                        w_, wk = wt[i2], f"wt{i2}"
                        if kb == 0:
                            init = 0.0
                            ik = []
                        else:
                            init = inits[:, gl:gl + 1]
                            ik = [("inits", gl)]
                        s.dve(lambda e: e.tensor_tensor_scan(w_[:], sm["mag"][:, g:g + 1].to_broadcast([128, 512]), bt_ps[i2][:], init,
                                                             ALU.mult, ALU.add), [f"bt_ps{i2}", "mag"] + ik, [wk])
                        if kb < 7:
                            s.act(lambda e: e.copy(wlast[:, gl:gl + 1], w_[:, 511:512]), [wk], [("wlast", gl)])
                        s.act(lambda e: e.copy(wb[i2][:], w_[:]), [wk], [f"wb{i2}"])

                    def B_(kb=kb, gl=gl, i2=i2, b=b, rows=rows, bp=bp):
                        tcs = slice(kb * 512, (kb + 1) * 512)
                        yp, ypk = y_ps[0], "y_ps0"
                        s.dve(lambda e: e.tensor_tensor(P12[i2][:], wb[i2][:].unsqueeze(1).to_broadcast([128, 2, 512]), CSb[bp][:, gl, :, :], ALU.mult),
                              [f"wb{i2}", ("CSb", bp, gl, 0), ("CSb", bp, gl, 1)], [f"P12_{i2}"])
                        s.pe(lambda e: e.matmul(yp[:], M1pad[bp][:, gl, :], P12[i2][:, 0, :], start=(gl == 0), stop=False),
                             [("M1pad", bp, gl), f"P12_{i2}"], [ypk])
                        s.pe(lambda e: e.matmul(yp[:], M2pad[bp][:, gl, :], P12[i2][:, 1, :], start=False, stop=(gl == 7)),
                             [("M2pad", bp, gl), f"P12_{i2}"], [ypk])
                        keep_warm(1)
                        if gl == 7:
                            yb = kb % 2
                            s.dve(lambda e: e.scalar_tensor_tensor(yt[yb][:], u32[bp][:, tcs], k.s5d[:, b:b + 1], yp[:], ALU.mult, ALU.add),
                                  [f"u32_{bp}", ypk], [f"yt{yb}"])
                            s.act(lambda e: e.activation(yg[yb][:], yt[yb][:], AF.Gelu_apprx_tanh), [f"yt{yb}"], [f"yg{yb}"])
                            s.dma(dr["YG"][rows, tcs], yg[yb][:], reads=[f"yg{yb}"], writes=[("YG", b, kb)])
                    stZ.append(Z_)
                    stZi.append(Zi_)
                    stZa.append(Za_)
                    stA.append(A_)
                    stB0.append(B0_)
                    stB.append(B_)
            n = len(stA)
            for t in range(-2, n):
                if 0 <= t < n:
                    stB0[t]()
                if 0 <= t + 1 < n:
                    stZa[t + 1]()
                    stA[t + 1]()
                if 0 <= t + 2 < n:
                    stZ[t + 2]()
                    stZi[t + 2]()
                if 0 <= t < n:
                    stB[t]()
                if b + 1 < 8 and t >= 0 and t % 8 == 4:
                    setup_group(b + 1, t // 8)
        s.emit_phase()


def build(debug_upto=None):
    nc = bass.Bass("TRN2", target_bir_lowering=False)
    k = K()
    k.nc = nc
    din = {}

    def inp(name, shape):
        din[name] = nc.dram_tensor(name, list(shape), F32, kind="ExternalInput").ap()

    inp("x", [S, D]); inp("c", [1, D])
    inp("norm_mix_g", [2, D]); inp("norm_ffn_g", [2, D])
    inp("ada_w", [2, D, 6 * D]); inp("ada_b", [2, 6 * D])
    inp("ffn_w_in", [2, D, 2 * DFF]); inp("ffn_w_out", [2, DFF, D])
    inp("final_norm_g", [1, D])
    inp("hy_w_in", [1, D, 3584]); inp("hy_w_out", [1, D, D])
    inp("hg_norm_g", [1, 512]); inp("hg_lb_logits", [3, 512])
    inp("s5_w_in", [1, D, D])
    inp("s5_lam_re", [1, 64, 64]); inp("s5_lam_im", [1, 64, 64]); inp("s5_log_dt", [1, 64])
    inp("s5_b_re", [1, 64, 64, 16]); inp("s5_b_im", [1, 64, 64, 16])
    inp("s5_c_re", [1, 64, 16, 64]); inp("s5_c_im", [1, 64, 16, 64])
    inp("s5_d", [1, D]); inp("s5_w_glu", [1, D, 2 * D])
    k.din = din
    kind = "ExternalOutput" if debug_upto is not None else "Internal"
    dr = {}

    def scr(name, shape, dt):
        dr[name] = nc.dram_tensor(name, list(shape), dt, kind=kind).ap()

    scr("XT", [D, S], F32); scr("QK", [D, S], BF16)
    scr("VA", [S, 512], BF16); scr("IB", [S, 512], BF16)
    scr("QB", [512, S], F32); scr("FB", [512, S], F32); scr("GB", [512, S], F32)
    scr("MT", [D, S], BF16); scr("U", [D, S], F32); scr("YG", [D, S], BF16)
    k.dr = dr
    k.out = nc.dram_tensor("out", [S, D], F32, kind="ExternalOutput").ap()
    with ExitStack() as es:
        es.enter_context(nc.allow_non_contiguous_dma(reason="small strided parameter loads"))
        T = lambda name, shape, dt=F32: es.enter_context(nc.sbuf_tensor(name, shape, dt))
        k.s = s = Sched(nc, es)
        k.ident = T("ident", [128, 128])
        k.ident_bf = T("ident_bf", [128, 128], BF16)
        k.ones_bf = T("ones_bf", [128, 128], BF16)
        k.eps_col = T("eps_col", [128, 1])
        k.warm_src = T("warm_src", [128, 512], BF16)
        k.nmg = T("nmg", [128, 2, 8]); k.nfg = T("nfg", [128, 2, 8]); k.fng = T("fng", [128, 8])
        k.s5d = T("s5d", [128, 8]); k.hgg = T("hgg", [128, 4])
        k.lb = T("lb", [128, 4]); k.oml = T("oml", [128, 4])
        k.mod = T("mod", [128, 2, 48]); k.gm = T("gm", [128, 2, 8]); k.gf = T("gf", [128, 2, 8])
        s.pool(lambda e: e.memset(k.ident[:], 1.0), [], ["ident"])
        s.pool(lambda e: e.affine_select(k.ident[:], k.ident[:], pattern=[[-1, 128]], compare_op=ALU.is_equal, fill=0.0,
                                         base=0, channel_multiplier=1), ["ident"], ["ident"])
        s.pool(lambda e: e.tensor_copy(k.ident_bf[:], k.ident[:]), ["ident"], ["ident_bf"])
        s.pool(lambda e: e.memset(k.ones_bf[:], 1.0), [], ["ones_bf"])
        s.pool(lambda e: e.memset(k.eps_col[:], EPS), [], ["eps_col"])
        s.pool(lambda e: e.memset(k.warm_src[:], 1.0), [], ["warm_src"])
        phases = [phase_pre, phase_l0a, phase_attn, phase_hgrn, lambda k: phase_b1(k, 0), lambda k: phase_ffn(k, 0, False), phase_l1a,
                  phase_s5, lambda k: phase_b1(k, 1), lambda k: phase_ffn(k, 1, True)]
        k.pre = {}
        scopes = {0: (1, [("w_in0", [128, 8, 3584])]), 4: (5, [("ffn_w1", [128, 8, 2 * DFF]), ("ffn_w2", [128, 22, D])]),
                  8: (9, [("ffn_w1", [128, 8, 2 * DFF]), ("ffn_w2", [128, 22, D])])}
        open_scope = None
        for i, ph in enumerate(phases):
            k.pid = i
            if i in scopes:
                open_scope = (scopes[i][0], ExitStack())
                for nm, shp in scopes[i][1]:
                    k.pre[nm] = open_scope[1].enter_context(nc.sbuf_tensor(f"pre{i}_" + nm, shp, BF16))
            ph(k)
            if open_scope is not None and open_scope[0] == i:
                open_scope[1].close()
                open_scope = None
            if debug_upto is not None and i >= debug_upto:
                break
        if open_scope is not None:
            open_scope[1].close()
    return nc


_NC_CACHE = {}


def kernel(**inputs):
    if "nc" not in _NC_CACHE:
        _NC_CACHE["nc"] = build()
    nc = _NC_CACHE["nc"]
    n = 8
    shared = {}
    for name, v in inputs.items():
        if name in ("x", "c"):
            continue
        a = np.ascontiguousarray(np.asarray(v, dtype=np.float32))
        if name == "final_norm_g":
            a = a.reshape(1, -1)
        shared[name] = a
    x = np.asarray(inputs["x"], dtype=np.float32)
    c = np.asarray(inputs["c"], dtype=np.float32)
    in_maps = []
    for b in range(n):
        m = dict(shared)
        m["x"] = np.ascontiguousarray(x[b])
        m["c"] = np.ascontiguousarray(c[b:b + 1])
        in_maps.append(m)
    res = run_bass_kernel_spmd(nc, in_maps, core_ids=list(range(n)))
    return np.stack([np.asarray(r["out"], dtype=np.float32) for r in res.results], axis=0)
```

```python
from contextlib import ExitStack

import numpy as np
import concourse.bass as bass
import concourse.mybir as mybir
from concourse.bass_utils import run_bass_kernel_spmd

F32 = mybir.dt.float32
BF16 = mybir.dt.bfloat16
I32 = mybir.dt.int32
AF = mybir.ActivationFunctionType
ALU = mybir.AluOpType

ENGS = ["pe", "act", "dve", "pool", "sp"]
RING = 8
S = 4096
D = 1024
TB = 512
NB = S // TB
DFF = 2816
EPS = 1e-6
TWO_PI = float(2 * np.pi)


class Op:
    __slots__ = ("eng", "fn", "deps", "cnt", "need_inc", "dma", "ring", "gen")

    def __init__(self, eng, fn, dma):
        self.eng = eng
        self.fn = fn
        self.dma = dma
        self.deps = []
        self.cnt = 0
        self.need_inc = False
        self.ring = 0
        self.gen = 0


class Sched:
    def __init__(self, nc, es):
        self.nc = nc
        self.ops = {e: [] for e in ENGS}
        self.last_w = {}
        self.readers = {}
        self.sems = {e: es.enter_context(nc.semaphore(f"s_{e}")) for e in ENGS}
        self.rings = {e: [es.enter_context(nc.semaphore(f"r_{e}{i}")) for i in range(RING)] for e in ("sp", "pool", "act")}
        self.cnt = {e: 0 for e in ENGS}
        self.ndma = {e: 0 for e in ENGS}
        self.waited = {e: {} for e in ENGS}

    def add(self, eng, fn, reads=(), writes=(), dma=False):
        op = Op(eng, fn, dma)
        deps = {}
        for b in reads:
            w = self.last_w.get(b)
            if w is not None:
                deps[id(w)] = w
        for b in writes:
            w = self.last_w.get(b)
            if w is not None:
                deps[id(w)] = w
            for r in self.readers.get(b, ()):
                deps[id(r)] = r
        for d in deps.values():
            if d.eng == "pe" and eng == "pe" and not d.dma and not dma:
                continue
            op.deps.append(d)
            d.need_inc = True
        for b in reads:
            self.readers.setdefault(b, []).append(op)
        for b in writes:
            self.last_w[b] = op
            self.readers[b] = []
        self.ops[eng].append(op)
        return op

    def pe(self, fn, reads=(), writes=()):
        return self.add("pe", fn, reads, writes)

    def act(self, fn, reads=(), writes=()):
        return self.add("act", fn, reads, writes)

    def dve(self, fn, reads=(), writes=()):
        return self.add("dve", fn, reads, writes)

    def pool(self, fn, reads=(), writes=()):
        return self.add("pool", fn, reads, writes)

    def dma(self, out, in_, reads=(), writes=(), eng="sp"):
        return self.add(eng, lambda e: e.dma_start(out=out, in_=in_), reads, writes, dma=True)

    def emit_phase(self):
        nc = self.nc
        ops = self.ops
        self.ops = {e: [] for e in ENGS}
        for e in ENGS:
            for op in ops[e]:
                if op.dma:
                    di = self.ndma[e]
                    op.ring = di % RING
                    op.gen = di // RING + 1
                    self.ndma[e] = di + 1
                elif op.need_inc:
                    self.cnt[e] += 1
                    op.cnt = self.cnt[e]
        sems, rings = self.sems, self.rings

        def run_engine(e, eng):
            waited = self.waited[e]

            def wait(key, sem, val):
                if waited.get(key, 0) >= val:
                    return
                eng.wait_ge(sem, val)
                waited[key] = val

            for op in ops[e]:
                for d in op.deps:
                    if d.dma:
                        wait((d.eng, d.ring), rings[d.eng][d.ring], 16 * d.gen)
                    else:
                        wait((d.eng,), sems[d.eng], d.cnt)
                if op.dma:
                    if op.gen > 1:
                        wait((e, op.ring), rings[e][op.ring], 16 * (op.gen - 1))
                    op.fn(eng).then_inc(rings[e][op.ring], 16)
                else:
                    ins = op.fn(eng)
                    if op.need_inc:
                        ins.then_inc(sems[e], 1)
            if e in rings:
                n = self.ndma[e]
                for r in range(RING):
                    c = (n - r + RING - 1) // RING if n > r else 0
                    if c > 0:
                        wait((e, r), rings[e][r], 16 * c)

        with nc.Block() as block:
            @block.tensor
            def _(eng):
                run_engine("pe", eng)

            @block.scalar
            def _(eng):
                run_engine("act", eng)

            @block.vector
            def _(eng):
                run_engine("dve", eng)

            @block.gpsimd
            def _(eng):
                run_engine("pool", eng)

            @block.sync
            def _(eng):
                run_engine("sp", eng)

        self.last_w = {}
        self.readers = {}


class K:
    pass


def mm_group(s, out, pairs, reads, writes):
    n = len(pairs)

    def fn(e):
        ins = None
        for i, (l, r) in enumerate(pairs):
            ins = e.matmul(out, l, r, start=(i == 0), stop=(i == n - 1))
        return ins

    return s.pe(fn, reads, writes)


def phase_pre(k):
    nc, s, din = k.nc, k.s, k.din
    with ExitStack() as es:
        T = lambda name, shape, dt=F32: es.enter_context(nc.sbuf_tensor(f"p{k.pid}_{name}", shape, dt))
        P = lambda name, shape, dt=F32: es.enter_context(nc.psum_tensor(f"p{k.pid}_{name}", shape, dt))
        load_weight_bf16(k, k.pre["w_in0"], din["hy_w_in"][0], 8, 3584, "w_in0")
        c_sb = T("c_sb", [128, 8])
        cs = T("cs", [128, 8])
        s.dma(c_sb[:], din["c"].rearrange("o (j p) -> p (o j)", p=128), writes=["c_sb"])
        s.act(lambda e: e.activation(cs[:], c_sb[:], AF.Silu), ["c_sb"], ["cs"])
        s.dma(k.nmg[:], din["norm_mix_g"].rearrange("l (j p) -> p l j", p=128), writes=["nmg"])
        s.dma(k.nfg[:], din["norm_ffn_g"].rearrange("l (j p) -> p l j", p=128), writes=["nfg"])
        s.dma(k.fng[:], din["final_norm_g"].rearrange("o (j p) -> p (o j)", p=128), writes=["fng"])
        s.dma(k.s5d[:], din["s5_d"].rearrange("o (j p) -> p (o j)", p=128), writes=["s5d"])
        s.dma(k.hgg[:], din["hg_norm_g"].rearrange("o (j p) -> p (o j)", p=128), writes=["hgg"])
        adb = T("adb", [128, 2, 48])
        s.dma(adb[:], din["ada_b"].rearrange("l (j p) -> p l j", p=128), writes=["adb"])
        lg = T("lg", [128, 3, 4])
        s.dma(lg[:], din["hg_lb_logits"].rearrange("r (h p) -> p r h", p=128), writes=["lg"])
        dl = T("dl", [128, 2, 4])
        s.dve(lambda e: e.tensor_tensor(dl[:, 0, :], lg[:, 1, :], lg[:, 0, :], ALU.subtract), ["lg"], ["dl0"])
        s.dve(lambda e: e.tensor_tensor(dl[:, 1, :], lg[:, 2, :], lg[:, 0, :], ALU.subtract), ["lg"], ["dl1"])
        s.act(lambda e: e.activation(dl[:], dl[:], AF.Exp), ["dl0", "dl1"], ["dle"])
        den = T("den", [128, 4])
        s.dve(lambda e: e.scalar_tensor_tensor(den[:], dl[:, 0, :], 1.0, dl[:, 1, :], ALU.add, ALU.add), ["dle"], ["den"])
        s.dve(lambda e: e.reciprocal(k.lb[:], den[:]), ["den"], ["lb"])
        s.dve(lambda e: e.tensor_scalar(k.oml[:], k.lb[:], -1.0, 1.0, ALU.mult, ALU.add), ["lb"], ["oml"])
        slabs = [T(f"slab{i}", [128, 8, 512]) for i in range(3)]
        mod_ps = [P(f"mod_ps{l}", [128, 48]) for l in range(2)]
        it = 0
        for l in range(2):
            for si in range(12):
                sl = slabs[it % 3]
                key = f"slab{it % 3}"
                it += 1
                src = din["ada_w"][l, :, si * 512:(si + 1) * 512].rearrange("(kk p) n -> p kk n", p=128)
                s.dma(sl[:], src, writes=[key])
                for jj in range(4):
                    j = si * 4 + jj
                    mm_group(s, mod_ps[l][:, j:j + 1],
                             [(sl[:, kk, jj * 128:(jj + 1) * 128], cs[:, kk:kk + 1]) for kk in range(8)],
                             [key, "cs"], [f"mod_ps{l}"])
            s.dve(lambda e, l=l: e.tensor_tensor(k.mod[:, l, :], mod_ps[l][:], adb[:, l, :], ALU.add),
                  [f"mod_ps{l}", "adb"], [f"mod{l}"])
            s.dve(lambda e, l=l: e.scalar_tensor_tensor(k.gm[:, l, :], k.mod[:, l, 8:16], 1.0, k.nmg[:, l, :], ALU.add, ALU.mult),
                  [f"mod{l}", "nmg"], [f"gm{l}"])
            s.dve(lambda e, l=l: e.scalar_tensor_tensor(k.gf[:, l, :], k.mod[:, l, 32:40], 1.0, k.nfg[:, l, :], ALU.add, ALU.mult),
                  [f"mod{l}", "nfg"], [f"gf{l}"])
        s.emit_phase()


def load_weight_bf16(k, dst, src_rows, nk, ncols, key, chunk=2048):
    s = k.s
    for kk in range(nk):
        for c0 in range(0, ncols, chunk):
            c1 = min(ncols, c0 + chunk)
            s.dma(dst[:, kk, c0:c1], src_rows[kk * 128:(kk + 1) * 128, c0:c1], writes=[(key, kk, c0)], eng="pool")


def wkeys(key, nk, ncols, chunk=2048):
    return [(key, kk, c0) for kk in range(nk) for c0 in range(0, ncols, chunk)]


def norm_block(k, X, xkey, hT, hkey, sq, sqkey, G, SH, tmps, ss_ps, sskey, rstd, tag):
    s = k.s
    n = X.shape[2]
    for c in range(8):
        s.act(lambda e, c=c: e.activation(sq[:, c, :], X[:, c, :], AF.Square), [xkey], [(sqkey, c)])
    mm_group(s, ss_ps[:, :n], [(k.ones_bf[:], sq[:, c, :]) for c in range(8)], [(sqkey, c) for c in range(8)], [sskey])
    s.act(lambda e: e.activation(rstd[:, :n], ss_ps[:, :n], AF.Sqrt, bias=k.eps_col[:, 0:1], scale=1.0 / D), [sskey], [tag + "rs0"])
    s.dve(lambda e: e.reciprocal(rstd[:, :n], rstd[:, :n]), [tag + "rs0"], [tag + "rstd"])
    if hT is None:
        return
    for c in range(8):
        tm = tmps[c % 2]
        tk = f"{tag}tmp{c % 2}"
        s.dve(lambda e, c=c, tm=tm: e.scalar_tensor_tensor(tm[:, :n], X[:, c, :], G[:, c:c + 1], rstd[:, :n], ALU.mult, ALU.mult),
              [xkey, tag + "rstd"], [tk])
        s.act(lambda e, c=c, tm=tm: e.activation(hT[:, c, :], tm[:, :n], AF.Identity, bias=SH[:, c:c + 1], scale=1.0),
              [tk], [(hkey, c)])


def phase_l0a(k):
    nc, s, din, dr = k.nc, k.s, k.din, k.dr
    TQ = 256
    NQ = S // TQ
    with ExitStack() as es:
        T = lambda name, shape, dt=F32: es.enter_context(nc.sbuf_tensor(f"p{k.pid}_{name}", shape, dt))
        P = lambda name, shape, dt=F32: es.enter_context(nc.psum_tensor(f"p{k.pid}_{name}", shape, dt))
        w = k.pre["w_in0"]
        xtok = T("xtok", [128, 2, 1024])
        X = [T(f"X{i}", [128, 8, TQ]) for i in range(2)]
        hT = [T(f"hT{i}", [128, 8, TQ], BF16) for i in range(2)]
        sq = T("sq", [128, 8, TQ], BF16)
        tmps = [T(f"tmp{i}", [128, TQ]) for i in range(2)]
        rstd = [T(f"rstd{i}", [128, TQ]) for i in range(2)]
        tr_ps = [P(f"tr_ps{i}", [128, 512]) for i in range(2)]
        ssb = P("ssb", [128, 512])
        mm_ps = [P(f"mm_ps{i}", [128, 512]) for i in range(4)]
        st_qk = [T(f"st_qk{i}", [128, 8, TQ], BF16) for i in range(2)]
        st_f = {nm: [T(f"st_{nm}{i}", [128, 4, TQ]) for i in range(2)] for nm in ("qb", "fb", "gb")}
        st_v = {nm: [T(f"st_{nm}{i}", [128, 2, 512], BF16) for i in range(2)] for nm in ("va", "ib")}
        G = k.gm[:, 0, :]
        SH = k.mod[:, 0, 0:8]
        cnt = {"ev": 0, "mi": 0}

        def loadx(n):
            s.dma(xtok[:], din["x"][n * TQ:(n + 1) * TQ, :].rearrange("(a p) d -> p a d", p=128), writes=["xtok"])

        def pre(n):
            pb = n % 2
            for c in range(8):
                tp, tk = tr_ps[c % 2], f"tr_ps{c % 2}"

                def trf(e, c=c, tp=tp):
                    ins = None
                    for a_ in range(2):
                        ins = e.transpose(tp[:, a_ * 128:(a_ + 1) * 128], xtok[:, a_, c * 128:(c + 1) * 128], k.ident[:])
                    return ins
                s.pe(trf, ["xtok", "ident"], [tk])
                if c % 2 == 0:
                    s.dve(lambda e, c=c, tp=tp: e.tensor_copy(X[pb][:, c, :], tp[:, 0:TQ]), [tk], [(f"X{pb}", c)])
                else:
                    s.act(lambda e, c=c, tp=tp: e.copy(X[pb][:, c, :], tp[:, 0:TQ]), [tk], [(f"X{pb}", c)])
            norm256(k, n, X, hT, sq, tmps, rstd, ssb, G, SH)
            s.dma(dr["XT"].rearrange("(c p) t -> p c t", p=128)[:, :, n * TQ:(n + 1) * TQ], X[pb][:],
                  reads=[(f"X{pb}", c) for c in range(8)], writes=[("XT", n)])

        def evac(dst, ps, pk, dk):
            if cnt["ev"] % 2 == 0:
                s.act(lambda e: e.copy(dst, ps), [pk], [dk])
            else:
                s.dve(lambda e: e.tensor_copy(dst, ps), [pk], [dk])
            cnt["ev"] += 1

        def groups(n):
            pb = n % 2
            hk = [(f"hT{pb}", c) for c in range(8)]
            h = hT[pb]
            tcs = slice(n * TQ, (n + 1) * TQ)
            out = []
            fm = [("qk", 0, 0, 4), ("qk", 512, 4, 4), ("qb", 1536, 0, 4), ("fb", 2048, 0, 4), ("gb", 3072, 0, 4)]
            for nm, col0, slot0, nchunk in fm:
                st = st_qk[pb] if nm == "qk" else st_f[nm][pb]
                stkey = f"st_{nm}{pb}"
                for j in range(nchunk):
                    def g_(nm=nm, col0=col0, slot0=slot0, j=j, st=st, stkey=stkey):
                        ps, pk = mm_ps[cnt["mi"] % 4], f"mm_ps{cnt['mi'] % 4}"
                        cnt["mi"] += 1
                        cc = col0 + j * 128
                        mm_group(s, ps[:, 0:TQ], [(w[:, kk, cc:cc + 128], h[:, kk, :]) for kk in range(8)], hk, [pk])
                        evac(st[:, slot0 + j, :], ps[:, 0:TQ], pk, (stkey, slot0 + j))
                        if nm == "qk" and slot0 + j == 7:
                            s.dma(dr["QK"].rearrange("(c p) t -> p c t", p=128)[:, :, tcs], st[:], reads=[(stkey, x) for x in range(8)],
                                  writes=[("QK", n)])
                        if nm != "qk" and j == 3:
                            dn = {"qb": "QB", "fb": "FB", "gb": "GB"}[nm]
                            s.dma(dr[dn].rearrange("(c p) t -> p c t", p=128)[:, :, tcs], st[:], reads=[(stkey, x) for x in range(4)],
                                  writes=[(dn, n)])
                    out.append(g_)
            for nm, col0, dn in (("va", 1024, "VA"), ("ib", 2560, "IB")):
                st = st_v[nm][pb]
                stkey = f"st_{nm}{pb}"
                for a_ in range(2):
                    def g_(nm=nm, col0=col0, dn=dn, a_=a_, st=st, stkey=stkey):
                        ps, pk = mm_ps[cnt["mi"] % 4], f"mm_ps{cnt['mi'] % 4}"
                        cnt["mi"] += 1
                        mm_group(s, ps[:], [(h[:, kk, a_ * 128:(a_ + 1) * 128], w[:, kk, col0:col0 + 512]) for kk in range(8)], hk, [pk])
                        evac(st[:, a_, :], ps[:], pk, (stkey, a_))
                        if a_ == 1:
                            s.dma(dr[dn][tcs, :].rearrange("(a p) f -> p a f", p=128), st[:], reads=[(stkey, 0), (stkey, 1)], writes=[(dn, n)])
                    out.append(g_)
            return out

        loadx(0)
        pre(0)
        for n in range(NQ):
            if n + 1 < NQ:
                loadx(n + 1)
            gs = groups(n)
            for g_ in gs[:12]:
                g_()
            if n + 1 < NQ:
                pre(n + 1)
            for g_ in gs[12:]:
                g_()
        s.emit_phase()


def norm_block_multi(k, X, xkeys, hT, hkey, sq, sqkey, G, SH, tmps, ss_ps, sskey, rstd, tag):
    s = k.s
    n = X.shape[2]
    for c in range(8):
        s.act(lambda e, c=c: e.activation(sq[:, c, :], X[:, c, :], AF.Square), [xkeys[c]], [(sqkey, c)])
    mm_group(s, ss_ps[:, :n], [(k.ones_bf[:], sq[:, c, :]) for c in range(8)], [(sqkey, c) for c in range(8)], [sskey])
    s.act(lambda e: e.activation(rstd[:, :n], ss_ps[:, :n], AF.Sqrt, bias=k.eps_col[:, 0:1], scale=1.0 / D), [sskey], [tag + "rs0"])
    s.dve(lambda e: e.reciprocal(rstd[:, :n], rstd[:, :n]), [tag + "rs0"], [tag + "rstd"])
    if hT is None:
        return
    for c in range(8):
        tm = tmps[c % 2]
        tk = f"{tag}tmp{c % 2}"
        s.dve(lambda e, c=c, tm=tm: e.scalar_tensor_tensor(tm[:, :n], X[:, c, :], G[:, c:c + 1], rstd[:, :n], ALU.mult, ALU.mult),
              [xkeys[c], tag + "rstd"], [tk])
        s.act(lambda e, c=c, tm=tm: e.activation(hT[:, c, :], tm[:, :n], AF.Identity, bias=SH[:, c:c + 1], scale=1.0),
              [tk], [(hkey, c)])


def phase_attn(k):
    nc, s, dr = k.nc, k.s, k.dr
    with ExitStack() as es:
        T = lambda name, shape, dt=F32: es.enter_context(nc.sbuf_tensor(f"p{k.pid}_{name}", shape, dt))
        P = lambda name, shape, dt=F32: es.enter_context(nc.psum_tensor(f"p{k.pid}_{name}", shape, dt))
        Um = T("Um", [128, 128], BF16)
        mask = T("mask", [128, 128])
        s.pool(lambda e: e.memset(mask[:], 1.0), [], ["mask"])
        s.pool(lambda e: e.affine_select(mask[:], mask[:], pattern=[[-1, 128]], compare_op=ALU.is_gt, fill=0.0, base=0,
                                         channel_multiplier=1), ["mask"], ["mask"])
        s.pool(lambda e: e.tensor_copy(Um[:], mask[:]), ["mask"], ["Um"])
        maskb = T("maskb", [128, 128], BF16)
        s.pool(lambda e: e.memset(mask[:], 1.0), ["Um"], ["mask"])
        s.pool(lambda e: e.affine_select(mask[:], mask[:], pattern=[[1, 128]], compare_op=ALU.is_gt, fill=0.0, base=0,
                                         channel_multiplier=-1), ["mask"], ["mask"])
        s.pool(lambda e: e.tensor_copy(maskb[:], mask[:]), ["mask"], ["maskb"])
        qT = [T(f"qT{i}", [128, S], BF16) for i in range(2)]
        kT = [T(f"kT{i}", [128, S], BF16) for i in range(2)]
        V2 = [T(f"V2{i}", [128, 32, 128], BF16) for i in range(2)]
        Et = [T(f"Et{i}", [128, 512]) for i in range(2)]
        Lt = [T(f"Lt{i}", [128, 512], BF16) for i in range(3)]
        LKt = [T(f"LKt{i}", [128, 512], BF16) for i in range(2)]
        At = [T(f"At{i}", [128, 512]) for i in range(2)]
        Wt = [T(f"Wt{i}", [128, 512], BF16) for i in range(2)]
        Ss = [T(f"Ss{i}", [128, 512], BF16) for i in range(3)]
        acc = [T(f"acc{i}", [128, 4, 128]) for i in range(2)]
        mst = [T(f"mst{i}", [128, 512], BF16) for i in range(2)]
        z_ps = [P(f"z_ps{i}", [128, 512]) for i in range(2)]
        tri_ps = [P(f"tri_ps{i}", [128, 512]) for i in range(3)]
        pv_ps = [P(f"pv_ps{i}", [128, 4, 64]) for i in range(2)]
        tr_ps = P("atr_ps", [128, 512])
        stZ, stA, stB0, stB = [], [], [], []
        it = 0
        rnd = 0

        def keep_warm(nrep):
            def fn(e):
                ins = None
                for _ in range(nrep):
                    ins = e.matmul(tr_ps[:], k.ident_bf[:], k.warm_src[:], start=True, stop=True)
                return ins
            s.pe(fn, [], ["atr_ps"])
        for hp in range(4):
            hb = hp % 2
            for G in range(8):
                ab = G % 2
                for hl in range(2):
                    rb = rnd % 2
                    rnd += 1
                    for j in range(4 * G + 3, -1, -1):
                        first = (j == 4 * G + 3)
                        last = (j == 0)
                        ib = it % 2
                        i3 = it % 3
                        it += 1

                        def Z_(hp=hp, hb=hb, G=G, hl=hl, j=j, ib=ib, first=first):
                            q, kk_, v = qT[hb], kT[hb], V2[hb]
                            qk_, kk_key, vk = f"qT{hb}", f"kT{hb}", f"V2{hb}"
                            if first and G == 0 and hl == 0:
                                s.dma(q[:], dr["QK"][hp * 128:(hp + 1) * 128, :], writes=[qk_])
                                s.dma(kk_[:], dr["QK"][512 + hp * 128:512 + (hp + 1) * 128, :], writes=[kk_key])
                                s.dma(v[:], dr["VA"][:, hp * 128:(hp + 1) * 128].rearrange("(n p) f -> p n f", p=128), writes=[vk])
                            hs = slice(64 * hl, 64 * hl + 64)
                            c0 = max(j - 4 * G, 0) * 128
                            zp = z_ps[ib]
                            s.pe(lambda e: e.matmul(zp[:, c0:512], kk_[hs, j * 128:(j + 1) * 128], q[hs, G * 512 + c0:(G + 1) * 512],
                                                    start=True, stop=True), [qk_, kk_key], [f"z_ps{ib}"])
                            keep_warm(2)

                        def A_(hp=hp, hb=hb, G=G, hl=hl, j=j, ib=ib, i3=i3, rb=rb, first=first):
                            q, kk_, v = qT[hb], kT[hb], V2[hb]
                            qk_, kk_key, vk = f"qT{hb}", f"kT{hb}", f"V2{hb}"
                            so, sn = (j + 1) % 3, j % 3
                            Sk, Snk = f"Ss{so}", f"Ss{sn}"
                            Sm, Sn = Ss[so], Ss[sn]
                            hs = slice(64 * hl, 64 * hl + 64)
                            jl = j - 4 * G
                            c0 = max(jl, 0) * 128
                            cols = slice(c0, 512)
                            zp, tp = z_ps[ib], tri_ps[i3]
                            E, L, LK = Et[ib], Lt[i3], LKt[ib]
                            zk, tk = f"z_ps{ib}", f"tri_ps{i3}"
                            Ek, Lk, LKk = f"Et{ib}", f"Lt{i3}", f"LKt{ib}"
                            s.act(lambda e: e.activation(E[:, cols], zp[:, cols], AF.Exp, scale=-0.125), [zk], [Ek])
                            s.act(lambda e: e.activation(L[:, cols], E[:, cols], AF.Ln, bias=1.0), [Ek], [Lk])
                            s.dve(lambda e: e.scalar_tensor_tensor(LK[:, cols], zp[:, cols], 0.125, L[:, cols], ALU.mult, ALU.add),
                                  [zk, Lk], [LKk])
                            if jl >= 0:
                                s.dve(lambda e: e.tensor_tensor(LK[:, c0:c0 + 128], LK[:, c0:c0 + 128], maskb[:], ALU.mult),
                                      [LKk, "maskb"], [LKk])
                            cc0 = c0 + 128 if jl >= 0 else 0

                            def trif(e):
                                e.matmul(tp[:, cols], Um[:], LK[:, cols], start=True, stop=False)
                                ins = e.matmul(tp[:, cols], k.ident_bf[:], L[:, cols], start=False, stop=(first or cc0 >= 512))
                                if (not first) and cc0 < 512:
                                    ins = e.matmul(tp[:, cc0:512], k.ones_bf[:], Sm[:, cc0:512], start=False, stop=True)
                                return ins
                            if first:
                                s.pe(trif, [LKk, Lk, "Um", "ident_bf"], [tk])
                            else:
                                s.pe(trif, [LKk, Lk, "Um", "ident_bf", Sk, (Sk, 0)], [tk])
                            if j > 0:
                                if jl >= 0:
                                    s.dve(lambda e: e.tensor_copy(Sn[:, c0:c0 + 128], LK[:, c0:c0 + 128]), [LKk], [(Snk, 0)])
                                    if c0 + 128 < 512:
                                        s.dve(lambda e: e.tensor_tensor(Sn[:, c0 + 128:512], Sm[:, c0 + 128:512], LK[:, c0 + 128:512], ALU.add),
                                              [Sk, (Sk, 0), LKk], [Snk])
                                else:
                                    s.dve(lambda e: e.tensor_tensor(Sn[:, cols], Sm[:, cols], LK[:, cols], ALU.add), [Sk, (Sk, 0), LKk], [Snk])

                        def B0_(G=G, j=j, ib=ib, i3=i3):
                            c0 = max(j - 4 * G, 0) * 128
                            cols = slice(c0, 512)
                            pass

                        def B_(hp=hp, hb=hb, G=G, hl=hl, j=j, ib=ib, i3=i3, rb=rb, first=first, last=last, ab=ab):
                            v, vk = V2[hb], f"V2{hb}"
                            hs = slice(64 * hl, 64 * hl + 64)
                            jl = j - 4 * G
                            c0 = max(jl, 0) * 128
                            cols = slice(c0, 512)
                            b0 = max(jl, 0)
                            pp = pv_ps[rb]
                            A, W = At[ib], Wt[ib]
                            pk = f"pv_ps{rb}"
                            Ak, Wk = f"At{ib}", f"Wt{ib}"
                            ac = acc[ab]
                            tp = tri_ps[i3]
                            s.act(lambda e: e.activation(W[:, cols], tp[:, cols], AF.Exp, scale=-1.0), [f"tri_ps{i3}"], [Wk])
                            if jl >= 0:
                                s.dve(lambda e: e.tensor_tensor(W[:, c0:c0 + 128], W[:, c0:c0 + 128], maskb[:], ALU.mult),
                                      [Wk, "maskb"], [Wk])
                            def pvf(e):
                                ins = None
                                for b in range(3, b0 - 1, -1):
                                    ins = e.matmul(pp[:, b, :], W[:, b * 128:(b + 1) * 128], v[:, j, hs],
                                                   start=(first and b == 3), stop=last, skip_group_check=True)
                                return ins
                            s.pe(pvf, [Wk, vk], [pk])
                            if last:
                                s.dve(lambda e: e.tensor_copy(ac[:, :, hs], pp[:]), [pk], [(f"acc{ab}", hl)])
                                if hl == 1:
                                    def trf(e):
                                        ins = None
                                        for b in range(4):
                                            ins = e.transpose(tr_ps[:, b * 128:(b + 1) * 128], ac[:, b, :], k.ident[:])
                                        return ins
                                    s.pe(trf, [(f"acc{ab}", 0), (f"acc{ab}", 1), "ident"], ["atr_ps"])
                                    ms = mst[G % 2]
                                    s.dve(lambda e: e.tensor_copy(ms[:], tr_ps[:]), ["atr_ps"], [f"mst{G % 2}"])
                                    s.dma(dr["MT"][hp * 128:(hp + 1) * 128, G * 512:(G + 1) * 512], ms[:], reads=[f"mst{G % 2}"],
                                          writes=[("MT", hp, G)])
                        stZ.append(Z_)
                        stA.append(A_)
                        stB0.append(B0_)
                        stB.append(B_)
        n = len(stA)
        for t in range(-3, n):
            if 0 <= t + 3 < n:
                stZ[t + 3]()
            if 0 <= t < n:
                stB0[t]()
            if 0 <= t + 2 < n:
                stA[t + 2]()
            if 0 <= t < n:
                stB[t]()
        s.emit_phase()


def phase_hgrn(k):
    nc, s, dr = k.nc, k.s, k.dr
    with ExitStack() as es:
        T = lambda name, shape, dt=F32: es.enter_context(nc.sbuf_tensor(f"p{k.pid}_{name}", shape, dt))
        P = lambda name, shape, dt=F32: es.enter_context(nc.psum_tensor(f"p{k.pid}_{name}", shape, dt))
        rmask = T("rmask", [128, S])
        s.pool(lambda e: e.memset(rmask[:], 1.0), [], ["rmask"])
        s.pool(lambda e: e.memset(rmask[:].rearrange("p (c t) -> p c t", t=64)[:, :, 0:1], 0.0), ["rmask"], ["rmask"])
        mle = T("mle", [128, 64])
        s.pool(lambda e: e.memset(mle[:], 1.0), [], ["mle"])
        for half in range(2):
            s.pool(lambda e, half=half: e.affine_select(mle[64 * half:64 * half + 64, :], mle[64 * half:64 * half + 64, :],
                                                        pattern=[[1, 64]], compare_op=ALU.is_ge, fill=0.0, base=0,
                                                        channel_multiplier=-1), ["mle"], ["mle"])
        a1 = T("a1", [128, S]); a2 = T("a2", [128, S]); a3 = T("a3", [128, S]); a4 = T("a4", [128, S])
        kh = T("kh", [128, S], BF16)
        qt = [T(f"qt{i}", [128, S], BF16) for i in range(2)]
        kt = [T(f"kt{i}", [128, S], BF16) for i in range(2)]
        khT = [T(f"khT{i}", [128, 32, 128], BF16) for i in range(2)]
        ibt = [T(f"ibt{i}", [128, 32, 128], BF16) for i in range(2)]
        elast = [T(f"elast{i}", [128, 64]) for i in range(2)]
        St = [T(f"St{i}", [128, 128]) for i in range(2)]
        Sb = [[T(f"Sb{i}_{j}", [128, 128], BF16) for j in range(2)] for i in range(2)]
        sT = [T(f"sT{i}", [128, 64], BF16) for i in range(2)]
        Ob = [[T(f"Ob{i}_{pb}", [128, 512]) for pb in range(2)] for i in range(2)]
        gbk = [T(f"gbk{i}", [128, 512]) for i in range(2)]
        sqo = T("sqo", [128, 512], BF16)
        rs = T("hrs", [128, 512])
        t1 = T("ht1", [128, 512])
        obf = [T(f"obf{i}", [128, 512], BF16) for i in range(2)]
        sc_ps = [P(f"sc_ps{i}", [128, 64]) for i in range(2)]
        o_ps = [P(f"o_ps{i}", [128, 64]) for i in range(2)]
        kv_ps = [P(f"kv_ps{i}", [128, 128]) for i in range(2)]
        ms_ps = P("ms_ps", [128, 512])
        ktr_ps = P("ktr_ps", [128, 512], BF16)
        a4v = a4[:].rearrange("p (c t) -> p c t", t=64)
        fin = 0
        for hg in range(2):
            for i in range(2):
                hd = 2 * hg + i
                rows = slice(hd * 128, (hd + 1) * 128)
                s.dma(a1[:], dr["FB"][rows, :], writes=["a1"])
                s.dma(a2[:], dr["QB"][rows, :], writes=["a2"])
                s.dma(ibt[i][:], dr["IB"][:, rows].rearrange("(n p) f -> p n f", p=128), writes=[f"ibt{i}"])
                s.act(lambda e: e.activation(a1[:], a1[:], AF.Sigmoid), ["a1"], ["a1"])
                s.dve(lambda e, hd=hd: e.tensor_scalar(a1[:], a1[:], k.oml[:, hd:hd + 1], k.lb[:, hd:hd + 1], ALU.mult, ALU.add), ["a1"], ["a1"])
                s.dve(lambda e: e.tensor_scalar(a3[:], a1[:], -1.0, 1.0, ALU.mult, ALU.add), ["a1"], ["a3"])
                s.act(lambda e: e.activation(a1[:], a1[:], AF.Ln), ["a1"], ["a1"])
                s.dve(lambda e: e.tensor_tensor_scan(a4[:], rmask[:], a1[:], 0.0, ALU.mult, ALU.add), ["a1", "rmask"], ["a4"])
                s.act(lambda e: e.activation(a2[:], a2[:], AF.Silu), ["a2"], ["a2"])
                s.act(lambda e: e.activation(a1[:], a4[:], AF.Exp), ["a4"], ["a1"])
                s.dve(lambda e, i=i: e.tensor_tensor(qt[i][:], a2[:], a1[:], ALU.mult), ["a1", "a2"], [f"qt{i}"])
                s.act(lambda e: e.activation(a1[:], a4[:], AF.Exp, scale=-1.0), ["a4", f"qt{i}"], ["a1"])
                s.dve(lambda e, i=i: e.tensor_tensor(kt[i][:], a3[:], a1[:], ALU.mult), ["a1", "a3"], [f"kt{i}"])
                s.act(lambda e, i=i: e.activation(elast[i][:], a4v[:, :, 63], AF.Exp), ["a4"], [f"elast{i}"])
                s.dve(lambda e: e.tensor_tensor(a1[:].rearrange("p (c t) -> p c t", t=64), a4v[:, :, 63:64].to_broadcast([128, 64, 64]),
                                                a4v, ALU.subtract), ["a4", f"kt{i}"], ["a1"])
                s.act(lambda e: e.activation(a1[:], a1[:], AF.Exp), ["a1"], ["a1"])
                s.dve(lambda e: e.tensor_tensor(kh[:], a3[:], a1[:], ALU.mult), ["a1", "a3"], ["kh"])
                for n4 in range(8):
                    def trf(e, n4=n4):
                        ins = None
                        for a in range(4):
                            n = n4 * 4 + a
                            ins = e.transpose(ktr_ps[:, a * 128:(a + 1) * 128], kh[:, n * 128:(n + 1) * 128], k.ident_bf[:])
                        return ins
                    s.pe(trf, ["kh", "ident_bf"], ["ktr_ps"])
                    s.act(lambda e, i=i, n4=n4: e.copy(khT[i][:, n4 * 4:(n4 + 1) * 4, :], ktr_ps[:].rearrange("p (a f) -> p a f", f=128)),
                          ["ktr_ps"], [(f"khT{i}", n4)])
            pend = [None]
            for c in range(64):
                n, half = c // 2, c % 2
                pbs = slice(64 * half, 64 * half + 64)
                cs_ = slice(c * 64, (c + 1) * 64)
                kb, cl = c // 8, c % 8
                for i in range(2):
                    hd = 2 * hg + i
                    O = Ob[i][kb % 2]
                    Ok = f"Ob{i}_{kb % 2}"
                    sbn, sbo = Sb[i][c % 2], Sb[i][(c + 1) % 2]
                    sbnk, sbok = f"Sb{i}_{c % 2}", f"Sb{i}_{(c + 1) % 2}"
                    s.pe(lambda e, i=i, pbs=pbs, cs_=cs_: e.matmul(sc_ps[i][pbs, :], kt[i][:, cs_], qt[i][:, cs_], start=True, stop=True),
                         [f"kt{i}", f"qt{i}"], [f"sc_ps{i}"])
                    if c < 63:
                        s.pe(lambda e, i=i, pbs=pbs, n=n: e.matmul(kv_ps[i][:], khT[i][pbs, n, :], ibt[i][pbs, n, :], start=True, stop=True),
                             [(f"khT{i}", n // 4), f"ibt{i}"], [f"kv_ps{i}"])
                    s.dve(lambda e, i=i, pbs=pbs: e.tensor_tensor(sT[i][pbs, :], sc_ps[i][pbs, :], mle[pbs, :], ALU.mult),
                          [f"sc_ps{i}", "mle"], [f"sT{i}"])
                    if c < 63:
                        if c == 0:
                            s.dve(lambda e, i=i: e.tensor_copy(St[i][:], kv_ps[i][:]), [f"kv_ps{i}"], [f"St{i}"])
                        else:
                            s.dve(lambda e, i=i, c=c: e.scalar_tensor_tensor(St[i][:], St[i][:], elast[i][:, c:c + 1], kv_ps[i][:],
                                                                              ALU.mult, ALU.add), [f"kv_ps{i}", f"St{i}", f"elast{i}"], [f"St{i}"])
                        s.act(lambda e, i=i, sbn=sbn: e.copy(sbn[:], St[i][:]), [f"St{i}"], [sbnk])
                    def late_(i=i, pbs=pbs, n=n, c=c, cs_=cs_, sbo=sbo, sbok=sbok, O=O, Ok=Ok, cl=cl):
                        pairs = [(ibt[i][pbs, n, :], sT[i][pbs, :])]
                        rd = [f"ibt{i}", f"sT{i}", f"qt{i}"]
                        if c > 0:
                            pairs.append((sbo[:], qt[i][:, cs_]))
                            rd.append(sbok)
                        mm_group(s, o_ps[i][:], pairs, rd, [f"o_ps{i}"])
                        s.act(lambda e: e.copy(O[:, cl * 64:(cl + 1) * 64], o_ps[i][:]), [f"o_ps{i}"], [(Ok, cl)])
                    if pend[0] is not None:
                        pend[0]()
                    pend[0] = late_
                    if cl == 7:
                        pend[0]()
                        pend[0] = None
                    if cl == 7:
                        rows = slice(hd * 128, (hd + 1) * 128)
                        tcols = slice(kb * 512, (kb + 1) * 512)
                        fb_ = fin % 2
                        fin += 1
                        Oks = [(Ok, x) for x in range(8)]
                        s.dma(gbk[fb_][:], dr["GB"][rows, tcols], writes=[f"gbk{fb_}"])
                        s.act(lambda e, O=O: e.activation(sqo[:], O[:], AF.Square), Oks, ["sqo"])
                        s.pe(lambda e: e.matmul(ms_ps[:], k.ones_bf[:], sqo[:], start=True, stop=True), ["sqo"], ["ms_ps"])
                        s.act(lambda e: e.activation(rs[:], ms_ps[:], AF.Sqrt, bias=k.eps_col[:, 0:1], scale=1.0 / 128), ["ms_ps"], ["hrs"])
                        s.dve(lambda e: e.reciprocal(rs[:], rs[:]), ["hrs"], ["hrs"])
                        s.act(lambda e, fb_=fb_: e.activation(gbk[fb_][:], gbk[fb_][:], AF.Silu), [f"gbk{fb_}"], [f"gbk{fb_}"])
                        s.dve(lambda e, O=O, hd=hd: e.scalar_tensor_tensor(t1[:], O[:], k.hgg[:, hd:hd + 1], rs[:], ALU.mult, ALU.mult),
                              Oks + ["hrs"], ["ht1"])
                        s.dve(lambda e, fb_=fb_: e.tensor_tensor(obf[fb_][:], t1[:], gbk[fb_][:], ALU.mult), ["ht1", f"gbk{fb_}"], [f"obf{fb_}"])
                        s.dma(dr["MT"][512 + hd * 128:512 + (hd + 1) * 128, tcols], obf[fb_][:], reads=[f"obf{fb_}"], writes=[("MTB", hd, kb)])
        s.emit_phase()


def phase_b1(k, layer):
    nc, s, din, dr = k.nc, k.s, k.din, k.dr
    TQ = 256
    NQ = S // TQ
    with ExitStack() as es:
        T = lambda name, shape, dt=F32: es.enter_context(nc.sbuf_tensor(f"p{k.pid}_{name}", shape, dt))
        P = lambda name, shape, dt=F32: es.enter_context(nc.psum_tensor(f"p{k.pid}_{name}", shape, dt))
        ncol = 1024 if layer == 0 else 2048
        w = T("w_b1", [128, 8, ncol], BF16)
        load_weight_bf16(k, w, din["hy_w_out"][0] if layer == 0 else din["s5_w_glu"][0], 8, ncol, "w")
        WK = wkeys("w", 8, ncol)
        load_weight_bf16(k, k.pre["ffn_w1"], din["ffn_w_in"][layer], 8, 2 * DFF, "pw1")
        load_weight_bf16(k, k.pre["ffn_w2"], din["ffn_w_out"][layer], 22, D, "pw2")
        src = (dr["MT"] if layer == 0 else dr["YG"]).rearrange("(c p) t -> p c t", p=128)
        XTv = dr["XT"].rearrange("(c p) t -> p c t", p=128)
        g = k.mod[:, layer, 16:24]
        mT = [T(f"mT{i}", [128, 8, TQ], BF16) for i in range(2)]
        X = [T(f"X{i}", [128, 8, TQ]) for i in range(2)]
        sg = [T(f"sg{i}", [128, TQ]) for i in range(2)]
        mix = [T(f"mix{i}", [128, TQ]) for i in range(2)]
        v_ps = [P(f"v_ps{i}", [128, 512]) for i in range(2)]
        g_ps = [P(f"g_ps{i}", [128, 512]) for i in range(2)]

        def load(n):
            pb = n % 2
            tc_ = slice(n * TQ, (n + 1) * TQ)
            s.dma(mT[pb][:], src[:, :, tc_], writes=[f"mT{pb}"])
            s.dma(X[pb][:], XTv[:, :, tc_], writes=[(f"X{pb}", c) for c in range(8)])
        it = 0
        load(0)
        for n in range(NQ):
            pb = n % 2
            tc_ = slice(n * TQ, (n + 1) * TQ)
            if n + 1 < NQ:
                load(n + 1)
            for dc in range(8):
                ib = it % 2
                it += 1
                vp, gp = v_ps[ib][:, 0:TQ], g_ps[ib][:, 0:TQ]
                mm_group(s, vp, [(w[:, kk, dc * 128:(dc + 1) * 128], mT[pb][:, kk, :]) for kk in range(8)],
                         WK + [f"mT{pb}"], [f"v_ps{ib}"])
                if layer == 0:
                    s.dve(lambda e, pb=pb, dc=dc, vp=vp: e.scalar_tensor_tensor(X[pb][:, dc, :], vp, g[:, dc:dc + 1], X[pb][:, dc, :],
                                                                                ALU.mult, ALU.add), [f"v_ps{ib}", (f"X{pb}", dc)], [(f"X{pb}", dc)])
                else:
                    mm_group(s, gp, [(w[:, kk, 1024 + dc * 128:1024 + (dc + 1) * 128], mT[pb][:, kk, :]) for kk in range(8)],
                             WK + [f"mT{pb}"], [f"g_ps{ib}"])
                    s.act(lambda e, ib=ib, gp=gp: e.activation(sg[ib][:], gp, AF.Sigmoid), [f"g_ps{ib}"], [f"sg{ib}"])
                    s.dve(lambda e, ib=ib, vp=vp: e.tensor_tensor(mix[ib][:], vp, sg[ib][:], ALU.mult), [f"v_ps{ib}", f"sg{ib}"], [f"mix{ib}"])
                    s.dve(lambda e, pb=pb, dc=dc, ib=ib: e.scalar_tensor_tensor(X[pb][:, dc, :], mix[ib][:], g[:, dc:dc + 1], X[pb][:, dc, :],
                                                                                ALU.mult, ALU.add), [f"mix{ib}", (f"X{pb}", dc)], [(f"X{pb}", dc)])
            s.dma(XTv[:, :, tc_], X[pb][:], reads=[(f"X{pb}", c) for c in range(8)], writes=[("XT", n)])
        s.emit_phase()


def phase_ffn(k, layer, final):
    nc, s, din, dr = k.nc, k.s, k.din, k.dr
    TQ = 256
    NQ = S // TQ
    with ExitStack() as es:
        T = lambda name, shape, dt=F32: es.enter_context(nc.sbuf_tensor(f"p{k.pid}_{name}", shape, dt))
        P = lambda name, shape, dt=F32: es.enter_context(nc.psum_tensor(f"p{k.pid}_{name}", shape, dt))
        w1, w2 = k.pre["ffn_w1"], k.pre["ffn_w2"]
        W1K, W2K = [], []

        def w1k(col):
            return W1K
        NX = 3 if final else 2
        X = [T(f"X{i}", [128, 8, TQ]) for i in range(NX)]
        hT = [T(f"hT{i}", [128, 8, TQ], BF16) for i in range(2)]
        sq = T("sq", [128, 8, TQ], BF16)
        a = T("a", [128, 22, TQ], BF16)
        tmps = [T(f"tmp{i}", [128, TQ]) for i in range(2)]
        rstd = [T(f"rstd{i}", [128, TQ]) for i in range(2)]
        rstdf = T("rstdf", [128, TQ])
        sg = [T(f"sg{i}", [128, TQ]) for i in range(2)]
        ys = [T(f"ys{i}", [128, 1024]) for i in range(2)] if final else None
        mmb = [P(f"mmb{i}", [128, 512]) for i in range(4)]
        ob = [P(f"ob{i}", [128, 512]) for i in range(2)]
        ssb = P("ssb", [128, 512])
        trp = [P(f"trp{i}", [128, 512]) for i in range(1)] * 2 if final else None
        G = k.gf[:, layer, :]
        SH = k.mod[:, layer, 24:32]
        gate = k.mod[:, layer, 40:48]
        XTv = dr["XT"].rearrange("(c p) t -> p c t", p=128)
        cnt = {"mi": 0, "oi": 0, "yi": 0, "ti": 0}

        def load(n):
            xb = n % NX
            s.dma(X[xb][:], XTv[:, :, n * TQ:(n + 1) * TQ], writes=[(f"X{xb}", c) for c in range(8)])

        def norm(n):
            pb = n % 2
            xb = n % NX
            Xk = [(f"X{xb}", c) for c in range(8)]
            ssp = ssb[:, 0:TQ]
            for c in range(8):
                s.act(lambda e, c=c: e.activation(sq[:, c, :], X[xb][:, c, :], AF.Square), [Xk[c]], [("sq", c)])
            mm_group(s, ssp, [(k.ones_bf[:], sq[:, c, :]) for c in range(8)], [("sq", c) for c in range(8)], ["ssb"])
            s.act(lambda e: e.activation(rstd[pb][:], ssp, AF.Sqrt, bias=k.eps_col[:, 0:1], scale=1.0 / D), ["ssb"], [f"rs0{pb}"])
            s.dve(lambda e: e.reciprocal(rstd[pb][:], rstd[pb][:]), [f"rs0{pb}"], [f"rstd{pb}"])
            for c in range(8):
                tm, tk = tmps[c % 2], f"tmp{c % 2}"
                s.dve(lambda e, c=c, tm=tm: e.scalar_tensor_tensor(tm[:], X[xb][:, c, :], G[:, c:c + 1], rstd[pb][:], ALU.mult, ALU.mult),
                      [Xk[c], f"rstd{pb}"], [tk])
                s.act(lambda e, c=c, tm=tm: e.activation(hT[pb][:, c, :], tm[:], AF.Identity, bias=SH[:, c:c + 1], scale=1.0),
                      [tk], [(f"hT{pb}", c)])

        def gu(n):
            pb = n % 2
            hk = [(f"hT{pb}", c) for c in range(8)]
            h = hT[pb]
            out = []
            for j in range(22):
                out.append(lambda j=j: gu1(j, h, hk))
            return out

        def gu1(j, h, hk):
            if True:
                m0, m1 = cnt["mi"] % 4, (cnt["mi"] + 1) % 4
                cnt["mi"] += 2
                gp, gk = mmb[m0][:, 0:TQ], ("mmb", m0)
                up, uk = mmb[m1][:, 0:TQ], ("mmb", m1)
                mm_group(s, gp, [(w1[:, kk, j * 128:(j + 1) * 128], h[:, kk, :]) for kk in range(8)], w1k(j * 128) + hk, [gk])
                mm_group(s, up, [(w1[:, kk, DFF + j * 128:DFF + (j + 1) * 128], h[:, kk, :]) for kk in range(8)], w1k(DFF + j * 128) + hk, [uk])
                sb = j % 2
                s.act(lambda e, sb=sb, gp=gp: e.activation(sg[sb][:], gp, AF.Silu), [gk], [f"sg{sb}"])
                s.dve(lambda e, sb=sb, up=up, j=j: e.tensor_tensor(a[:, j, :], up, sg[sb][:], ALU.mult), [uk, f"sg{sb}"], [("a", j)])

        def down(n):
            pb = n % NX
            ak = [("a", j) for j in range(22)]
            for dc in range(8):
                o0 = cnt["oi"] % 2
                cnt["oi"] += 1
                op_, ok = ob[o0][:, 0:TQ], ("ob", o0)
                mm_group(s, op_, [(w2[:, j, dc * 128:(dc + 1) * 128], a[:, j, :]) for j in range(22)], W2K + ak, [ok])
                s.dve(lambda e, dc=dc, op_=op_: e.scalar_tensor_tensor(X[pb][:, dc, :], op_, gate[:, dc:dc + 1], X[pb][:, dc, :], ALU.mult, ALU.add),
                      [ok, (f"X{pb}", dc)], [(f"X{pb}", dc)])

        def finish(n):
            pb = n % NX
            Xk = [(f"X{pb}", c) for c in range(8)]
            s.dma(XTv[:, :, n * TQ:(n + 1) * TQ], X[pb][:], reads=Xk, writes=[("XT", n)])

        def fin_a(n):
            pb = n % NX
            rb = n % 2
            Xk = [(f"X{pb}", c) for c in range(8)]
            ssp = ssb[:, 0:TQ]
            for c in range(8):
                s.act(lambda e, c=c: e.activation(sq[:, c, :], X[pb][:, c, :], AF.Square), [Xk[c]], [("sq", c)])
            mm_group(s, ssp, [(k.ones_bf[:], sq[:, c, :]) for c in range(8)], [("sq", c) for c in range(8)], ["ssb"])
            s.act(lambda e: e.activation(rstdf[:], ssp, AF.Sqrt, bias=k.eps_col[:, 0:1], scale=1.0 / D), ["ssb"], ["rsf0"])
            s.dve(lambda e: e.reciprocal(rstdf[:], rstdf[:]), ["rsf0"], ["rstdf"])
            for c in range(8):
                s.dve(lambda e, c=c: e.scalar_tensor_tensor(X[pb][:, c, :], X[pb][:, c, :], k.fng[:, c:c + 1], rstdf[:], ALU.mult, ALU.mult),
                      [Xk[c], "rstdf"], [Xk[c]])

        def fin_b(n):
            pb = n % NX
            Xk = [(f"X{pb}", c) for c in range(8)]
            for a2 in range(TQ // 128):
                y, yk = ys[cnt["yi"] % 2], f"ys{cnt['yi'] % 2}"
                cnt["yi"] += 1
                for hh in range(2):
                    tp, tk = trp[0], "trp0"
                    cnt["ti"] += 1

                    def trf(e, tp=tp, hh=hh, a2=a2):
                        ins = None
                        for cc in range(4):
                            c = hh * 4 + cc
                            ins = e.transpose(tp[:, cc * 128:(cc + 1) * 128], X[pb][:, c, a2 * 128:(a2 + 1) * 128], k.ident[:])
                        return ins
                    s.pe(trf, Xk + ["ident"], [tk])
                    if hh == 0:
                        s.act(lambda e, y=y, tp=tp, hh=hh: e.copy(y[:, hh * 512:(hh + 1) * 512], tp[:]), [tk], [(yk, hh)])
                    else:
                        s.dve(lambda e, y=y, tp=tp, hh=hh: e.tensor_copy(y[:, hh * 512:(hh + 1) * 512], tp[:]), [tk], [(yk, hh)])
                r0 = n * TQ + a2 * 128
                s.dma(k.out[r0:r0 + 128, :], y[:], reads=[(yk, 0), (yk, 1)], writes=[("out", r0)])

        load(0)
        norm(0)
        for n in range(NQ):
            if n + 1 < NQ:
                load(n + 1)
            gs = gu(n)
            if final and n > 0:
                for g_ in gs[:5]:
                    g_()
                fin_a(n - 1)
                for g_ in gs[5:14]:
                    g_()
                fin_b(n - 1)
                for g_ in gs[14:]:
                    g_()
            else:
                for g_ in gs:
                    g_()
            if n + 1 < NQ:
                norm(n + 1)
            down(n)
            if not final:
                finish(n)
        if final:
            fin_a(NQ - 1)
            fin_b(NQ - 1)
        s.emit_phase()


def norm256(k, n, X, hT, sq, tmps, rstd, ssb, G, SH):
    s = k.s
    pb = n % 2
    Xk = [(f"X{pb}", c) for c in range(8)]
    ssp = ssb[:, 0:256]
    for c in range(8):
        s.act(lambda e, c=c: e.activation(sq[:, c, :], X[pb][:, c, :], AF.Square), [Xk[c]], [("sq", c)])
    mm_group(s, ssp, [(k.ones_bf[:], sq[:, c, :]) for c in range(8)], [("sq", c) for c in range(8)], ["ssb"])
    s.act(lambda e: e.activation(rstd[pb][:], ssp, AF.Sqrt, bias=k.eps_col[:, 0:1], scale=1.0 / D), ["ssb"], [f"rs0{pb}"])
    s.dve(lambda e: e.reciprocal(rstd[pb][:], rstd[pb][:]), [f"rs0{pb}"], [f"rstd{pb}"])
    for c in range(8):
        tm, tk = tmps[c % 2], f"tmp{c % 2}"
        s.dve(lambda e, c=c, tm=tm: e.scalar_tensor_tensor(tm[:], X[pb][:, c, :], G[:, c:c + 1], rstd[pb][:], ALU.mult, ALU.mult),
              [Xk[c], f"rstd{pb}"], [tk])
        s.act(lambda e, c=c, tm=tm: e.activation(hT[pb][:, c, :], tm[:], AF.Identity, bias=SH[:, c:c + 1], scale=1.0),
              [tk], [(f"hT{pb}", c)])


def phase_l1a(k):
    nc, s, din, dr = k.nc, k.s, k.din, k.dr
    TQ = 256
    NQ = S // TQ
    with ExitStack() as es:
        T = lambda name, shape, dt=F32: es.enter_context(nc.sbuf_tensor(f"p{k.pid}_{name}", shape, dt))
        P = lambda name, shape, dt=F32: es.enter_context(nc.psum_tensor(f"p{k.pid}_{name}", shape, dt))
        w = T("w_s5in", [128, 8, D], BF16)
        load_weight_bf16(k, w, din["s5_w_in"][0], 8, D, "w")
        WK = wkeys("w", 8, D)
        X = [T(f"X{i}", [128, 8, TQ]) for i in range(2)]
        hT = [T(f"hT{i}", [128, 8, TQ], BF16) for i in range(2)]
        sq = T("sq", [128, 8, TQ], BF16)
        tmps = [T(f"tmp{i}", [128, TQ]) for i in range(2)]
        rstd = [T(f"rstd{i}", [128, TQ]) for i in range(2)]
        st = [T(f"st{i}", [128, 8, TQ]) for i in range(2)]
        mm_ps = [P(f"mm_ps{i}", [128, 512]) for i in range(4)]
        ssb = P("ssb", [128, 512])
        XTv = dr["XT"].rearrange("(c p) t -> p c t", p=128)
        Uv = dr["U"].rearrange("(c p) t -> p c t", p=128)

        def load(n):
            pb = n % 2
            s.dma(X[pb][:], XTv[:, :, n * TQ:(n + 1) * TQ], writes=[(f"X{pb}", c) for c in range(8)])
        mi = 0
        load(0)
        norm256(k, 0, X, hT, sq, tmps, rstd, ssb, k.gm[:, 1, :], k.mod[:, 1, 0:8])
        for n in range(NQ):
            pb = n % 2
            if n + 1 < NQ:
                load(n + 1)
            hk = [(f"hT{pb}", c) for c in range(8)]
            for j in range(8):
                ps, pk = mm_ps[mi % 4][:, 0:TQ], f"mm_ps{mi % 4}"
                mi += 1
                mm_group(s, ps, [(w[:, kk, j * 128:(j + 1) * 128], hT[pb][:, kk, :]) for kk in range(8)], WK + hk, [pk])
                if j % 2 == 0:
                    s.act(lambda e, ps=ps, j=j, pb=pb: e.copy(st[pb][:, j, :], ps), [pk], [(f"st{pb}", j)])
                else:
                    s.dve(lambda e, ps=ps, j=j, pb=pb: e.tensor_copy(st[pb][:, j, :], ps), [pk], [(f"st{pb}", j)])
            if n + 1 < NQ:
                norm256(k, n + 1, X, hT, sq, tmps, rstd, ssb, k.gm[:, 1, :], k.mod[:, 1, 0:8])
            s.dma(Uv[:, :, n * TQ:(n + 1) * TQ], st[pb][:], reads=[(f"st{pb}", j) for j in range(8)], writes=[("U", n)])
        s.emit_phase()


def phase_s5(k):
    nc, s, din, dr = k.nc, k.s, k.din, k.dr
    with ExitStack() as es:
        T = lambda name, shape, dt=F32: es.enter_context(nc.sbuf_tensor(f"p{k.pid}_{name}", shape, dt))
        P = lambda name, shape, dt=F32: es.enter_context(nc.psum_tensor(f"p{k.pid}_{name}", shape, dt))
        LRe = T("LRe", [128, 64]); LIm = T("LIm", [128, 64]); dtb = T("dtb", [128, 64])
        for hf in range(2):
            ps_ = slice(64 * hf, 64 * hf + 64)
            s.dma(LRe[ps_, :], din["s5_lam_re"][0].rearrange("g p -> p g"), writes=[("LRe", hf)])
            s.dma(LIm[ps_, :], din["s5_lam_im"][0].rearrange("g p -> p g"), writes=[("LIm", hf)])
        s.dma(dtb[:], din["s5_log_dt"].to_broadcast([128, 64]), writes=["dtb"])
        Bre = T("Bre", [128, 64, 16]); Bim = T("Bim", [128, 64, 16])
        for hf in range(2):
            ps_ = slice(64 * hf, 64 * hf + 64)
            for g8 in range(8):
                gs = slice(g8 * 8, (g8 + 1) * 8)
                s.dma(Bre[ps_, gs, :], din["s5_b_re"][0, gs].rearrange("g p h -> p g h"), writes=[("Bre", hf, g8)])
                s.dma(Bim[ps_, gs, :], din["s5_b_im"][0, gs].rearrange("g p h -> p g h"), writes=[("Bim", hf, g8)])
        BreK = [("Bre", hf, g8) for hf in range(2) for g8 in range(8)]
        BimK = [("Bim", hf, g8) for hf in range(2) for g8 in range(8)]
        CCa = T("CCa", [128, 8, 128]); CCb = T("CCb", [128, 8, 128])
        cre = din["s5_c_re"][0].rearrange("g h p -> (g h) p").rearrange("(b q) p -> q b p", q=128)
        cim = din["s5_c_im"][0].rearrange("g h p -> (g h) p").rearrange("(b q) p -> q b p", q=128)
        s.dma(CCa[:, :, 0:64], cre, writes=[("CCa", 0)]); s.dma(CCa[:, :, 64:128], cim, writes=[("CCa", 1)])
        s.dma(CCb[:, :, 0:64], cim, writes=[("CCb", 0)]); s.dma(CCb[:, :, 64:128], cre, writes=[("CCb", 1)])
        LReK = [("LRe", 0), ("LRe", 1)]; LImK = [("LIm", 0), ("LIm", 1)]
        sm = {}

        def SM(name):
            sm[name] = T("sm_" + name, [128, 64])
            return sm[name]
        for nm in ("lr", "mag", "phi", "phis", "t0", "t1", "fs", "fc", "sinv", "cosv", "are", "aim", "den", "zre", "zim",
                   "za", "zb", "zas", "zbs", "c512", "s512"):
            SM(nm)
        smi = T("smi", [128, 64], I32)
        sgnA = T("sgnA", [128, 1])
        s.pool(lambda e: e.memset(sgnA[0:64, :], 1.0), [], [("sgnA", 0)])
        s.pool(lambda e: e.memset(sgnA[64:128, :], -1.0), [], [("sgnA", 1)])
        SGK = [("sgnA", 0), ("sgnA", 1)]
        s.act(lambda e: e.activation(dtb[:], dtb[:], AF.Exp), ["dtb"], ["dtb"])
        s.dve(lambda e: e.tensor_scalar(sm["lr"][:], LRe[:], -1e-4, None, ALU.min), LReK, ["lr"])
        s.dve(lambda e: e.tensor_tensor(sm["t0"][:], sm["lr"][:], dtb[:], ALU.mult), ["lr", "dtb"], ["t0"])
        s.act(lambda e: e.activation(sm["mag"][:], sm["t0"][:], AF.Exp), ["t0"], ["mag"])
        s.dve(lambda e: e.scalar_tensor_tensor(sm["phi"][:], LIm[:], 1.0 / TWO_PI, dtb[:], ALU.mult, ALU.mult), LImK + ["dtb"], ["phi"])
        s.dve(lambda e: e.tensor_scalar(sm["phis"][:], sm["phi"][:], sgnA[:, 0:1], None, ALU.mult), ["phi"] + SGK, ["phis"])

        def frac_sin(dst, src, add, key_src, key_dst, mult=1.0):
            s.dve(lambda e: e.tensor_scalar(sm["t0"][:], sm[src][:], mult, add, ALU.mult, ALU.add), [key_src], ["t0"])
            s.dve(lambda e: e.tensor_copy(smi[:], sm["t0"][:]), ["t0"], ["smi"])
            s.dve(lambda e: e.tensor_tensor(sm["t1"][:], sm["t0"][:], smi[:], ALU.subtract), ["t0", "smi"], ["t1"])
            s.act(lambda e: e.activation(sm[dst][:], sm["t1"][:], AF.Sin, scale=TWO_PI), ["t1"], [key_dst])
        frac_sin("sinv", "phi", 0.0, "phi", "sinv")
        frac_sin("cosv", "phi", 0.25, "phi", "cosv")
        frac_sin("c512", "phi", 0.25, "phi", "c512", mult=512.0)
        frac_sin("s512", "phis", 0.0, "phis", "s512", mult=512.0)
        s.dve(lambda e: e.tensor_tensor(sm["are"][:], sm["mag"][:], sm["cosv"][:], ALU.mult), ["mag", "cosv"], ["are"])
        s.dve(lambda e: e.tensor_tensor(sm["aim"][:], sm["mag"][:], sm["sinv"][:], ALU.mult), ["mag", "sinv"], ["aim"])
        s.dve(lambda e: e.tensor_scalar(sm["are"][:], sm["are"][:], -1.0, None, ALU.add), ["are"], ["are"])
        s.dve(lambda e: e.tensor_tensor(sm["den"][:], sm["lr"][:], sm["lr"][:], ALU.mult), ["lr"], ["den"])
        s.dve(lambda e: e.tensor_tensor(sm["t0"][:], LIm[:], LIm[:], ALU.mult), LImK, ["t0"])
        s.dve(lambda e: e.tensor_tensor(sm["den"][:], sm["den"][:], sm["t0"][:], ALU.add), ["den", "t0"], ["den"])
        s.dve(lambda e: e.reciprocal(sm["den"][:], sm["den"][:]), ["den"], ["den"])
        s.dve(lambda e: e.tensor_tensor(sm["t0"][:], sm["are"][:], sm["lr"][:], ALU.mult), ["are", "lr"], ["t0"])
        s.dve(lambda e: e.tensor_tensor(sm["t1"][:], sm["aim"][:], LIm[:], ALU.mult), ["aim"] + LImK, ["t1"])
        s.dve(lambda e: e.tensor_tensor(sm["zre"][:], sm["t0"][:], sm["t1"][:], ALU.add), ["t0", "t1"], ["zre"])
        s.dve(lambda e: e.tensor_tensor(sm["zre"][:], sm["zre"][:], sm["den"][:], ALU.mult), ["zre", "den"], ["zre"])
        s.dve(lambda e: e.tensor_tensor(sm["t0"][:], sm["aim"][:], sm["lr"][:], ALU.mult), ["aim", "lr", "zre"], ["t0"])
        s.dve(lambda e: e.tensor_tensor(sm["t1"][:], sm["are"][:], LIm[:], ALU.mult), ["are", "zre"] + LImK, ["t1"])
        s.dve(lambda e: e.tensor_tensor(sm["zim"][:], sm["t0"][:], sm["t1"][:], ALU.subtract), ["t0", "t1"], ["zim"])
        s.dve(lambda e: e.tensor_tensor(sm["zim"][:], sm["zim"][:], sm["den"][:], ALU.mult), ["zim", "den"], ["zim"])
        lo, hi = slice(0, 64), slice(64, 128)
        s.dve(lambda e: e.tensor_copy(sm["za"][lo, :], sm["zre"][lo, :]), ["zre"], [("za", 0)])
        s.dve(lambda e: e.tensor_copy(sm["za"][hi, :], sm["zim"][hi, :]), ["zim"], [("za", 1)])
        s.dve(lambda e: e.tensor_scalar(sm["zb"][lo, :], sm["zim"][lo, :], -1.0, None, ALU.mult), ["zim"], [("zb", 0)])
        s.dve(lambda e: e.tensor_copy(sm["zb"][hi, :], sm["zre"][hi, :]), ["zre"], [("zb", 1)])
        s.dve(lambda e: e.tensor_copy(sm["zas"][lo, :], sm["zim"][lo, :]), ["zim"], [("zas", 0)])
        s.dve(lambda e: e.tensor_copy(sm["zas"][hi, :], sm["zre"][hi, :]), ["zre"], [("zas", 1)])
        s.dve(lambda e: e.tensor_copy(sm["zbs"][lo, :], sm["zre"][lo, :]), ["zre"], [("zbs", 0)])
        s.dve(lambda e: e.tensor_scalar(sm["zbs"][hi, :], sm["zim"][hi, :], -1.0, None, ALU.mult), ["zim"], [("zbs", 1)])
        BB = T("BB", [128, 64, 16]); BBs = T("BBs", [128, 64, 16]); bt0 = T("bbt0", [128, 64, 16])

        def mkbb(dst, dkey, za, zb):
            zak = [(za, 0), (za, 1)]; zbk = [(zb, 0), (zb, 1)]
            s.dve(lambda e: e.tensor_tensor(bt0[:], Bre[:], sm[za][:].unsqueeze(2).to_broadcast([128, 64, 16]), ALU.mult), BreK + zak, ["bt0"])
            s.dve(lambda e: e.tensor_tensor(dst[:], Bim[:], sm[zb][:].unsqueeze(2).to_broadcast([128, 64, 16]), ALU.mult), BimK + zbk, [dkey])
            s.dve(lambda e: e.tensor_tensor(dst[:], dst[:], bt0[:], ALU.add), [dkey, "bt0"], [dkey])
        mkbb(BB, "BB", "za", "zb")
        mkbb(BBs, "BBs", "zas", "zbs")
        Bfull = T("Bfull", [128, 8, 128]); Bsfull = T("Bsfull", [128, 8, 128])
        M1full = T("M1full", [128, 8, 128]); M2full = T("M2full", [128, 8, 128])
        ctr_ps = P("ctr_ps", [128, 512])
        BBv = BB[:].rearrange("p (b g) h -> p b (g h)", b=8)
        BBsv = BBs[:].rearrange("p (b g) h -> p b (g h)", b=8)
        for (src, skeys, dst, dkey, sgn) in ((BBv, ["BB"], Bfull, "Bfull", None), (BBsv, ["BBs"], Bsfull, "Bsfull", None),
                                             (CCa[:], [("CCa", 0), ("CCa", 1)], M1full, "M1full", 1.0),
                                             (CCb[:], [("CCb", 0), ("CCb", 1)], M2full, "M2full", -1.0)):
            for b4 in range(2):
                def trf(e, src=src, b4=b4):
                    ins = None
                    for a in range(4):
                        ins = e.transpose(ctr_ps[:, a * 128:(a + 1) * 128], src[:, b4 * 4 + a, :], k.ident[:])
                    return ins
                s.pe(trf, skeys + ["ident"], ["ctr_ps"])
                dv = dst[:, b4 * 4:(b4 + 1) * 4, :]
                pv = ctr_ps[:].rearrange("p (a f) -> p a f", f=128)
                if sgn is None:
                    s.act(lambda e, dv=dv, pv=pv: e.copy(dv, pv), ["ctr_ps"], [(dkey, b4)])
                else:
                    s.dve(lambda e, dv=dv, pv=pv, sgn=sgn: e.tensor_scalar(dv, pv, sgnA[:, 0:1], sgn, ALU.mult, ALU.mult), ["ctr_ps"] + SGK, [(dkey, b4)])
        rowm = T("rowm", [128, 8]); rtmp = T("rtmp", [128, 8]); rm1 = T("rm1", [128, 8])
        s.pool(lambda e: e.iota(rtmp[:], pattern=[[-16, 8]], base=0, channel_multiplier=1, allow_small_or_imprecise_dtypes=True), [], ["rtmp"])
        s.dve(lambda e: e.tensor_single_scalar(rm1[:], rtmp[:], 0.0, ALU.is_ge), ["rtmp"], ["rm1"])
        s.dve(lambda e: e.scalar_tensor_tensor(rowm[:], rtmp[:], 16.0, rm1[:], ALU.is_lt, ALU.mult), ["rtmp", "rm1"], ["rowm"])
        colm = T("colm", [128, 8, 128]); ctmp = T("ctmp", [128, 8, 128]); cm1 = T("cm1", [128, 8, 128])
        s.pool(lambda e: e.iota(ctmp[:], pattern=[[-16, 8], [1, 128]], base=0, channel_multiplier=0, allow_small_or_imprecise_dtypes=True), [], ["ctmp"])
        s.dve(lambda e: e.tensor_single_scalar(cm1[:], ctmp[:], 0.0, ALU.is_ge), ["ctmp"], ["cm1"])
        s.dve(lambda e: e.scalar_tensor_tensor(colm[:], ctmp[:], 16.0, cm1[:], ALU.is_lt, ALU.mult), ["ctmp", "cm1"], ["colm"])
        Jsw = T("Jsw", [128, 128]); jt = T("jt", [128, 128]); je = T("je", [128, 128])
        s.pool(lambda e: e.iota(jt[:], pattern=[[1, 128]], base=0, channel_multiplier=-1, allow_small_or_imprecise_dtypes=True), [], ["jt"])
        s.dve(lambda e: e.tensor_single_scalar(je[:], jt[:], 64.0, ALU.is_equal), ["jt"], ["je"])
        s.dve(lambda e: e.scalar_tensor_tensor(Jsw[:], jt[:], -64.0, je[:], ALU.is_equal, ALU.add), ["jt", "je"], ["Jsw"])
        iot = T("iot", [128, 512])
        s.pool(lambda e: e.iota(iot[:], pattern=[[1, 512]], base=0, channel_multiplier=0, allow_small_or_imprecise_dtypes=True), [], ["iot"])
        u32 = [T(f"u32_{i}", [128, S]) for i in range(2)]
        ub = [T(f"ub{i}", [128, S], BF16) for i in range(2)]
        Bpad = [T(f"Bpad{i}", [128, 8, 128], BF16) for i in range(2)]
        Bspad = [T(f"Bspad{i}", [128, 8, 128], BF16) for i in range(2)]
        M1pad = [T(f"M1pad{i}", [128, 8, 128], BF16) for i in range(2)]
        M2pad = [T(f"M2pad{i}", [128, 8, 128], BF16) for i in range(2)]
        CSb = [T(f"CStabb{i}", [128, 8, 2, 512], BF16) for i in range(2)]
        t12b = [T(f"t12b{i}", [128, 2, 512], BF16) for i in range(2)]
        P12 = [T(f"P12_{i}", [128, 2, 512], BF16) for i in range(2)]
        wb = [T(f"wb{i}", [128, 512], BF16) for i in range(2)]
        Rot = [T(f"Rot{i}", [128, 8, 128]) for i in range(2)]
        vt = T("vt", [128, 512])
        vti = T("vti", [128, 512], I32)
        vf = T("vf", [128, 512])
        wt = [T(f"wt{i}", [128, 512]) for i in range(2)]
        wlast = T("wlast", [128, 8])
        inits = T("inits", [128, 8])
        yt = [T(f"yt{i}", [128, 512]) for i in range(2)]
        yg = [T(f"yg{i}", [128, 512], BF16) for i in range(2)]
        bub = [P(f"bub{i}", [128, 2, 512]) for i in range(2)]
        y_ps = [P(f"y_ps{i}", [128, 512]) for i in range(1)]
        bt_ps = [P(f"bt_ps{i}", [128, 512]) for i in range(1)] * 2
        dm_ps = P("dm_ps", [128, 512])

        def keep_warm(nrep):
            def fn(e):
                ins = None
                for _ in range(nrep):
                    ins = e.matmul(dm_ps[:], k.ident_bf[:], k.warm_src[:], start=True, stop=True)
                return ins
            s.pe(fn, [], [])

        in_ps = ctr_ps
        it = 0

        def load_u(b):
            bp = b % 2
            rows = slice(b * 128, (b + 1) * 128)
            s.dma(u32[bp][:], dr["U"][rows, :], writes=[f"u32_{bp}"])
            s.dma(ub[bp][:], dr["U"][rows, :], writes=[f"ub{bp}"], eng="pool")

        def setup_group(b, gl):
            bp = b % 2
            g = b * 8 + gl
            s.dve(lambda e: e.tensor_scalar(Bpad[bp][:, gl, :], Bfull[:, b, :], rowm[:, gl:gl + 1], None, ALU.mult),
                  [("Bfull", b // 4), "rowm"], [("Bpad", bp, gl)])
            s.dve(lambda e: e.tensor_scalar(Bspad[bp][:, gl, :], Bsfull[:, b, :], rowm[:, gl:gl + 1], None, ALU.mult),
                  [("Bsfull", b // 4), "rowm"], [("Bspad", bp, gl)])
            s.dve(lambda e: e.tensor_tensor(M1pad[bp][:, gl, :], M1full[:, b, :], colm[:, gl, :], ALU.mult),
                  [("M1full", b // 4), "colm"], [("M1pad", bp, gl)])
            s.dve(lambda e: e.tensor_tensor(M2pad[bp][:, gl, :], M2full[:, b, :], colm[:, gl, :], ALU.mult),
                  [("M2full", b // 4), "colm"], [("M2pad", bp, gl)])
            for (ti, ph, add) in ((0, "phi", 0.25), (1, "phis", 0.0)):
                s.act(lambda e, ph=ph, add=add: e.activation(vt[:], iot[:], AF.Identity, bias=add, scale=sm[ph][:, g:g + 1]), ["iot", ph], ["vt"])
                s.act(lambda e: e.copy(vti[:], vt[:]), ["vt"], ["vti"])
                s.pool(lambda e: e.tensor_tensor(vf[:], vt[:], vti[:], ALU.subtract), ["vt", "vti"], ["vf"])
                s.act(lambda e, ti=ti: e.activation(CSb[bp][:, gl, ti, :], vf[:], AF.Sin, scale=TWO_PI), ["vf"], [("CSb", bp, gl, ti)])
            s.dve(lambda e: e.tensor_scalar(Rot[bp][:, gl, :], k.ident[:], sm["c512"][:, g:g + 1], None, ALU.mult),
                  ["ident", "c512"], [("Rot", bp, gl)])
            s.dve(lambda e: e.scalar_tensor_tensor(Rot[bp][:, gl, :], Jsw[:], sm["s512"][:, g:g + 1], Rot[bp][:, gl, :], ALU.mult, ALU.add),
                  ["Jsw", "s512", ("Rot", bp, gl)], [("Rot", bp, gl)])

        load_u(0)
        for gl in range(8):
            setup_group(0, gl)
        for b in range(8):
            bp = b % 2
            rows = slice(b * 128, (b + 1) * 128)
            if b + 1 < 8:
                load_u(b + 1)
            stZ, stZi, stZa, stA, stB0, stB = [], [], [], [], [], []
            for kb in range(8):
                for gl in range(8):
                    i2 = it % 2
                    it += 1

                    def Z_(kb=kb, gl=gl, i2=i2, bp=bp):
                        tcs = slice(kb * 512, (kb + 1) * 512)
                        def buf(e):
                            e.matmul(bub[i2][:, 0, :], Bpad[bp][:, gl, :], ub[bp][:, tcs], start=True, stop=True)
                            return e.matmul(bub[i2][:, 1, :], Bspad[bp][:, gl, :], ub[bp][:, tcs], start=True, stop=True)
                        s.pe(buf, [("Bpad", bp, gl), ("Bspad", bp, gl), f"ub{bp}"], [f"bub{i2}"])
                        keep_warm(2)

                    def Zi_(kb=kb, gl=gl, bp=bp):
                        if kb > 0:
                            s.pe(lambda e: e.matmul(in_ps[:, gl:gl + 1], Rot[bp][:, gl, :], wlast[:, gl:gl + 1], start=True, stop=True),
                                 [("Rot", bp, gl), ("wlast", gl)], ["in_ps"])

                    def Za_(kb=kb, gl=gl):
                        if kb > 0:
                            s.act(lambda e: e.copy(inits[:, gl:gl + 1], in_ps[:, gl:gl + 1]), ["in_ps"], [("inits", gl)])

                    def A_(kb=kb, gl=gl, i2=i2, bp=bp):
                        s.dve(lambda e: e.tensor_tensor(t12b[i2][:, 0, :], bub[i2][:, 0, :], CSb[bp][:, gl, 0, :], ALU.mult),
                              [f"bub{i2}", ("CSb", bp, gl, 0)], [(f"t12b{i2}", 0)])
                        s.pe(lambda e: e.matmul(bt_ps[i2][:], k.ident_bf[:], t12b[i2][:, 0, :], start=True, stop=False),
                             [(f"t12b{i2}", 0), "ident_bf"], [f"bt_ps{i2}"])
                        s.dve(lambda e: e.tensor_tensor(t12b[i2][:, 1, :], bub[i2][:, 1, :], CSb[bp][:, gl, 1, :], ALU.mult),
                              [f"bub{i2}", ("CSb", bp, gl, 1)], [(f"t12b{i2}", 1)])
                        s.pe(lambda e: e.matmul(bt_ps[i2][:], k.ident_bf[:], t12b[i2][:, 1, :], start=False, stop=True),
                             [(f"t12b{i2}", 1), "ident_bf"], [f"bt_ps{i2}"])
                        keep_warm(1)

                    def B0_(kb=kb, gl=gl, i2=i2, b=b, rows=rows):
                        g = b * 8 + gl
                        tcs = slice(kb * 512, (kb + 1) * 512)
                        yp, ypk = y_ps[0], "y_ps0"
                        w_, wk = wt[i2], f"wt{i2}"
                        if kb == 0:
                            init = 0.0
                            ik = []
                        else:
                            init = inits[:, gl:gl + 1]
                            ik = [("inits", gl)]
                        s.dve(lambda e: e.tensor_tensor_scan(w_[:], sm["mag"][:, g:g + 1].to_broadcast([128, 512]), bt_ps[i2][:], init,
                                                             ALU.mult, ALU.add), [f"bt_ps{i2}", "mag"] + ik, [wk])
                        if kb < 7:
                            s.act(lambda e: e.copy(wlast[:, gl:gl + 1], w_[:, 511:512]), [wk], [("wlast", gl)])
                        s.act(lambda e: e.copy(wb[i2][:], w_[:]), [wk], [f"wb{i2}"])

                    def B_(kb=kb, gl=gl, i2=i2, b=b, rows=rows, bp=bp):
                        tcs = slice(kb * 512, (kb + 1) * 512)
                        yp, ypk = y_ps[0], "y_ps0"
                        s.dve(lambda e: e.tensor_tensor(P12[i2][:], wb[i2][:].unsqueeze(1).to_broadcast([128, 2, 512]), CSb[bp][:, gl, :, :], ALU.mult),
                              [f"wb{i2}", ("CSb", bp, gl, 0), ("CSb", bp, gl, 1)], [f"P12_{i2}"])
                        s.pe(lambda e: e.matmul(yp[:], M1pad[bp][:, gl, :], P12[i2][:, 0, :], start=(gl == 0), stop=False),
                             [("M1pad", bp, gl), f"P12_{i2}"], [ypk])
                        s.pe(lambda e: e.matmul(yp[:], M2pad[bp][:, gl, :], P12[i2][:, 1, :], start=False, stop=(gl == 7)),
                             [("M2pad", bp, gl), f"P12_{i2}"], [ypk])
                        keep_warm(1)
                        if gl == 7:
                            yb = kb % 2
                            s.dve(lambda e: e.scalar_tensor_tensor(yt[yb][:], u32[bp][:, tcs], k.s5d[:, b:b + 1], yp[:], ALU.mult, ALU.add),
                                  [f"u32_{bp}", ypk], [f"yt{yb}"])
                            s.act(lambda e: e.activation(yg[yb][:], yt[yb][:], AF.Gelu_apprx_tanh), [f"yt{yb}"], [f"yg{yb}"])
                            s.dma(dr["YG"][rows, tcs], yg[yb][:], reads=[f"yg{yb}"], writes=[("YG", b, kb)])
                    stZ.append(Z_)
                    stZi.append(Zi_)
                    stZa.append(Za_)
                    stA.append(A_)
                    stB0.append(B0_)
                    stB.append(B_)
            n = len(stA)
            for t in range(-2, n):
                if 0 <= t < n:
                    stB0[t]()
                if 0 <= t + 1 < n:
                    stZa[t + 1]()
                    stA[t + 1]()
                if 0 <= t + 2 < n:
                    stZ[t + 2]()
                    stZi[t + 2]()
                if 0 <= t < n:
                    stB[t]()
                if b + 1 < 8 and t >= 0 and t % 8 == 4:
                    setup_group(b + 1, t // 8)
        s.emit_phase()


def build(debug_upto=None):
    nc = bass.Bass("TRN2", target_bir_lowering=False)
    k = K()
    k.nc = nc
    din = {}

    def inp(name, shape):
        din[name] = nc.dram_tensor(name, list(shape), F32, kind="ExternalInput").ap()

    inp("x", [S, D]); inp("c", [1, D])
    inp("norm_mix_g", [2, D]); inp("norm_ffn_g", [2, D])
    inp("ada_w", [2, D, 6 * D]); inp("ada_b", [2, 6 * D])
    inp("ffn_w_in", [2, D, 2 * DFF]); inp("ffn_w_out", [2, DFF, D])
    inp("final_norm_g", [1, D])
    inp("hy_w_in", [1, D, 3584]); inp("hy_w_out", [1, D, D])
    inp("hg_norm_g", [1, 512]); inp("hg_lb_logits", [3, 512])
    inp("s5_w_in", [1, D, D])
    inp("s5_lam_re", [1, 64, 64]); inp("s5_lam_im", [1, 64, 64]); inp("s5_log_dt", [1, 64])
    inp("s5_b_re", [1, 64, 64, 16]); inp("s5_b_im", [1, 64, 64, 16])
    inp("s5_c_re", [1, 64, 16, 64]); inp("s5_c_im", [1, 64, 16, 64])
    inp("s5_d", [1, D]); inp("s5_w_glu", [1, D, 2 * D])
    k.din = din
    kind = "ExternalOutput" if debug_upto is not None else "Internal"
    dr = {}

    def scr(name, shape, dt):
        dr[name] = nc.dram_tensor(name, list(shape), dt, kind=kind).ap()

    scr("XT", [D, S], F32); scr("QK", [D, S], BF16)
    scr("VA", [S, 512], BF16); scr("IB", [S, 512], BF16)
    scr("QB", [512, S], F32); scr("FB", [512, S], F32); scr("GB", [512, S], F32)
    scr("MT", [D, S], BF16); scr("U", [D, S], F32); scr("YG", [D, S], BF16)
    k.dr = dr
    k.out = nc.dram_tensor("out", [S, D], F32, kind="ExternalOutput").ap()
    with ExitStack() as es:
        es.enter_context(nc.allow_non_contiguous_dma(reason="small strided parameter loads"))
        T = lambda name, shape, dt=F32: es.enter_context(nc.sbuf_tensor(name, shape, dt))
        k.s = s = Sched(nc, es)
        k.ident = T("ident", [128, 128])
        k.ident_bf = T("ident_bf", [128, 128], BF16)
        k.ones_bf = T("ones_bf", [128, 128], BF16)
        k.eps_col = T("eps_col", [128, 1])
        k.warm_src = T("warm_src", [128, 512], BF16)
        k.nmg = T("nmg", [128, 2, 8]); k.nfg = T("nfg", [128, 2, 8]); k.fng = T("fng", [128, 8])
        k.s5d = T("s5d", [128, 8]); k.hgg = T("hgg", [128, 4])
        k.lb = T("lb", [128, 4]); k.oml = T("oml", [128, 4])
        k.mod = T("mod", [128, 2, 48]); k.gm = T("gm", [128, 2, 8]); k.gf = T("gf", [128, 2, 8])
        s.pool(lambda e: e.memset(k.ident[:], 1.0), [], ["ident"])
        s.pool(lambda e: e.affine_select(k.ident[:], k.ident[:], pattern=[[-1, 128]], compare_op=ALU.is_equal, fill=0.0,
                                         base=0, channel_multiplier=1), ["ident"], ["ident"])
        s.pool(lambda e: e.tensor_copy(k.ident_bf[:], k.ident[:]), ["ident"], ["ident_bf"])
        s.pool(lambda e: e.memset(k.ones_bf[:], 1.0), [], ["ones_bf"])
        s.pool(lambda e: e.memset(k.eps_col[:], EPS), [], ["eps_col"])
        s.pool(lambda e: e.memset(k.warm_src[:], 1.0), [], ["warm_src"])
        phases = [phase_pre, phase_l0a, phase_attn, phase_hgrn, lambda k: phase_b1(k, 0), lambda k: phase_ffn(k, 0, False), phase_l1a,
                  phase_s5, lambda k: phase_b1(k, 1), lambda k: phase_ffn(k, 1, True)]
        k.pre = {}
        scopes = {0: (1, [("w_in0", [128, 8, 3584])]), 4: (5, [("ffn_w1", [128, 8, 2 * DFF]), ("ffn_w2", [128, 22, D])]),
                  8: (9, [("ffn_w1", [128, 8, 2 * DFF]), ("ffn_w2", [128, 22, D])])}
        open_scope = None
        for i, ph in enumerate(phases):
            k.pid = i
            if i in scopes:
                open_scope = (scopes[i][0], ExitStack())
                for nm, shp in scopes[i][1]:
                    k.pre[nm] = open_scope[1].enter_context(nc.sbuf_tensor(f"pre{i}_" + nm, shp, BF16))
            ph(k)
            if open_scope is not None and open_scope[0] == i:
                open_scope[1].close()
                open_scope = None
            if debug_upto is not None and i >= debug_upto:
                break
        if open_scope is not None:
            open_scope[1].close()
    return nc


_NC_CACHE = {}


def kernel(**inputs):
    if "nc" not in _NC_CACHE:
        _NC_CACHE["nc"] = build()
    nc = _NC_CACHE["nc"]
    n = 8
    shared = {}
    for name, v in inputs.items():
        if name in ("x", "c"):
            continue
        a = np.ascontiguousarray(np.asarray(v, dtype=np.float32))
        if name == "final_norm_g":
            a = a.reshape(1, -1)
        shared[name] = a
    x = np.asarray(inputs["x"], dtype=np.float32)
    c = np.asarray(inputs["c"], dtype=np.float32)
    in_maps = []
    for b in range(n):
        m = dict(shared)
        m["x"] = np.ascontiguousarray(x[b])
        m["c"] = np.ascontiguousarray(c[b:b + 1])
        in_maps.append(m)
    res = run_bass_kernel_spmd(nc, in_maps, core_ids=list(range(n)))
    return np.stack([np.asarray(r["out"], dtype=np.float32) for r in res.results], axis=0)
```

```python
from contextlib import ExitStack

import numpy as np
import concourse.bass as bass
import concourse.mybir as mybir
from concourse.bass_utils import run_bass_kernel_spmd

F32 = mybir.dt.float32
BF16 = mybir.dt.bfloat16
I32 = mybir.dt.int32
AF = mybir.ActivationFunctionType
ALU = mybir.AluOpType

ENGS = ["pe", "act", "dve", "pool", "sp"]
RING = 8
S = 4096
D = 1024
TB = 512
NB = S // TB
DFF = 2816
EPS = 1e-6
TWO_PI = float(2 * np.pi)


class Op:
    __slots__ = ("eng", "fn", "deps", "cnt", "need_inc", "dma", "ring", "gen")

    def __init__(self, eng, fn, dma):
        self.eng = eng
        self.fn = fn
        self.dma = dma
        self.deps = []
        self.cnt = 0
        self.need_inc = False
        self.ring = 0
        self.gen = 0


class Sched:
    def __init__(self, nc, es):
        self.nc = nc
        self.ops = {e: [] for e in ENGS}
        self.last_w = {}
        self.readers = {}
        self.sems = {e: es.enter_context(nc.semaphore(f"s_{e}")) for e in ENGS}
        self.rings = {e: [es.enter_context(nc.semaphore(f"r_{e}{i}")) for i in range(RING)] for e in ("sp", "pool", "act")}
        self.cnt = {e: 0 for e in ENGS}
        self.ndma = {e: 0 for e in ENGS}
        self.waited = {e: {} for e in ENGS}

    def add(self, eng, fn, reads=(), writes=(), dma=False):
        op = Op(eng, fn, dma)
        deps = {}
        for b in reads:
            w = self.last_w.get(b)
            if w is not None:
                deps[id(w)] = w
        for b in writes:
            w = self.last_w.get(b)
            if w is not None:
                deps[id(w)] = w
            for r in self.readers.get(b, ()):
                deps[id(r)] = r
        for d in deps.values():
            if d.eng == "pe" and eng == "pe" and not d.dma and not dma:
                continue
            op.deps.append(d)
            d.need_inc = True
        for b in reads:
            self.readers.setdefault(b, []).append(op)
        for b in writes:
            self.last_w[b] = op
            self.readers[b] = []
        self.ops[eng].append(op)
        return op

    def pe(self, fn, reads=(), writes=()):
        return self.add("pe", fn, reads, writes)

    def act(self, fn, reads=(), writes=()):
        return self.add("act", fn, reads, writes)

    def dve(self, fn, reads=(), writes=()):
        return self.add("dve", fn, reads, writes)

    def pool(self, fn, reads=(), writes=()):
        return self.add("pool", fn, reads, writes)

    def dma(self, out, in_, reads=(), writes=(), eng="sp"):
        return self.add(eng, lambda e: e.dma_start(out=out, in_=in_), reads, writes, dma=True)

    def emit_phase(self):
        nc = self.nc
        ops = self.ops
        self.ops = {e: [] for e in ENGS}
        for e in ENGS:
            for op in ops[e]:
                if op.dma:
                    di = self.ndma[e]
                    op.ring = di % RING
                    op.gen = di // RING + 1
                    self.ndma[e] = di + 1
                elif op.need_inc:
                    self.cnt[e] += 1
                    op.cnt = self.cnt[e]
        sems, rings = self.sems, self.rings

        def run_engine(e, eng):
            waited = self.waited[e]

            def wait(key, sem, val):
                if waited.get(key, 0) >= val:
                    return
                eng.wait_ge(sem, val)
                waited[key] = val

            for op in ops[e]:
                for d in op.deps:
                    if d.dma:
                        wait((d.eng, d.ring), rings[d.eng][d.ring], 16 * d.gen)
                    else:
                        wait((d.eng,), sems[d.eng], d.cnt)
                if op.dma:
                    if op.gen > 1:
                        wait((e, op.ring), rings[e][op.ring], 16 * (op.gen - 1))
                    op.fn(eng).then_inc(rings[e][op.ring], 16)
                else:
                    ins = op.fn(eng)
                    if op.need_inc:
                        ins.then_inc(sems[e], 1)
            if e in rings:
                n = self.ndma[e]
                for r in range(RING):
                    c = (n - r + RING - 1) // RING if n > r else 0
                    if c > 0:
                        wait((e, r), rings[e][r], 16 * c)

        with nc.Block() as block:
            @block.tensor
            def _(eng):
                run_engine("pe", eng)

            @block.scalar
            def _(eng):
                run_engine("act", eng)

            @block.vector
            def _(eng):
                run_engine("dve", eng)

            @block.gpsimd
            def _(eng):
                run_engine("pool", eng)

            @block.sync
            def _(eng):
                run_engine("sp", eng)

        self.last_w = {}
        self.readers = {}


class K:
    pass


def mm_group(s, out, pairs, reads, writes):
    n = len(pairs)

    def fn(e):
        ins = None
        for i, (l, r) in enumerate(pairs):
            ins = e.matmul(out, l, r, start=(i == 0), stop=(i == n - 1))
        return ins

    return s.pe(fn, reads, writes)


def phase_pre(k):
    nc, s, din = k.nc, k.s, k.din
    with ExitStack() as es:
        T = lambda name, shape, dt=F32: es.enter_context(nc.sbuf_tensor(f"p{k.pid}_{name}", shape, dt))
        P = lambda name, shape, dt=F32: es.enter_context(nc.psum_tensor(f"p{k.pid}_{name}", shape, dt))
        load_weight_bf16(k, k.pre["w_in0"], din["hy_w_in"][0], 8, 3584, "w_in0")
        c_sb = T("c_sb", [128, 8])
        cs = T("cs", [128, 8])
        s.dma(c_sb[:], din["c"].rearrange("o (j p) -> p (o j)", p=128), writes=["c_sb"])
        s.act(lambda e: e.activation(cs[:], c_sb[:], AF.Silu), ["c_sb"], ["cs"])
        s.dma(k.nmg[:], din["norm_mix_g"].rearrange("l (j p) -> p l j", p=128), writes=["nmg"])
        s.dma(k.nfg[:], din["norm_ffn_g"].rearrange("l (j p) -> p l j", p=128), writes=["nfg"])
        s.dma(k.fng[:], din["final_norm_g"].rearrange("o (j p) -> p (o j)", p=128), writes=["fng"])
        s.dma(k.s5d[:], din["s5_d"].rearrange("o (j p) -> p (o j)", p=128), writes=["s5d"])
        s.dma(k.hgg[:], din["hg_norm_g"].rearrange("o (j p) -> p (o j)", p=128), writes=["hgg"])
        adb = T("adb", [128, 2, 48])
        s.dma(adb[:], din["ada_b"].rearrange("l (j p) -> p l j", p=128), writes=["adb"])
        lg = T("lg", [128, 3, 4])
        s.dma(lg[:], din["hg_lb_logits"].rearrange("r (h p) -> p r h", p=128), writes=["lg"])
        dl = T("dl", [128, 2, 4])
        s.dve(lambda e: e.tensor_tensor(dl[:, 0, :], lg[:, 1, :], lg[:, 0, :], ALU.subtract), ["lg"], ["dl0"])
        s.dve(lambda e: e.tensor_tensor(dl[:, 1, :], lg[:, 2, :], lg[:, 0, :], ALU.subtract), ["lg"], ["dl1"])
        s.act(lambda e: e.activation(dl[:], dl[:], AF.Exp), ["dl0", "dl1"], ["dle"])
        den = T("den", [128, 4])
        s.dve(lambda e: e.scalar_tensor_tensor(den[:], dl[:, 0, :], 1.0, dl[:, 1, :], ALU.add, ALU.add), ["dle"], ["den"])
        s.dve(lambda e: e.reciprocal(k.lb[:], den[:]), ["den"], ["lb"])
        s.dve(lambda e: e.tensor_scalar(k.oml[:], k.lb[:], -1.0, 1.0, ALU.mult, ALU.add), ["lb"], ["oml"])
        slabs = [T(f"slab{i}", [128, 8, 512]) for i in range(3)]
        mod_ps = [P(f"mod_ps{l}", [128, 48]) for l in range(2)]
        it = 0
        for l in range(2):
            for si in range(12):
                sl = slabs[it % 3]
                key = f"slab{it % 3}"
                it += 1
                src = din["ada_w"][l, :, si * 512:(si + 1) * 512].rearrange("(kk p) n -> p kk n", p=128)
                s.dma(sl[:], src, writes=[key])
                for jj in range(4):
                    j = si * 4 + jj
                    mm_group(s, mod_ps[l][:, j:j + 1],
                             [(sl[:, kk, jj * 128:(jj + 1) * 128], cs[:, kk:kk + 1]) for kk in range(8)],
                             [key, "cs"], [f"mod_ps{l}"])
            s.dve(lambda e, l=l: e.tensor_tensor(k.mod[:, l, :], mod_ps[l][:], adb[:, l, :], ALU.add),
                  [f"mod_ps{l}", "adb"], [f"mod{l}"])
            s.dve(lambda e, l=l: e.scalar_tensor_tensor(k.gm[:, l, :], k.mod[:, l, 8:16], 1.0, k.nmg[:, l, :], ALU.add, ALU.mult),
                  [f"mod{l}", "nmg"], [f"gm{l}"])
            s.dve(lambda e, l=l: e.scalar_tensor_tensor(k.gf[:, l, :], k.mod[:, l, 32:40], 1.0, k.nfg[:, l, :], ALU.add, ALU.mult),
                  [f"mod{l}", "nfg"], [f"gf{l}"])
        s.emit_phase()


def load_weight_bf16(k, dst, src_rows, nk, ncols, key, chunk=2048):
    s = k.s
    for kk in range(nk):
        for c0 in range(0, ncols, chunk):
            c1 = min(ncols, c0 + chunk)
            s.dma(dst[:, kk, c0:c1], src_rows[kk * 128:(kk + 1) * 128, c0:c1], writes=[(key, kk, c0)], eng="pool")


def wkeys(key, nk, ncols, chunk=2048):
    return [(key, kk, c0) for kk in range(nk) for c0 in range(0, ncols, chunk)]


def norm_block(k, X, xkey, hT, hkey, sq, sqkey, G, SH, tmps, ss_ps, sskey, rstd, tag):
    s = k.s
    n = X.shape[2]
    for c in range(8):
        s.act(lambda e, c=c: e.activation(sq[:, c, :], X[:, c, :], AF.Square), [xkey], [(sqkey, c)])
    mm_group(s, ss_ps[:, :n], [(k.ones_bf[:], sq[:, c, :]) for c in range(8)], [(sqkey, c) for c in range(8)], [sskey])
    s.act(lambda e: e.activation(rstd[:, :n], ss_ps[:, :n], AF.Sqrt, bias=k.eps_col[:, 0:1], scale=1.0 / D), [sskey], [tag + "rs0"])
    s.dve(lambda e: e.reciprocal(rstd[:, :n], rstd[:, :n]), [tag + "rs0"], [tag + "rstd"])
    if hT is None:
        return
    for c in range(8):
        tm = tmps[c % 2]
        tk = f"{tag}tmp{c % 2}"
        s.dve(lambda e, c=c, tm=tm: e.scalar_tensor_tensor(tm[:, :n], X[:, c, :], G[:, c:c + 1], rstd[:, :n], ALU.mult, ALU.mult),
              [xkey, tag + "rstd"], [tk])
        s.act(lambda e, c=c, tm=tm: e.activation(hT[:, c, :], tm[:, :n], AF.Identity, bias=SH[:, c:c + 1], scale=1.0),
              [tk], [(hkey, c)])


def phase_l0a(k):
    nc, s, din, dr = k.nc, k.s, k.din, k.dr
    TQ = 256
    NQ = S // TQ
    with ExitStack() as es:
        T = lambda name, shape, dt=F32: es.enter_context(nc.sbuf_tensor(f"p{k.pid}_{name}", shape, dt))
        P = lambda name, shape, dt=F32: es.enter_context(nc.psum_tensor(f"p{k.pid}_{name}", shape, dt))
        w = k.pre["w_in0"]
        xtok = T("xtok", [128, 2, 1024])
        X = [T(f"X{i}", [128, 8, TQ]) for i in range(2)]
        hT = [T(f"hT{i}", [128, 8, TQ], BF16) for i in range(2)]
        sq = T("sq", [128, 8, TQ], BF16)
        tmps = [T(f"tmp{i}", [128, TQ]) for i in range(2)]
        rstd = [T(f"rstd{i}", [128, TQ]) for i in range(2)]
        tr_ps = [P(f"tr_ps{i}", [128, 512]) for i in range(2)]
        ssb = P("ssb", [128, 512])
        mm_ps = [P(f"mm_ps{i}", [128, 512]) for i in range(4)]
        st_qk = [T(f"st_qk{i}", [128, 8, TQ], BF16) for i in range(2)]
        st_f = {nm: [T(f"st_{nm}{i}", [128, 4, TQ]) for i in range(2)] for nm in ("qb", "fb", "gb")}
        st_v = {nm: [T(f"st_{nm}{i}", [128, 2, 512], BF16) for i in range(2)] for nm in ("va", "ib")}
        G = k.gm[:, 0, :]
        SH = k.mod[:, 0, 0:8]
        cnt = {"ev": 0, "mi": 0}

        def loadx(n):
            s.dma(xtok[:], din["x"][n * TQ:(n + 1) * TQ, :].rearrange("(a p) d -> p a d", p=128), writes=["xtok"])

        def pre(n):
            pb = n % 2
            for c in range(8):
                tp, tk = tr_ps[c % 2], f"tr_ps{c % 2}"

                def trf(e, c=c, tp=tp):
                    ins = None
                    for a_ in range(2):
                        ins = e.transpose(tp[:, a_ * 128:(a_ + 1) * 128], xtok[:, a_, c * 128:(c + 1) * 128], k.ident[:])
                    return ins
                s.pe(trf, ["xtok", "ident"], [tk])
                if c % 2 == 0:
                    s.dve(lambda e, c=c, tp=tp: e.tensor_copy(X[pb][:, c, :], tp[:, 0:TQ]), [tk], [(f"X{pb}", c)])
                else:
                    s.act(lambda e, c=c, tp=tp: e.copy(X[pb][:, c, :], tp[:, 0:TQ]), [tk], [(f"X{pb}", c)])
            norm256(k, n, X, hT, sq, tmps, rstd, ssb, G, SH)
            s.dma(dr["XT"].rearrange("(c p) t -> p c t", p=128)[:, :, n * TQ:(n + 1) * TQ], X[pb][:],
                  reads=[(f"X{pb}", c) for c in range(8)], writes=[("XT", n)])

        def evac(dst, ps, pk, dk):
            if cnt["ev"] % 2 == 0:
                s.act(lambda e: e.copy(dst, ps), [pk], [dk])
            else:
                s.dve(lambda e: e.tensor_copy(dst, ps), [pk], [dk])
            cnt["ev"] += 1

        def groups(n):
            pb = n % 2
            hk = [(f"hT{pb}", c) for c in range(8)]
            h = hT[pb]
            tcs = slice(n * TQ, (n + 1) * TQ)
            out = []
            fm = [("qk", 0, 0, 4), ("qk", 512, 4, 4), ("qb", 1536, 0, 4), ("fb", 2048, 0, 4), ("gb", 3072, 0, 4)]
            for nm, col0, slot0, nchunk in fm:
                st = st_qk[pb] if nm == "qk" else st_f[nm][pb]
                stkey = f"st_{nm}{pb}"
                for j in range(nchunk):
                    def g_(nm=nm, col0=col0, slot0=slot0, j=j, st=st, stkey=stkey):
                        ps, pk = mm_ps[cnt["mi"] % 4], f"mm_ps{cnt['mi'] % 4}"
                        cnt["mi"] += 1
                        cc = col0 + j * 128
                        mm_group(s, ps[:, 0:TQ], [(w[:, kk, cc:cc + 128], h[:, kk, :]) for kk in range(8)], hk, [pk])
                        evac(st[:, slot0 + j, :], ps[:, 0:TQ], pk, (stkey, slot0 + j))
                        if nm == "qk" and slot0 + j == 7:
                            s.dma(dr["QK"].rearrange("(c p) t -> p c t", p=128)[:, :, tcs], st[:], reads=[(stkey, x) for x in range(8)],
                                  writes=[("QK", n)])
                        if nm != "qk" and j == 3:
                            dn = {"qb": "QB", "fb": "FB", "gb": "GB"}[nm]
                            s.dma(dr[dn].rearrange("(c p) t -> p c t", p=128)[:, :, tcs], st[:], reads=[(stkey, x) for x in range(4)],
                                  writes=[(dn, n)])
                    out.append(g_)
            for nm, col0, dn in (("va", 1024, "VA"), ("ib", 2560, "IB")):
                st = st_v[nm][pb]
                stkey = f"st_{nm}{pb}"
                for a_ in range(2):
                    def g_(nm=nm, col0=col0, dn=dn, a_=a_, st=st, stkey=stkey):
                        ps, pk = mm_ps[cnt["mi"] % 4], f"mm_ps{cnt['mi'] % 4}"
                        cnt["mi"] += 1
                        mm_group(s, ps[:], [(h[:, kk, a_ * 128:(a_ + 1) * 128], w[:, kk, col0:col0 + 512]) for kk in range(8)], hk, [pk])
                        evac(st[:, a_, :], ps[:], pk, (stkey, a_))
                        if a_ == 1:
                            s.dma(dr[dn][tcs, :].rearrange("(a p) f -> p a f", p=128), st[:], reads=[(stkey, 0), (stkey, 1)], writes=[(dn, n)])
                    out.append(g_)
            return out

        loadx(0)
        pre(0)
        for n in range(NQ):
            if n + 1 < NQ:
                loadx(n + 1)
            gs = groups(n)
            for g_ in gs[:12]:
                g_()
            if n + 1 < NQ:
                pre(n + 1)
            for g_ in gs[12:]:
                g_()
        s.emit_phase()


def norm_block_multi(k, X, xkeys, hT, hkey, sq, sqkey, G, SH, tmps, ss_ps, sskey, rstd, tag):
    s = k.s
    n = X.shape[2]
    for c in range(8):
        s.act(lambda e, c=c: e.activation(sq[:, c, :], X[:, c, :], AF.Square), [xkeys[c]], [(sqkey, c)])
    mm_group(s, ss_ps[:, :n], [(k.ones_bf[:], sq[:, c, :]) for c in range(8)], [(sqkey, c) for c in range(8)], [sskey])
    s.act(lambda e: e.activation(rstd[:, :n], ss_ps[:, :n], AF.Sqrt, bias=k.eps_col[:, 0:1], scale=1.0 / D), [sskey], [tag + "rs0"])
    s.dve(lambda e: e.reciprocal(rstd[:, :n], rstd[:, :n]), [tag + "rs0"], [tag + "rstd"])
    if hT is None:
        return
    for c in range(8):
        tm = tmps[c % 2]
        tk = f"{tag}tmp{c % 2}"
        s.dve(lambda e, c=c, tm=tm: e.scalar_tensor_tensor(tm[:, :n], X[:, c, :], G[:, c:c + 1], rstd[:, :n], ALU.mult, ALU.mult),
              [xkeys[c], tag + "rstd"], [tk])
        s.act(lambda e, c=c, tm=tm: e.activation(hT[:, c, :], tm[:, :n], AF.Identity, bias=SH[:, c:c + 1], scale=1.0),
              [tk], [(hkey, c)])


def phase_attn(k):
    nc, s, dr = k.nc, k.s, k.dr
    with ExitStack() as es:
        T = lambda name, shape, dt=F32: es.enter_context(nc.sbuf_tensor(f"p{k.pid}_{name}", shape, dt))
        P = lambda name, shape, dt=F32: es.enter_context(nc.psum_tensor(f"p{k.pid}_{name}", shape, dt))
        Um = T("Um", [128, 128], BF16)
        mask = T("mask", [128, 128])
        s.pool(lambda e: e.memset(mask[:], 1.0), [], ["mask"])
        s.pool(lambda e: e.affine_select(mask[:], mask[:], pattern=[[-1, 128]], compare_op=ALU.is_gt, fill=0.0, base=0,
                                         channel_multiplier=1), ["mask"], ["mask"])
        s.pool(lambda e: e.tensor_copy(Um[:], mask[:]), ["mask"], ["Um"])
        maskb = T("maskb", [128, 128], BF16)
        s.pool(lambda e: e.memset(mask[:], 1.0), ["Um"], ["mask"])
        s.pool(lambda e: e.affine_select(mask[:], mask[:], pattern=[[1, 128]], compare_op=ALU.is_gt, fill=0.0, base=0,
                                         channel_multiplier=-1), ["mask"], ["mask"])
        s.pool(lambda e: e.tensor_copy(maskb[:], mask[:]), ["mask"], ["maskb"])
        qT = [T(f"qT{i}", [128, S], BF16) for i in range(2)]
        kT = [T(f"kT{i}", [128, S], BF16) for i in range(2)]
        V2 = [T(f"V2{i}", [128, 32, 128], BF16) for i in range(2)]
        Et = [T(f"Et{i}", [128, 512]) for i in range(2)]
        Lt = [T(f"Lt{i}", [128, 512], BF16) for i in range(3)]
        LKt = [T(f"LKt{i}", [128, 512], BF16) for i in range(2)]
        At = [T(f"At{i}", [128, 512]) for i in range(2)]
        Wt = [T(f"Wt{i}", [128, 512], BF16) for i in range(2)]
        Ss = [T(f"Ss{i}", [128, 512], BF16) for i in range(3)]
        acc = [T(f"acc{i}", [128, 4, 128]) for i in range(2)]
        mst = [T(f"mst{i}", [128, 512], BF16) for i in range(2)]
        z_ps = [P(f"z_ps{i}", [128, 512]) for i in range(2)]
        tri_ps = [P(f"tri_ps{i}", [128, 512]) for i in range(3)]
        pv_ps = [P(f"pv_ps{i}", [128, 4, 64]) for i in range(2)]
        tr_ps = P("atr_ps", [128, 512])
        stZ, stA, stB0, stB = [], [], [], []
        it = 0
        rnd = 0

        def keep_warm(nrep):
            def fn(e):
                ins = None
                for _ in range(nrep):
                    ins = e.matmul(tr_ps[:], k.ident_bf[:], k.warm_src[:], start=True, stop=True)
                return ins
            s.pe(fn, [], ["atr_ps"])
        for hp in range(4):
            hb = hp % 2
            for G in range(8):
                ab = G % 2
                for hl in range(2):
                    rb = rnd % 2
                    rnd += 1
                    for j in range(4 * G + 3, -1, -1):
                        first = (j == 4 * G + 3)
                        last = (j == 0)
                        ib = it % 2
                        i3 = it % 3
                        it += 1

                        def Z_(hp=hp, hb=hb, G=G, hl=hl, j=j, ib=ib, first=first):
                            q, kk_, v = qT[hb], kT[hb], V2[hb]
                            qk_, kk_key, vk = f"qT{hb}", f"kT{hb}", f"V2{hb}"
                            if first and G == 0 and hl == 0:
                                s.dma(q[:], dr["QK"][hp * 128:(hp + 1) * 128, :], writes=[qk_])
                                s.dma(kk_[:], dr["QK"][512 + hp * 128:512 + (hp + 1) * 128, :], writes=[kk_key])
                                s.dma(v[:], dr["VA"][:, hp * 128:(hp + 1) * 128].rearrange("(n p) f -> p n f", p=128), writes=[vk])
                            hs = slice(64 * hl, 64 * hl + 64)
                            c0 = max(j - 4 * G, 0) * 128
                            zp = z_ps[ib]
                            s.pe(lambda e: e.matmul(zp[:, c0:512], kk_[hs, j * 128:(j + 1) * 128], q[hs, G * 512 + c0:(G + 1) * 512],
                                                    start=True, stop=True), [qk_, kk_key], [f"z_ps{ib}"])
                            keep_warm(2)

                        def A_(hp=hp, hb=hb, G=G, hl=hl, j=j, ib=ib, i3=i3, rb=rb, first=first):
                            q, kk_, v = qT[hb], kT[hb], V2[hb]
                            qk_, kk_key, vk = f"qT{hb}", f"kT{hb}", f"V2{hb}"
                            so, sn = (j + 1) % 3, j % 3
                            Sk, Snk = f"Ss{so}", f"Ss{sn}"
                            Sm, Sn = Ss[so], Ss[sn]
                            hs = slice(64 * hl, 64 * hl + 64)
                            jl = j - 4 * G
                            c0 = max(jl, 0) * 128
                            cols = slice(c0, 512)
                            zp, tp = z_ps[ib], tri_ps[i3]
                            E, L, LK = Et[ib], Lt[i3], LKt[ib]
                            zk, tk = f"z_ps{ib}", f"tri_ps{i3}"
                            Ek, Lk, LKk = f"Et{ib}", f"Lt{i3}", f"LKt{ib}"
                            s.act(lambda e: e.activation(E[:, cols], zp[:, cols], AF.Exp, scale=-0.125), [zk], [Ek])
                            s.act(lambda e: e.activation(L[:, cols], E[:, cols], AF.Ln, bias=1.0), [Ek], [Lk])
                            s.dve(lambda e: e.scalar_tensor_tensor(LK[:, cols], zp[:, cols], 0.125, L[:, cols], ALU.mult, ALU.add),
                                  [zk, Lk], [LKk])
                            if jl >= 0:
                                s.dve(lambda e: e.tensor_tensor(LK[:, c0:c0 + 128], LK[:, c0:c0 + 128], maskb[:], ALU.mult),
                                      [LKk, "maskb"], [LKk])
                            cc0 = c0 + 128 if jl >= 0 else 0

                            def trif(e):
                                e.matmul(tp[:, cols], Um[:], LK[:, cols], start=True, stop=False)
                                ins = e.matmul(tp[:, cols], k.ident_bf[:], L[:, cols], start=False, stop=(first or cc0 >= 512))
                                if (not first) and cc0 < 512:
                                    ins = e.matmul(tp[:, cc0:512], k.ones_bf[:], Sm[:, cc0:512], start=False, stop=True)
                                return ins
                            if first:
                                s.pe(trif, [LKk, Lk, "Um", "ident_bf"], [tk])
                            else:
                                s.pe(trif, [LKk, Lk, "Um", "ident_bf", Sk, (Sk, 0)], [tk])
                            if j > 0:
                                if jl >= 0:
                                    s.dve(lambda e: e.tensor_copy(Sn[:, c0:c0 + 128], LK[:, c0:c0 + 128]), [LKk], [(Snk, 0)])
                                    if c0 + 128 < 512:
                                        s.dve(lambda e: e.tensor_tensor(Sn[:, c0 + 128:512], Sm[:, c0 + 128:512], LK[:, c0 + 128:512], ALU.add),
                                              [Sk, (Sk, 0), LKk], [Snk])
                                else:
                                    s.dve(lambda e: e.tensor_tensor(Sn[:, cols], Sm[:, cols], LK[:, cols], ALU.add), [Sk, (Sk, 0), LKk], [Snk])

                        def B0_(G=G, j=j, ib=ib, i3=i3):
                            c0 = max(j - 4 * G, 0) * 128
                            cols = slice(c0, 512)
                            pass

                        def B_(hp=hp, hb=hb, G=G, hl=hl, j=j, ib=ib, i3=i3, rb=rb, first=first, last=last, ab=ab):
                            v, vk = V2[hb], f"V2{hb}"
                            hs = slice(64 * hl, 64 * hl + 64)
                            jl = j - 4 * G
                            c0 = max(jl, 0) * 128
                            cols = slice(c0, 512)
                            b0 = max(jl, 0)
                            pp = pv_ps[rb]
                            A, W = At[ib], Wt[ib]
                            pk = f"pv_ps{rb}"
                            Ak, Wk = f"At{ib}", f"Wt{ib}"
                            ac = acc[ab]
                            tp = tri_ps[i3]
                            s.act(lambda e: e.activation(W[:, cols], tp[:, cols], AF.Exp, scale=-1.0), [f"tri_ps{i3}"], [Wk])
                            if jl >= 0:
                                s.dve(lambda e: e.tensor_tensor(W[:, c0:c0 + 128], W[:, c0:c0 + 128], maskb[:], ALU.mult),
                                      [Wk, "maskb"], [Wk])
                            def pvf(e):
                                ins = None
                                for b in range(3, b0 - 1, -1):
                                    ins = e.matmul(pp[:, b, :], W[:, b * 128:(b + 1) * 128], v[:, j, hs],
                                                   start=(first and b == 3), stop=last, skip_group_check=True)
                                return ins
                            s.pe(pvf, [Wk, vk], [pk])
                            if last:
                                s.dve(lambda e: e.tensor_copy(ac[:, :, hs], pp[:]), [pk], [(f"acc{ab}", hl)])
                                if hl == 1:
                                    def trf(e):
                                        ins = None
                                        for b in range(4):
                                            ins = e.transpose(tr_ps[:, b * 128:(b + 1) * 128], ac[:, b, :], k.ident[:])
                                        return ins
                                    s.pe(trf, [(f"acc{ab}", 0), (f"acc{ab}", 1), "ident"], ["atr_ps"])
                                    ms = mst[G % 2]
                                    s.dve(lambda e: e.tensor_copy(ms[:], tr_ps[:]), ["atr_ps"], [f"mst{G % 2}"])
                                    s.dma(dr["MT"][hp * 128:(hp + 1) * 128, G * 512:(G + 1) * 512], ms[:], reads=[f"mst{G % 2}"],
                                          writes=[("MT", hp, G)])
                        stZ.append(Z_)
                        stA.append(A_)
                        stB0.append(B0_)
                        stB.append(B_)
        n = len(stA)
        for t in range(-3, n):
            if 0 <= t + 3 < n:
                stZ[t + 3]()
            if 0 <= t < n:
                stB0[t]()
            if 0 <= t + 2 < n:
                stA[t + 2]()
            if 0 <= t < n:
                stB[t]()
        s.emit_phase()


def phase_hgrn(k):
    nc, s, dr = k.nc, k.s, k.dr
    with ExitStack() as es:
        T = lambda name, shape, dt=F32: es.enter_context(nc.sbuf_tensor(f"p{k.pid}_{name}", shape, dt))
        P = lambda name, shape, dt=F32: es.enter_context(nc.psum_tensor(f"p{k.pid}_{name}", shape, dt))
        rmask = T("rmask", [128, S])
        s.pool(lambda e: e.memset(rmask[:], 1.0), [], ["rmask"])
        s.pool(lambda e: e.memset(rmask[:].rearrange("p (c t) -> p c t", t=64)[:, :, 0:1], 0.0), ["rmask"], ["rmask"])
        mle = T("mle", [128, 64])
        s.pool(lambda e: e.memset(mle[:], 1.0), [], ["mle"])
        for half in range(2):
            s.pool(lambda e, half=half: e.affine_select(mle[64 * half:64 * half + 64, :], mle[64 * half:64 * half + 64, :],
                                                        pattern=[[1, 64]], compare_op=ALU.is_ge, fill=0.0, base=0,
                                                        channel_multiplier=-1), ["mle"], ["mle"])
        a1 = T("a1", [128, S]); a2 = T("a2", [128, S]); a3 = T("a3", [128, S]); a4 = T("a4", [128, S])
        kh = T("kh", [128, S], BF16)
        qt = [T(f"qt{i}", [128, S], BF16) for i in range(2)]
        kt = [T(f"kt{i}", [128, S], BF16) for i in range(2)]
        khT = [T(f"khT{i}", [128, 32, 128], BF16) for i in range(2)]
        ibt = [T(f"ibt{i}", [128, 32, 128], BF16) for i in range(2)]
        elast = [T(f"elast{i}", [128, 64]) for i in range(2)]
        St = [T(f"St{i}", [128, 128]) for i in range(2)]
        Sb = [[T(f"Sb{i}_{j}", [128, 128], BF16) for j in range(2)] for i in range(2)]
        sT = [T(f"sT{i}", [128, 64], BF16) for i in range(2)]
        Ob = [[T(f"Ob{i}_{pb}", [128, 512]) for pb in range(2)] for i in range(2)]
        gbk = [T(f"gbk{i}", [128, 512]) for i in range(2)]
        sqo = T("sqo", [128, 512], BF16)
        rs = T("hrs", [128, 512])
        t1 = T("ht1", [128, 512])
        obf = [T(f"obf{i}", [128, 512], BF16) for i in range(2)]
        sc_ps = [P(f"sc_ps{i}", [128, 64]) for i in range(2)]
        o_ps = [P(f"o_ps{i}", [128, 64]) for i in range(2)]
        kv_ps = [P(f"kv_ps{i}", [128, 128]) for i in range(2)]
        ms_ps = P("ms_ps", [128, 512])
        ktr_ps = P("ktr_ps", [128, 512], BF16)
        a4v = a4[:].rearrange("p (c t) -> p c t", t=64)
        fin = 0
        for hg in range(2):
            for i in range(2):
                hd = 2 * hg + i
                rows = slice(hd * 128, (hd + 1) * 128)
                s.dma(a1[:], dr["FB"][rows, :], writes=["a1"])
                s.dma(a2[:], dr["QB"][rows, :], writes=["a2"])
                s.dma(ibt[i][:], dr["IB"][:, rows].rearrange("(n p) f -> p n f", p=128), writes=[f"ibt{i}"])
                s.act(lambda e: e.activation(a1[:], a1[:], AF.Sigmoid), ["a1"], ["a1"])
                s.dve(lambda e, hd=hd: e.tensor_scalar(a1[:], a1[:], k.oml[:, hd:hd + 1], k.lb[:, hd:hd + 1], ALU.mult, ALU.add), ["a1"], ["a1"])
                s.dve(lambda e: e.tensor_scalar(a3[:], a1[:], -1.0, 1.0, ALU.mult, ALU.add), ["a1"], ["a3"])
                s.act(lambda e: e.activation(a1[:], a1[:], AF.Ln), ["a1"], ["a1"])
                s.dve(lambda e: e.tensor_tensor_scan(a4[:], rmask[:], a1[:], 0.0, ALU.mult, ALU.add), ["a1", "rmask"], ["a4"])
                s.act(lambda e: e.activation(a2[:], a2[:], AF.Silu), ["a2"], ["a2"])
                s.act(lambda e: e.activation(a1[:], a4[:], AF.Exp), ["a4"], ["a1"])
                s.dve(lambda e, i=i: e.tensor_tensor(qt[i][:], a2[:], a1[:], ALU.mult), ["a1", "a2"], [f"qt{i}"])
                s.act(lambda e: e.activation(a1[:], a4[:], AF.Exp, scale=-1.0), ["a4", f"qt{i}"], ["a1"])
                s.dve(lambda e, i=i: e.tensor_tensor(kt[i][:], a3[:], a1[:], ALU.mult), ["a1", "a3"], [f"kt{i}"])
                s.act(lambda e, i=i: e.activation(elast[i][:], a4v[:, :, 63], AF.Exp), ["a4"], [f"elast{i}"])
                s.dve(lambda e: e.tensor_tensor(a1[:].rearrange("p (c t) -> p c t", t=64), a4v[:, :, 63:64].to_broadcast([128, 64, 64]),
                                                a4v, ALU.subtract), ["a4", f"kt{i}"], ["a1"])
                s.act(lambda e: e.activation(a1[:], a1[:], AF.Exp), ["a1"], ["a1"])
                s.dve(lambda e: e.tensor_tensor(kh[:], a3[:], a1[:], ALU.mult), ["a1", "a3"], ["kh"])
                for n4 in range(8):
                    def trf(e, n4=n4):
                        ins = None
                        for a in range(4):
                            n = n4 * 4 + a
                            ins = e.transpose(ktr_ps[:, a * 128:(a + 1) * 128], kh[:, n * 128:(n + 1) * 128], k.ident_bf[:])
                        return ins
                    s.pe(trf, ["kh", "ident_bf"], ["ktr_ps"])
                    s.act(lambda e, i=i, n4=n4: e.copy(khT[i][:, n4 * 4:(n4 + 1) * 4, :], ktr_ps[:].rearrange("p (a f) -> p a f", f=128)),
                          ["ktr_ps"], [(f"khT{i}", n4)])
            pend = [None]
            for c in range(64):
                n, half = c // 2, c % 2
                pbs = slice(64 * half, 64 * half + 64)
                cs_ = slice(c * 64, (c + 1) * 64)
                kb, cl = c // 8, c % 8
                for i in range(2):
                    hd = 2 * hg + i
                    O = Ob[i][kb % 2]
                    Ok = f"Ob{i}_{kb % 2}"
                    sbn, sbo = Sb[i][c % 2], Sb[i][(c + 1) % 2]
                    sbnk, sbok = f"Sb{i}_{c % 2}", f"Sb{i}_{(c + 1) % 2}"
                    s.pe(lambda e, i=i, pbs=pbs, cs_=cs_: e.matmul(sc_ps[i][pbs, :], kt[i][:, cs_], qt[i][:, cs_], start=True, stop=True),
                         [f"kt{i}", f"qt{i}"], [f"sc_ps{i}"])
                    if c < 63:
                        s.pe(lambda e, i=i, pbs=pbs, n=n: e.matmul(kv_ps[i][:], khT[i][pbs, n, :], ibt[i][pbs, n, :], start=True, stop=True),
                             [(f"khT{i}", n // 4), f"ibt{i}"], [f"kv_ps{i}"])
                    s.dve(lambda e, i=i, pbs=pbs: e.tensor_tensor(sT[i][pbs, :], sc_ps[i][pbs, :], mle[pbs, :], ALU.mult),
                          [f"sc_ps{i}", "mle"], [f"sT{i}"])
                    if c < 63:
                        if c == 0:
                            s.dve(lambda e, i=i: e.tensor_copy(St[i][:], kv_ps[i][:]), [f"kv_ps{i}"], [f"St{i}"])
                        else:
                            s.dve(lambda e, i=i, c=c: e.scalar_tensor_tensor(St[i][:], St[i][:], elast[i][:, c:c + 1], kv_ps[i][:],
                                                                              ALU.mult, ALU.add), [f"kv_ps{i}", f"St{i}", f"elast{i}"], [f"St{i}"])
                        s.act(lambda e, i=i, sbn=sbn: e.copy(sbn[:], St[i][:]), [f"St{i}"], [sbnk])
                    def late_(i=i, pbs=pbs, n=n, c=c, cs_=cs_, sbo=sbo, sbok=sbok, O=O, Ok=Ok, cl=cl):
                        pairs = [(ibt[i][pbs, n, :], sT[i][pbs, :])]
                        rd = [f"ibt{i}", f"sT{i}", f"qt{i}"]
                        if c > 0:
                            pairs.append((sbo[:], qt[i][:, cs_]))
                            rd.append(sbok)
                        mm_group(s, o_ps[i][:], pairs, rd, [f"o_ps{i}"])
                        s.act(lambda e: e.copy(O[:, cl * 64:(cl + 1) * 64], o_ps[i][:]), [f"o_ps{i}"], [(Ok, cl)])
                    if pend[0] is not None:
                        pend[0]()
                    pend[0] = late_
                    if cl == 7:
                        pend[0]()
                        pend[0] = None
                    if cl == 7:
                        rows = slice(hd * 128, (hd + 1) * 128)
                        tcols = slice(kb * 512, (kb + 1) * 512)
                        fb_ = fin % 2
                        fin += 1
                        Oks = [(Ok, x) for x in range(8)]
                        s.dma(gbk[fb_][:], dr["GB"][rows, tcols], writes=[f"gbk{fb_}"])
                        s.act(lambda e, O=O: e.activation(sqo[:], O[:], AF.Square), Oks, ["sqo"])
                        s.pe(lambda e: e.matmul(ms_ps[:], k.ones_bf[:], sqo[:], start=True, stop=True), ["sqo"], ["ms_ps"])
                        s.act(lambda e: e.activation(rs[:], ms_ps[:], AF.Sqrt, bias=k.eps_col[:, 0:1], scale=1.0 / 128), ["ms_ps"], ["hrs"])
                        s.dve(lambda e: e.reciprocal(rs[:], rs[:]), ["hrs"], ["hrs"])
                        s.act(lambda e, fb_=fb_: e.activation(gbk[fb_][:], gbk[fb_][:], AF.Silu), [f"gbk{fb_}"], [f"gbk{fb_}"])
                        s.dve(lambda e, O=O, hd=hd: e.scalar_tensor_tensor(t1[:], O[:], k.hgg[:, hd:hd + 1], rs[:], ALU.mult, ALU.mult),
                              Oks + ["hrs"], ["ht1"])
                        s.dve(lambda e, fb_=fb_: e.tensor_tensor(obf[fb_][:], t1[:], gbk[fb_][:], ALU.mult), ["ht1", f"gbk{fb_}"], [f"obf{fb_}"])
                        s.dma(dr["MT"][512 + hd * 128:512 + (hd + 1) * 128, tcols], obf[fb_][:], reads=[f"obf{fb_}"], writes=[("MTB", hd, kb)])
        s.emit_phase()


def phase_b1(k, layer):
    nc, s, din, dr = k.nc, k.s, k.din, k.dr
    TQ = 256
    NQ = S // TQ
    with ExitStack() as es:
        T = lambda name, shape, dt=F32: es.enter_context(nc.sbuf_tensor(f"p{k.pid}_{name}", shape, dt))
        P = lambda name, shape, dt=F32: es.enter_context(nc.psum_tensor(f"p{k.pid}_{name}", shape, dt))
        ncol = 1024 if layer == 0 else 2048
        w = T("w_b1", [128, 8, ncol], BF16)
        load_weight_bf16(k, w, din["hy_w_out"][0] if layer == 0 else din["s5_w_glu"][0], 8, ncol, "w")
        WK = wkeys("w", 8, ncol)
        load_weight_bf16(k, k.pre["ffn_w1"], din["ffn_w_in"][layer], 8, 2 * DFF, "pw1")
        load_weight_bf16(k, k.pre["ffn_w2"], din["ffn_w_out"][layer], 22, D, "pw2")
        src = (dr["MT"] if layer == 0 else dr["YG"]).rearrange("(c p) t -> p c t", p=128)
        XTv = dr["XT"].rearrange("(c p) t -> p c t", p=128)
        g = k.mod[:, layer, 16:24]
        mT = [T(f"mT{i}", [128, 8, TQ], BF16) for i in range(2)]
        X = [T(f"X{i}", [128, 8, TQ]) for i in range(2)]
        sg = [T(f"sg{i}", [128, TQ]) for i in range(2)]
        mix = [T(f"mix{i}", [128, TQ]) for i in range(2)]
        v_ps = [P(f"v_ps{i}", [128, 512]) for i in range(2)]
        g_ps = [P(f"g_ps{i}", [128, 512]) for i in range(2)]

        def load(n):
            pb = n % 2
            tc_ = slice(n * TQ, (n + 1) * TQ)
            s.dma(mT[pb][:], src[:, :, tc_], writes=[f"mT{pb}"])
            s.dma(X[pb][:], XTv[:, :, tc_], writes=[(f"X{pb}", c) for c in range(8)])
        it = 0
        load(0)
        for n in range(NQ):
            pb = n % 2
            tc_ = slice(n * TQ, (n + 1) * TQ)
            if n + 1 < NQ:
                load(n + 1)
            for dc in range(8):
                ib = it % 2
                it += 1
                vp, gp = v_ps[ib][:, 0:TQ], g_ps[ib][:, 0:TQ]
                mm_group(s, vp, [(w[:, kk, dc * 128:(dc + 1) * 128], mT[pb][:, kk, :]) for kk in range(8)],
                         WK + [f"mT{pb}"], [f"v_ps{ib}"])
                if layer == 0:
                    s.dve(lambda e, pb=pb, dc=dc, vp=vp: e.scalar_tensor_tensor(X[pb][:, dc, :], vp, g[:, dc:dc + 1], X[pb][:, dc, :],
                                                                                ALU.mult, ALU.add), [f"v_ps{ib}", (f"X{pb}", dc)], [(f"X{pb}", dc)])
                else:
                    mm_group(s, gp, [(w[:, kk, 1024 + dc * 128:1024 + (dc + 1) * 128], mT[pb][:, kk, :]) for kk in range(8)],
                             WK + [f"mT{pb}"], [f"g_ps{ib}"])
                    s.act(lambda e, ib=ib, gp=gp: e.activation(sg[ib][:], gp, AF.Sigmoid), [f"g_ps{ib}"], [f"sg{ib}"])
                    s.dve(lambda e, ib=ib, vp=vp: e.tensor_tensor(mix[ib][:], vp, sg[ib][:], ALU.mult), [f"v_ps{ib}", f"sg{ib}"], [f"mix{ib}"])
                    s.dve(lambda e, pb=pb, dc=dc, ib=ib: e.scalar_tensor_tensor(X[pb][:, dc, :], mix[ib][:], g[:, dc:dc + 1], X[pb][:, dc, :],
                                                                                ALU.mult, ALU.add), [f"mix{ib}", (f"X{pb}", dc)], [(f"X{pb}", dc)])
            s.dma(XTv[:, :, tc_], X[pb][:], reads=[(f"X{pb}", c) for c in range(8)], writes=[("XT", n)])
        s.emit_phase()


def phase_ffn(k, layer, final):
    nc, s, din, dr = k.nc, k.s, k.din, k.dr
    TQ = 256
    NQ = S // TQ
    with ExitStack() as es:
        T = lambda name, shape, dt=F32: es.enter_context(nc.sbuf_tensor(f"p{k.pid}_{name}", shape, dt))
        P = lambda name, shape, dt=F32: es.enter_context(nc.psum_tensor(f"p{k.pid}_{name}", shape, dt))
        w1, w2 = k.pre["ffn_w1"], k.pre["ffn_w2"]
        W1K, W2K = [], []

        def w1k(col):
            return W1K
        NX = 3 if final else 2
        X = [T(f"X{i}", [128, 8, TQ]) for i in range(NX)]
        hT = [T(f"hT{i}", [128, 8, TQ], BF16) for i in range(2)]
        sq = T("sq", [128, 8, TQ], BF16)
        a = T("a", [128, 22, TQ], BF16)
        tmps = [T(f"tmp{i}", [128, TQ]) for i in range(2)]
        rstd = [T(f"rstd{i}", [128, TQ]) for i in range(2)]
        rstdf = T("rstdf", [128, TQ])
        sg = [T(f"sg{i}", [128, TQ]) for i in range(2)]
        ys = [T(f"ys{i}", [128, 1024]) for i in range(2)] if final else None
        mmb = [P(f"mmb{i}", [128, 512]) for i in range(4)]
        ob = [P(f"ob{i}", [128, 512]) for i in range(2)]
        ssb = P("ssb", [128, 512])
        trp = [P(f"trp{i}", [128, 512]) for i in range(1)] * 2 if final else None
        G = k.gf[:, layer, :]
        SH = k.mod[:, layer, 24:32]
        gate = k.mod[:, layer, 40:48]
        XTv = dr["XT"].rearrange("(c p) t -> p c t", p=128)
        cnt = {"mi": 0, "oi": 0, "yi": 0, "ti": 0}

        def load(n):
            xb = n % NX
            s.dma(X[xb][:], XTv[:, :, n * TQ:(n + 1) * TQ], writes=[(f"X{xb}", c) for c in range(8)])

        def norm(n):
            pb = n % 2
            xb = n % NX
            Xk = [(f"X{xb}", c) for c in range(8)]
            ssp = ssb[:, 0:TQ]
            for c in range(8):
                s.act(lambda e, c=c: e.activation(sq[:, c, :], X[xb][:, c, :], AF.Square), [Xk[c]], [("sq", c)])
            mm_group(s, ssp, [(k.ones_bf[:], sq[:, c, :]) for c in range(8)], [("sq", c) for c in range(8)], ["ssb"])
            s.act(lambda e: e.activation(rstd[pb][:], ssp, AF.Sqrt, bias=k.eps_col[:, 0:1], scale=1.0 / D), ["ssb"], [f"rs0{pb}"])
            s.dve(lambda e: e.reciprocal(rstd[pb][:], rstd[pb][:]), [f"rs0{pb}"], [f"rstd{pb}"])
            for c in range(8):
                tm, tk = tmps[c % 2], f"tmp{c % 2}"
                s.dve(lambda e, c=c, tm=tm: e.scalar_tensor_tensor(tm[:], X[xb][:, c, :], G[:, c:c + 1], rstd[pb][:], ALU.mult, ALU.mult),
                      [Xk[c], f"rstd{pb}"], [tk])
                s.act(lambda e, c=c, tm=tm: e.activation(hT[pb][:, c, :], tm[:], AF.Identity, bias=SH[:, c:c + 1], scale=1.0),
                      [tk], [(f"hT{pb}", c)])

        def gu(n):
            pb = n % 2
            hk = [(f"hT{pb}", c) for c in range(8)]
            h = hT[pb]
            out = []
            for j in range(22):
                out.append(lambda j=j: gu1(j, h, hk))
            return out

        def gu1(j, h, hk):
            if True:
                m0, m1 = cnt["mi"] % 4, (cnt["mi"] + 1) % 4
                cnt["mi"] += 2
                gp, gk = mmb[m0][:, 0:TQ], ("mmb", m0)
                up, uk = mmb[m1][:, 0:TQ], ("mmb", m1)
                mm_group(s, gp, [(w1[:, kk, j * 128:(j + 1) * 128], h[:, kk, :]) for kk in range(8)], w1k(j * 128) + hk, [gk])
                mm_group(s, up, [(w1[:, kk, DFF + j * 128:DFF + (j + 1) * 128], h[:, kk, :]) for kk in range(8)], w1k(DFF + j * 128) + hk, [uk])
                sb = j % 2
                s.act(lambda e, sb=sb, gp=gp: e.activation(sg[sb][:], gp, AF.Silu), [gk], [f"sg{sb}"])
                s.dve(lambda e, sb=sb, up=up, j=j: e.tensor_tensor(a[:, j, :], up, sg[sb][:], ALU.mult), [uk, f"sg{sb}"], [("a", j)])

        def down(n):
            pb = n % NX
            ak = [("a", j) for j in range(22)]
            for dc in range(8):
                o0 = cnt["oi"] % 2
                cnt["oi"] += 1
                op_, ok = ob[o0][:, 0:TQ], ("ob", o0)
                mm_group(s, op_, [(w2[:, j, dc * 128:(dc + 1) * 128], a[:, j, :]) for j in range(22)], W2K + ak, [ok])
                s.dve(lambda e, dc=dc, op_=op_: e.scalar_tensor_tensor(X[pb][:, dc, :], op_, gate[:, dc:dc + 1], X[pb][:, dc, :], ALU.mult, ALU.add),
                      [ok, (f"X{pb}", dc)], [(f"X{pb}", dc)])

        def finish(n):
            pb = n % NX
            Xk = [(f"X{pb}", c) for c in range(8)]
            s.dma(XTv[:, :, n * TQ:(n + 1) * TQ], X[pb][:], reads=Xk, writes=[("XT", n)])

        def fin_a(n):
            pb = n % NX
            rb = n % 2
            Xk = [(f"X{pb}", c) for c in range(8)]
            ssp = ssb[:, 0:TQ]
            for c in range(8):
                s.act(lambda e, c=c: e.activation(sq[:, c, :], X[pb][:, c, :], AF.Square), [Xk[c]], [("sq", c)])
            mm_group(s, ssp, [(k.ones_bf[:], sq[:, c, :]) for c in range(8)], [("sq", c) for c in range(8)], ["ssb"])
            s.act(lambda e: e.activation(rstdf[:], ssp, AF.Sqrt, bias=k.eps_col[:, 0:1], scale=1.0 / D), ["ssb"], ["rsf0"])
            s.dve(lambda e: e.reciprocal(rstdf[:], rstdf[:]), ["rsf0"], ["rstdf"])
            for c in range(8):
                s.dve(lambda e, c=c: e.scalar_tensor_tensor(X[pb][:, c, :], X[pb][:, c, :], k.fng[:, c:c + 1], rstdf[:], ALU.mult, ALU.mult),
                      [Xk[c], "rstdf"], [Xk[c]])

        def fin_b(n):
            pb = n % NX
            Xk = [(f"X{pb}", c) for c in range(8)]
            for a2 in range(TQ // 128):
                y, yk = ys[cnt["yi"] % 2], f"ys{cnt['yi'] % 2}"
                cnt["yi"] += 1
                for hh in range(2):
                    tp, tk = trp[0], "trp0"
                    cnt["ti"] += 1

                    def trf(e, tp=tp, hh=hh, a2=a2):
                        ins = None
                        for cc in range(4):
                            c = hh * 4 + cc
                            ins = e.transpose(tp[:, cc * 128:(cc + 1) * 128], X[pb][:, c, a2 * 128:(a2 + 1) * 128], k.ident[:])
                        return ins
                    s.pe(trf, Xk + ["ident"], [tk])
                    if hh == 0:
                        s.act(lambda e, y=y, tp=tp, hh=hh: e.copy(y[:, hh * 512:(hh + 1) * 512], tp[:]), [tk], [(yk, hh)])
                    else:
                        s.dve(lambda e, y=y, tp=tp, hh=hh: e.tensor_copy(y[:, hh * 512:(hh + 1) * 512], tp[:]), [tk], [(yk, hh)])
                r0 = n * TQ + a2 * 128
                s.dma(k.out[r0:r0 + 128, :], y[:], reads=[(yk, 0), (yk, 1)], writes=[("out", r0)])

        load(0)
        norm(0)
        for n in range(NQ):
            if n + 1 < NQ:
                load(n + 1)
            gs = gu(n)
            if final and n > 0:
                for g_ in gs[:5]:
                    g_()
                fin_a(n - 1)
                for g_ in gs[5:14]:
                    g_()
                fin_b(n - 1)
                for g_ in gs[14:]:
                    g_()
            else:
                for g_ in gs:
                    g_()
            if n + 1 < NQ:
                norm(n + 1)
            down(n)
            if not final:
                finish(n)
        if final:
            fin_a(NQ - 1)
            fin_b(NQ - 1)
        s.emit_phase()


def norm256(k, n, X, hT, sq, tmps, rstd, ssb, G, SH):
    s = k.s
    pb = n % 2
    Xk = [(f"X{pb}", c) for c in range(8)]
    ssp = ssb[:, 0:256]
    for c in range(8):
        s.act(lambda e, c=c: e.activation(sq[:, c, :], X[pb][:, c, :], AF.Square), [Xk[c]], [("sq", c)])
    mm_group(s, ssp, [(k.ones_bf[:], sq[:, c, :]) for c in range(8)], [("sq", c) for c in range(8)], ["ssb"])
    s.act(lambda e: e.activation(rstd[pb][:], ssp, AF.Sqrt, bias=k.eps_col[:, 0:1], scale=1.0 / D), ["ssb"], [f"rs0{pb}"])
    s.dve(lambda e: e.reciprocal(rstd[pb][:], rstd[pb][:]), [f"rs0{pb}"], [f"rstd{pb}"])
    for c in range(8):
        tm, tk = tmps[c % 2], f"tmp{c % 2}"
        s.dve(lambda e, c=c, tm=tm: e.scalar_tensor_tensor(tm[:], X[pb][:, c, :], G[:, c:c + 1], rstd[pb][:], ALU.mult, ALU.mult),
              [Xk[c], f"rstd{pb}"], [tk])
        s.act(lambda e, c=c, tm=tm: e.activation(hT[pb][:, c, :], tm[:], AF.Identity, bias=SH[:, c:c + 1], scale=1.0),
              [tk], [(f"hT{pb}", c)])


def phase_l1a(k):
    nc, s, din, dr = k.nc, k.s, k.din, k.dr
    TQ = 256
    NQ = S // TQ
    with ExitStack() as es:
        T = lambda name, shape, dt=F32: es.enter_context(nc.sbuf_tensor(f"p{k.pid}_{name}", shape, dt))
        P = lambda name, shape, dt=F32: es.enter_context(nc.psum_tensor(f"p{k.pid}_{name}", shape, dt))
        w = T("w_s5in", [128, 8, D], BF16)
        load_weight_bf16(k, w, din["s5_w_in"][0], 8, D, "w")
        WK = wkeys("w", 8, D)
        X = [T(f"X{i}", [128, 8, TQ]) for i in range(2)]
        hT = [T(f"hT{i}", [128, 8, TQ], BF16) for i in range(2)]
        sq = T("sq", [128, 8, TQ], BF16)
        tmps = [T(f"tmp{i}", [128, TQ]) for i in range(2)]
        rstd = [T(f"rstd{i}", [128, TQ]) for i in range(2)]
        st = [T(f"st{i}", [128, 8, TQ]) for i in range(2)]
        mm_ps = [P(f"mm_ps{i}", [128, 512]) for i in range(4)]
        ssb = P("ssb", [128, 512])
        XTv = dr["XT"].rearrange("(c p) t -> p c t", p=128)
        Uv = dr["U"].rearrange("(c p) t -> p c t", p=128)

        def load(n):
            pb = n % 2
            s.dma(X[pb][:], XTv[:, :, n * TQ:(n + 1) * TQ], writes=[(f"X{pb}", c) for c in range(8)])
        mi = 0
        load(0)
        norm256(k, 0, X, hT, sq, tmps, rstd, ssb, k.gm[:, 1, :], k.mod[:, 1, 0:8])
        for n in range(NQ):
            pb = n % 2
            if n + 1 < NQ:
                load(n + 1)
            hk = [(f"hT{pb}", c) for c in range(8)]
            for j in range(8):
                ps, pk = mm_ps[mi % 4][:, 0:TQ], f"mm_ps{mi % 4}"
                mi += 1
                mm_group(s, ps, [(w[:, kk, j * 128:(j + 1) * 128], hT[pb][:, kk, :]) for kk in range(8)], WK + hk, [pk])
                if j % 2 == 0:
                    s.act(lambda e, ps=ps, j=j, pb=pb: e.copy(st[pb][:, j, :], ps), [pk], [(f"st{pb}", j)])
                else:
                    s.dve(lambda e, ps=ps, j=j, pb=pb: e.tensor_copy(st[pb][:, j, :], ps), [pk], [(f"st{pb}", j)])
            if n + 1 < NQ:
                norm256(k, n + 1, X, hT, sq, tmps, rstd, ssb, k.gm[:, 1, :], k.mod[:, 1, 0:8])
            s.dma(Uv[:, :, n * TQ:(n + 1) * TQ], st[pb][:], reads=[(f"st{pb}", j) for j in range(8)], writes=[("U", n)])
        s.emit_phase()


def phase_s5(k):
    nc, s, din, dr = k.nc, k.s, k.din, k.dr
    with ExitStack() as es:
        T = lambda name, shape, dt=F32: es.enter_context(nc.sbuf_tensor(f"p{k.pid}_{name}", shape, dt))
        P = lambda name, shape, dt=F32: es.enter_context(nc.psum_tensor(f"p{k.pid}_{name}", shape, dt))
        LRe = T("LRe", [128, 64]); LIm = T("LIm", [128, 64]); dtb = T("dtb", [128, 64])
        for hf in range(2):
            ps_ = slice(64 * hf, 64 * hf + 64)
            s.dma(LRe[ps_, :], din["s5_lam_re"][0].rearrange("g p -> p g"), writes=[("LRe", hf)])
            s.dma(LIm[ps_, :], din["s5_lam_im"][0].rearrange("g p -> p g"), writes=[("LIm", hf)])
        s.dma(dtb[:], din["s5_log_dt"].to_broadcast([128, 64]), writes=["dtb"])
        Bre = T("Bre", [128, 64, 16]); Bim = T("Bim", [128, 64, 16])
        for hf in range(2):
            ps_ = slice(64 * hf, 64 * hf + 64)
            for g8 in range(8):
                gs = slice(g8 * 8, (g8 + 1) * 8)
                s.dma(Bre[ps_, gs, :], din["s5_b_re"][0, gs].rearrange("g p h -> p g h"), writes=[("Bre", hf, g8)])
                s.dma(Bim[ps_, gs, :], din["s5_b_im"][0, gs].rearrange("g p h -> p g h"), writes=[("Bim", hf, g8)])
        BreK = [("Bre", hf, g8) for hf in range(2) for g8 in range(8)]
        BimK = [("Bim", hf, g8) for hf in range(2) for g8 in range(8)]
        CCa = T("CCa", [128, 8, 128]); CCb = T("CCb", [128, 8, 128])
        cre = din["s5_c_re"][0].rearrange("g h p -> (g h) p").rearrange("(b q) p -> q b p", q=128)
        cim = din["s5_c_im"][0].rearrange("g h p -> (g h) p").rearrange("(b q) p -> q b p", q=128)
        s.dma(CCa[:, :, 0:64], cre, writes=[("CCa", 0)]); s.dma(CCa[:, :, 64:128], cim, writes=[("CCa", 1)])
        s.dma(CCb[:, :, 0:64], cim, writes=[("CCb", 0)]); s.dma(CCb[:, :, 64:128], cre, writes=[("CCb", 1)])
        LReK = [("LRe", 0), ("LRe", 1)]; LImK = [("LIm", 0), ("LIm", 1)]
        sm = {}

        def SM(name):
            sm[name] = T("sm_" + name, [128, 64])
            return sm[name]
        for nm in ("lr", "mag", "phi", "phis", "t0", "t1", "fs", "fc", "sinv", "cosv", "are", "aim", "den", "zre", "zim",
                   "za", "zb", "zas", "zbs", "c512", "s512"):
            SM(nm)
        smi = T("smi", [128, 64], I32)
        sgnA = T("sgnA", [128, 1])
        s.pool(lambda e: e.memset(sgnA[0:64, :], 1.0), [], [("sgnA", 0)])
        s.pool(lambda e: e.memset(sgnA[64:128, :], -1.0), [], [("sgnA", 1)])
        SGK = [("sgnA", 0), ("sgnA", 1)]
        s.act(lambda e: e.activation(dtb[:], dtb[:], AF.Exp), ["dtb"], ["dtb"])
        s.dve(lambda e: e.tensor_scalar(sm["lr"][:], LRe[:], -1e-4, None, ALU.min), LReK, ["lr"])
        s.dve(lambda e: e.tensor_tensor(sm["t0"][:], sm["lr"][:], dtb[:], ALU.mult), ["lr", "dtb"], ["t0"])
        s.act(lambda e: e.activation(sm["mag"][:], sm["t0"][:], AF.Exp), ["t0"], ["mag"])
        s.dve(lambda e: e.scalar_tensor_tensor(sm["phi"][:], LIm[:], 1.0 / TWO_PI, dtb[:], ALU.mult, ALU.mult), LImK + ["dtb"], ["phi"])
        s.dve(lambda e: e.tensor_scalar(sm["phis"][:], sm["phi"][:], sgnA[:, 0:1], None, ALU.mult), ["phi"] + SGK, ["phis"])

        def frac_sin(dst, src, add, key_src, key_dst, mult=1.0):
            s.dve(lambda e: e.tensor_scalar(sm["t0"][:], sm[src][:], mult, add, ALU.mult, ALU.add), [key_src], ["t0"])
            s.dve(lambda e: e.tensor_copy(smi[:], sm["t0"][:]), ["t0"], ["smi"])
            s.dve(lambda e: e.tensor_tensor(sm["t1"][:], sm["t0"][:], smi[:], ALU.subtract), ["t0", "smi"], ["t1"])
            s.act(lambda e: e.activation(sm[dst][:], sm["t1"][:], AF.Sin, scale=TWO_PI), ["t1"], [key_dst])
        frac_sin("sinv", "phi", 0.0, "phi", "sinv")
        frac_sin("cosv", "phi", 0.25, "phi", "cosv")
        frac_sin("c512", "phi", 0.25, "phi", "c512", mult=512.0)
        frac_sin("s512", "phis", 0.0, "phis", "s512", mult=512.0)
        s.dve(lambda e: e.tensor_tensor(sm["are"][:], sm["mag"][:], sm["cosv"][:], ALU.mult), ["mag", "cosv"], ["are"])
        s.dve(lambda e: e.tensor_tensor(sm["aim"][:], sm["mag"][:], sm["sinv"][:], ALU.mult), ["mag", "sinv"], ["aim"])
        s.dve(lambda e: e.tensor_scalar(sm["are"][:], sm["are"][:], -1.0, None, ALU.add), ["are"], ["are"])
        s.dve(lambda e: e.tensor_tensor(sm["den"][:], sm["lr"][:], sm["lr"][:], ALU.mult), ["lr"], ["den"])
        s.dve(lambda e: e.tensor_tensor(sm["t0"][:], LIm[:], LIm[:], ALU.mult), LImK, ["t0"])
        s.dve(lambda e: e.tensor_tensor(sm["den"][:], sm["den"][:], sm["t0"][:], ALU.add), ["den", "t0"], ["den"])
        s.dve(lambda e: e.reciprocal(sm["den"][:], sm["den"][:]), ["den"], ["den"])
        s.dve(lambda e: e.tensor_tensor(sm["t0"][:], sm["are"][:], sm["lr"][:], ALU.mult), ["are", "lr"], ["t0"])
        s.dve(lambda e: e.tensor_tensor(sm["t1"][:], sm["aim"][:], LIm[:], ALU.mult), ["aim"] + LImK, ["t1"])
        s.dve(lambda e: e.tensor_tensor(sm["zre"][:], sm["t0"][:], sm["t1"][:], ALU.add), ["t0", "t1"], ["zre"])
        s.dve(lambda e: e.tensor_tensor(sm["zre"][:], sm["zre"][:], sm["den"][:], ALU.mult), ["zre", "den"], ["zre"])
        s.dve(lambda e: e.tensor_tensor(sm["t0"][:], sm["aim"][:], sm["lr"][:], ALU.mult), ["aim", "lr", "zre"], ["t0"])
        s.dve(lambda e: e.tensor_tensor(sm["t1"][:], sm["are"][:], LIm[:], ALU.mult), ["are", "zre"] + LImK, ["t1"])
        s.dve(lambda e: e.tensor_tensor(sm["zim"][:], sm["t0"][:], sm["t1"][:], ALU.subtract), ["t0", "t1"], ["zim"])
        s.dve(lambda e: e.tensor_tensor(sm["zim"][:], sm["zim"][:], sm["den"][:], ALU.mult), ["zim", "den"], ["zim"])
        lo, hi = slice(0, 64), slice(64, 128)
        s.dve(lambda e: e.tensor_copy(sm["za"][lo, :], sm["zre"][lo, :]), ["zre"], [("za", 0)])
        s.dve(lambda e: e.tensor_copy(sm["za"][hi, :], sm["zim"][hi, :]), ["zim"], [("za", 1)])
        s.dve(lambda e: e.tensor_scalar(sm["zb"][lo, :], sm["zim"][lo, :], -1.0, None, ALU.mult), ["zim"], [("zb", 0)])
        s.dve(lambda e: e.tensor_copy(sm["zb"][hi, :], sm["zre"][hi, :]), ["zre"], [("zb", 1)])
        s.dve(lambda e: e.tensor_copy(sm["zas"][lo, :], sm["zim"][lo, :]), ["zim"], [("zas", 0)])
        s.dve(lambda e: e.tensor_copy(sm["zas"][hi, :], sm["zre"][hi, :]), ["zre"], [("zas", 1)])
        s.dve(lambda e: e.tensor_copy(sm["zbs"][lo, :], sm["zre"][lo, :]), ["zre"], [("zbs", 0)])
        s.dve(lambda e: e.tensor_scalar(sm["zbs"][hi, :], sm["zim"][hi, :], -1.0, None, ALU.mult), ["zim"], [("zbs", 1)])
        BB = T("BB", [128, 64, 16]); BBs = T("BBs", [128, 64, 16]); bt0 = T("bbt0", [128, 64, 16])

        def mkbb(dst, dkey, za, zb):
            zak = [(za, 0), (za, 1)]; zbk = [(zb, 0), (zb, 1)]
            s.dve(lambda e: e.tensor_tensor(bt0[:], Bre[:], sm[za][:].unsqueeze(2).to_broadcast([128, 64, 16]), ALU.mult), BreK + zak, ["bt0"])
            s.dve(lambda e: e.tensor_tensor(dst[:], Bim[:], sm[zb][:].unsqueeze(2).to_broadcast([128, 64, 16]), ALU.mult), BimK + zbk, [dkey])
            s.dve(lambda e: e.tensor_tensor(dst[:], dst[:], bt0[:], ALU.add), [dkey, "bt0"], [dkey])
        mkbb(BB, "BB", "za", "zb")
        mkbb(BBs, "BBs", "zas", "zbs")
        Bfull = T("Bfull", [128, 8, 128]); Bsfull = T("Bsfull", [128, 8, 128])
        M1full = T("M1full", [128, 8, 128]); M2full = T("M2full", [128, 8, 128])
        ctr_ps = P("ctr_ps", [128, 512])
        BBv = BB[:].rearrange("p (b g) h -> p b (g h)", b=8)
        BBsv = BBs[:].rearrange("p (b g) h -> p b (g h)", b=8)
        for (src, skeys, dst, dkey, sgn) in ((BBv, ["BB"], Bfull, "Bfull", None), (BBsv, ["BBs"], Bsfull, "Bsfull", None),
                                             (CCa[:], [("CCa", 0), ("CCa", 1)], M1full, "M1full", 1.0),
                                             (CCb[:], [("CCb", 0), ("CCb", 1)], M2full, "M2full", -1.0)):
            for b4 in range(2):
                def trf(e, src=src, b4=b4):
                    ins = None
                    for a in range(4):
                        ins = e.transpose(ctr_ps[:, a * 128:(a + 1) * 128], src[:, b4 * 4 + a, :], k.ident[:])
                    return ins
                s.pe(trf, skeys + ["ident"], ["ctr_ps"])
                dv = dst[:, b4 * 4:(b4 + 1) * 4, :]
                pv = ctr_ps[:].rearrange("p (a f) -> p a f", f=128)
                if sgn is None:
                    s.act(lambda e, dv=dv, pv=pv: e.copy(dv, pv), ["ctr_ps"], [(dkey, b4)])
                else:
                    s.dve(lambda e, dv=dv, pv=pv, sgn=sgn: e.tensor_scalar(dv, pv, sgnA[:, 0:1], sgn, ALU.mult, ALU.mult), ["ctr_ps"] + SGK, [(dkey, b4)])
        rowm = T("rowm", [128, 8]); rtmp = T("rtmp", [128, 8]); rm1 = T("rm1", [128, 8])
        s.pool(lambda e: e.iota(rtmp[:], pattern=[[-16, 8]], base=0, channel_multiplier=1, allow_small_or_imprecise_dtypes=True), [], ["rtmp"])
        s.dve(lambda e: e.tensor_single_scalar(rm1[:], rtmp[:], 0.0, ALU.is_ge), ["rtmp"], ["rm1"])
        s.dve(lambda e: e.scalar_tensor_tensor(rowm[:], rtmp[:], 16.0, rm1[:], ALU.is_lt, ALU.mult), ["rtmp", "rm1"], ["rowm"])
        colm = T("colm", [128, 8, 128]); ctmp = T("ctmp", [128, 8, 128]); cm1 = T("cm1", [128, 8, 128])
        s.pool(lambda e: e.iota(ctmp[:], pattern=[[-16, 8], [1, 128]], base=0, channel_multiplier=0, allow_small_or_imprecise_dtypes=True), [], ["ctmp"])
        s.dve(lambda e: e.tensor_single_scalar(cm1[:], ctmp[:], 0.0, ALU.is_ge), ["ctmp"], ["cm1"])
        s.dve(lambda e: e.scalar_tensor_tensor(colm[:], ctmp[:], 16.0, cm1[:], ALU.is_lt, ALU.mult), ["ctmp", "cm1"], ["colm"])
        Jsw = T("Jsw", [128, 128]); jt = T("jt", [128, 128]); je = T("je", [128, 128])
        s.pool(lambda e: e.iota(jt[:], pattern=[[1, 128]], base=0, channel_multiplier=-1, allow_small_or_imprecise_dtypes=True), [], ["jt"])
        s.dve(lambda e: e.tensor_single_scalar(je[:], jt[:], 64.0, ALU.is_equal), ["jt"], ["je"])
        s.dve(lambda e: e.scalar_tensor_tensor(Jsw[:], jt[:], -64.0, je[:], ALU.is_equal, ALU.add), ["jt", "je"], ["Jsw"])
        iot = T("iot", [128, 512])
        s.pool(lambda e: e.iota(iot[:], pattern=[[1, 512]], base=0, channel_multiplier=0, allow_small_or_imprecise_dtypes=True), [], ["iot"])
        u32 = [T(f"u32_{i}", [128, S]) for i in range(2)]
        ub = [T(f"ub{i}", [128, S], BF16) for i in range(2)]
        Bpad = [T(f"Bpad{i}", [128, 8, 128], BF16) for i in range(2)]
        Bspad = [T(f"Bspad{i}", [128, 8, 128], BF16) for i in range(2)]
        M1pad = [T(f"M1pad{i}", [128, 8, 128], BF16) for i in range(2)]
        M2pad = [T(f"M2pad{i}", [128, 8, 128], BF16) for i in range(2)]
        CSb = [T(f"CStabb{i}", [128, 8, 2, 512], BF16) for i in range(2)]
        t12b = [T(f"t12b{i}", [128, 2, 512], BF16) for i in range(2)]
        P12 = [T(f"P12_{i}", [128, 2, 512], BF16) for i in range(2)]
        wb = [T(f"wb{i}", [128, 512], BF16) for i in range(2)]
        Rot = [T(f"Rot{i}", [128, 8, 128]) for i in range(2)]
        vt = T("vt", [128, 512])
        vti = T("vti", [128, 512], I32)
        vf = T("vf", [128, 512])
        wt = [T(f"wt{i}", [128, 512]) for i in range(2)]
        wlast = T("wlast", [128, 8])
        inits = T("inits", [128, 8])
        yt = [T(f"yt{i}", [128, 512]) for i in range(2)]
        yg = [T(f"yg{i}", [128, 512], BF16) for i in range(2)]
        bub = [P(f"bub{i}", [128, 2, 512]) for i in range(2)]
        y_ps = [P(f"y_ps{i}", [128, 512]) for i in range(1)]
        bt_ps = [P(f"bt_ps{i}", [128, 512]) for i in range(1)] * 2
        dm_ps = P("dm_ps", [128, 512])

        def keep_warm(nrep):
            def fn(e):
                ins = None
                for _ in range(nrep):
                    ins = e.matmul(dm_ps[:], k.ident_bf[:], k.warm_src[:], start=True, stop=True)
                return ins
            s.pe(fn, [], [])

        in_ps = ctr_ps
        it = 0

        def load_u(b):
            bp = b % 2
            rows = slice(b * 128, (b + 1) * 128)
            s.dma(u32[bp][:], dr["U"][rows, :], writes=[f"u32_{bp}"])
            s.dma(ub[bp][:], dr["U"][rows, :], writes=[f"ub{bp}"], eng="pool")

        def setup_group(b, gl):
            bp = b % 2
            g = b * 8 + gl
            s.dve(lambda e: e.tensor_scalar(Bpad[bp][:, gl, :], Bfull[:, b, :], rowm[:, gl:gl + 1], None, ALU.mult),
                  [("Bfull", b // 4), "rowm"], [("Bpad", bp, gl)])
            s.dve(lambda e: e.tensor_scalar(Bspad[bp][:, gl, :], Bsfull[:, b, :], rowm[:, gl:gl + 1], None, ALU.mult),
                  [("Bsfull", b // 4), "rowm"], [("Bspad", bp, gl)])
            s.pool(lambda e: e.tensor_tensor(M1pad[bp][:, gl, :], M1full[:, b, :], colm[:, gl, :], ALU.mult),
                  [("M1full", b // 4), "colm"], [("M1pad", bp, gl)])
            s.pool(lambda e: e.tensor_tensor(M2pad[bp][:, gl, :], M2full[:, b, :], colm[:, gl, :], ALU.mult),
                  [("M2full", b // 4), "colm"], [("M2pad", bp, gl)])
            for (ti, ph, add) in ((0, "phi", 0.25), (1, "phis", 0.0)):
                s.act(lambda e, ph=ph, add=add: e.activation(vt[:], iot[:], AF.Identity, bias=add, scale=sm[ph][:, g:g + 1]), ["iot", ph], ["vt"])
                s.act(lambda e: e.copy(vti[:], vt[:]), ["vt"], ["vti"])
                s.pool(lambda e: e.tensor_tensor(vf[:], vt[:], vti[:], ALU.subtract), ["vt", "vti"], ["vf"])
                s.act(lambda e, ti=ti: e.activation(CSb[bp][:, gl, ti, :], vf[:], AF.Sin, scale=TWO_PI), ["vf"], [("CSb", bp, gl, ti)])
            s.dve(lambda e: e.tensor_scalar(Rot[bp][:, gl, :], k.ident[:], sm["c512"][:, g:g + 1], None, ALU.mult),
                  ["ident", "c512"], [("Rot", bp, gl)])
            s.dve(lambda e: e.scalar_tensor_tensor(Rot[bp][:, gl, :], Jsw[:], sm["s512"][:, g:g + 1], Rot[bp][:, gl, :], ALU.mult, ALU.add),
                  ["Jsw", "s512", ("Rot", bp, gl)], [("Rot", bp, gl)])

        load_u(0)
        for gl in range(8):
            setup_group(0, gl)
        for b in range(8):
            bp = b % 2
            rows = slice(b * 128, (b + 1) * 128)
            if b + 1 < 8:
                load_u(b + 1)
            stZ, stZi, stZa, stA, stB0, stB = [], [], [], [], [], []
            for kb in range(8):
                for gl in range(8):
                    i2 = it % 2
                    it += 1

                    def Z_(kb=kb, gl=gl, i2=i2, bp=bp):
                        tcs = slice(kb * 512, (kb + 1) * 512)
                        def buf(e):
                            e.matmul(bub[i2][:, 0, :], Bpad[bp][:, gl, :], ub[bp][:, tcs], start=True, stop=True)
                            return e.matmul(bub[i2][:, 1, :], Bspad[bp][:, gl, :], ub[bp][:, tcs], start=True, stop=True)
                        s.pe(buf, [("Bpad", bp, gl), ("Bspad", bp, gl), f"ub{bp}"], [f"bub{i2}"])
                        keep_warm(2)

                    def Zi_(kb=kb, gl=gl, bp=bp):
                        if kb > 0:
                            s.pe(lambda e: e.matmul(in_ps[:, gl:gl + 1], Rot[bp][:, gl, :], wlast[:, gl:gl + 1], start=True, stop=True),
                                 [("Rot", bp, gl), ("wlast", gl)], ["in_ps"])

                    def Za_(kb=kb, gl=gl):
                        if kb > 0:
                            s.act(lambda e: e.copy(inits[:, gl:gl + 1], in_ps[:, gl:gl + 1]), ["in_ps"], [("inits", gl)])

                    def A_(kb=kb, gl=gl, i2=i2, bp=bp):
                        s.dve(lambda e: e.tensor_tensor(t12b[i2][:, 0, :], bub[i2][:, 0, :], CSb[bp][:, gl, 0, :], ALU.mult),
                              [f"bub{i2}", ("CSb", bp, gl, 0)], [(f"t12b{i2}", 0)])
                        s.pe(lambda e: e.matmul(bt_ps[i2][:], k.ident_bf[:], t12b[i2][:, 0, :], start=True, stop=False),
                             [(f"t12b{i2}", 0), "ident_bf"], [f"bt_ps{i2}"])
                        s.dve(lambda e: e.tensor_tensor(t12b[i2][:, 1, :], bub[i2][:, 1, :], CSb[bp][:, gl, 1, :], ALU.mult),
                              [f"bub{i2}", ("CSb", bp, gl, 1)], [(f"t12b{i2}", 1)])
                        s.pe(lambda e: e.matmul(bt_ps[i2][:], k.ident_bf[:], t12b[i2][:, 1, :], start=False, stop=True),
                             [(f"t12b{i2}", 1), "ident_bf"], [f"bt_ps{i2}"])
                        keep_warm(1)

                    def B0_(kb=kb, gl=gl, i2=i2, b=b, rows=rows):
                        g = b * 8 + gl
                        tcs = slice(kb * 512, (kb + 1) * 512)
                        yp, ypk = y_ps[0], "y_ps0"
                        w_, wk = wt[i2], f"wt{i2}"
                        if kb == 0:
                            init = 0.0
                            ik = []
                        else:
                            init = inits[:, gl:gl + 1]
                            ik = [("inits", gl)]
                        s.dve(lambda e: e.tensor_tensor_scan(w_[:], sm["mag"][:, g:g + 1].to_broadcast([128, 512]), bt_ps[i2][:], init,
                                                             ALU.mult, ALU.add), [f"bt_ps{i2}", "mag"] + ik, [wk])
                        if kb < 7:
                            s.act(lambda e: e.copy(wlast[:, gl:gl + 1], w_[:, 511:512]), [wk], [("wlast", gl)])
                        s.act(lambda e: e.copy(wb[i2][:], w_[:]), [wk], [f"wb{i2}"])

                    def B_(kb=kb, gl=gl, i2=i2, b=b, rows=rows, bp=bp):
                        tcs = slice(kb * 512, (kb + 1) * 512)
                        yp, ypk = y_ps[0], "y_ps0"
                        s.dve(lambda e: e.tensor_tensor(P12[i2][:], wb[i2][:].unsqueeze(1).to_broadcast([128, 2, 512]), CSb[bp][:, gl, :, :], ALU.mult),
                              [f"wb{i2}", ("CSb", bp, gl, 0), ("CSb", bp, gl, 1)], [f"P12_{i2}"])
                        s.pe(lambda e: e.matmul(yp[:], M1pad[bp][:, gl, :], P12[i2][:, 0, :], start=(gl == 0), stop=False),
                             [("M1pad", bp, gl), f"P12_{i2}"], [ypk])
                        s.pe(lambda e: e.matmul(yp[:], M2pad[bp][:, gl, :], P12[i2][:, 1, :], start=False, stop=(gl == 7)),
                             [("M2pad", bp, gl), f"P12_{i2}"], [ypk])
                        keep_warm(1)
                        if gl == 7:
                            yb = kb % 2
                            s.dve(lambda e: e.scalar_tensor_tensor(yt[yb][:], u32[bp][:, tcs], k.s5d[:, b:b + 1], yp[:], ALU.mult, ALU.add),
                                  [f"u32_{bp}", ypk], [f"yt{yb}"])
                            s.act(lambda e: e.activation(yg[yb][:], yt[yb][:], AF.Gelu_apprx_tanh), [f"yt{yb}"], [f"yg{yb}"])
                            s.dma(dr["YG"][rows, tcs], yg[yb][:], reads=[f"yg{yb}"], writes=[("YG", b, kb)])
                    stZ.append(Z_)
                    stZi.append(Zi_)
                    stZa.append(Za_)
                    stA.append(A_)
                    stB0.append(B0_)
                    stB.append(B_)
            n = len(stA)
            for t in range(-2, n):
                if 0 <= t < n:
                    stB0[t]()
                if 0 <= t + 1 < n:
                    stZa[t + 1]()
                    stA[t + 1]()
                if 0 <= t + 2 < n:
                    stZ[t + 2]()
                    stZi[t + 2]()
                if 0 <= t < n:
                    stB[t]()
                if b + 1 < 8 and t >= 0 and t % 8 == 4:
                    setup_group(b + 1, t // 8)
        s.emit_phase()


def build(debug_upto=None):
    nc = bass.Bass("TRN2", target_bir_lowering=False)
    k = K()
    k.nc = nc
    din = {}

    def inp(name, shape):
        din[name] = nc.dram_tensor(name, list(shape), F32, kind="ExternalInput").ap()

    inp("x", [S, D]); inp("c", [1, D])
    inp("norm_mix_g", [2, D]); inp("norm_ffn_g", [2, D])
    inp("ada_w", [2, D, 6 * D]); inp("ada_b", [2, 6 * D])
    inp("ffn_w_in", [2, D, 2 * DFF]); inp("ffn_w_out", [2, DFF, D])
    inp("final_norm_g", [1, D])
    inp("hy_w_in", [1, D, 3584]); inp("hy_w_out", [1, D, D])
    inp("hg_norm_g", [1, 512]); inp("hg_lb_logits", [3, 512])
    inp("s5_w_in", [1, D, D])
    inp("s5_lam_re", [1, 64, 64]); inp("s5_lam_im", [1, 64, 64]); inp("s5_log_dt", [1, 64])
    inp("s5_b_re", [1, 64, 64, 16]); inp("s5_b_im", [1, 64, 64, 16])
    inp("s5_c_re", [1, 64, 16, 64]); inp("s5_c_im", [1, 64, 16, 64])
    inp("s5_d", [1, D]); inp("s5_w_glu", [1, D, 2 * D])
    k.din = din
    kind = "ExternalOutput" if debug_upto is not None else "Internal"
    dr = {}

    def scr(name, shape, dt):
        dr[name] = nc.dram_tensor(name, list(shape), dt, kind=kind).ap()

    scr("XT", [D, S], F32); scr("QK", [D, S], BF16)
    scr("VA", [S, 512], BF16); scr("IB", [S, 512], BF16)
    scr("QB", [512, S], F32); scr("FB", [512, S], F32); scr("GB", [512, S], F32)
    scr("MT", [D, S], BF16); scr("U", [D, S], F32); scr("YG", [D, S], BF16)
    k.dr = dr
    k.out = nc.dram_tensor("out", [S, D], F32, kind="ExternalOutput").ap()
    with ExitStack() as es:
        es.enter_context(nc.allow_non_contiguous_dma(reason="small strided parameter loads"))
        T = lambda name, shape, dt=F32: es.enter_context(nc.sbuf_tensor(name, shape, dt))
        k.s = s = Sched(nc, es)
        k.ident = T("ident", [128, 128])
        k.ident_bf = T("ident_bf", [128, 128], BF16)
        k.ones_bf = T("ones_bf", [128, 128], BF16)
        k.eps_col = T("eps_col", [128, 1])
        k.warm_src = T("warm_src", [128, 512], BF16)
        k.nmg = T("nmg", [128, 2, 8]); k.nfg = T("nfg", [128, 2, 8]); k.fng = T("fng", [128, 8])
        k.s5d = T("s5d", [128, 8]); k.hgg = T("hgg", [128, 4])
        k.lb = T("lb", [128, 4]); k.oml = T("oml", [128, 4])
        k.mod = T("mod", [128, 2, 48]); k.gm = T("gm", [128, 2, 8]); k.gf = T("gf", [128, 2, 8])
        s.pool(lambda e: e.memset(k.ident[:], 1.0), [], ["ident"])
        s.pool(lambda e: e.affine_select(k.ident[:], k.ident[:], pattern=[[-1, 128]], compare_op=ALU.is_equal, fill=0.0,
                                         base=0, channel_multiplier=1), ["ident"], ["ident"])
        s.pool(lambda e: e.tensor_copy(k.ident_bf[:], k.ident[:]), ["ident"], ["ident_bf"])
        s.pool(lambda e: e.memset(k.ones_bf[:], 1.0), [], ["ones_bf"])
        s.pool(lambda e: e.memset(k.eps_col[:], EPS), [], ["eps_col"])
        s.pool(lambda e: e.memset(k.warm_src[:], 1.0), [], ["warm_src"])
        phases = [phase_pre, phase_l0a, phase_attn, phase_hgrn, lambda k: phase_b1(k, 0), lambda k: phase_ffn(k, 0, False), phase_l1a,
                  phase_s5, lambda k: phase_b1(k, 1), lambda k: phase_ffn(k, 1, True)]
        k.pre = {}
        scopes = {0: (1, [("w_in0", [128, 8, 3584])]), 4: (5, [("ffn_w1", [128, 8, 2 * DFF]), ("ffn_w2", [128, 22, D])]),
                  8: (9, [("ffn_w1", [128, 8, 2 * DFF]), ("ffn_w2", [128, 22, D])])}
        open_scope = None
        for i, ph in enumerate(phases):
            k.pid = i
            if i in scopes:
                open_scope = (scopes[i][0], ExitStack())
                for nm, shp in scopes[i][1]:
                    k.pre[nm] = open_scope[1].enter_context(nc.sbuf_tensor(f"pre{i}_" + nm, shp, BF16))
            ph(k)
            if open_scope is not None and open_scope[0] == i:
                open_scope[1].close()
                open_scope = None
            if debug_upto is not None and i >= debug_upto:
                break
        if open_scope is not None:
            open_scope[1].close()
    return nc


_NC_CACHE = {}


def kernel(**inputs):
    if "nc" not in _NC_CACHE:
        _NC_CACHE["nc"] = build()
    nc = _NC_CACHE["nc"]
    n = 8
    shared = {}
    for name, v in inputs.items():
        if name in ("x", "c"):
            continue
        a = np.ascontiguousarray(np.asarray(v, dtype=np.float32))
        if name == "final_norm_g":
            a = a.reshape(1, -1)
        shared[name] = a
    x = np.asarray(inputs["x"], dtype=np.float32)
    c = np.asarray(inputs["c"], dtype=np.float32)
    in_maps = []
    for b in range(n):
        m = dict(shared)
        m["x"] = np.ascontiguousarray(x[b])
        m["c"] = np.ascontiguousarray(c[b:b + 1])
        in_maps.append(m)
    res = run_bass_kernel_spmd(nc, in_maps, core_ids=list(range(n)))
    return np.stack([np.asarray(r["out"], dtype=np.float32) for r in res.results], axis=0)
```
